# Optimizing a Trainium2 kernel written in Bass

```python
import jax, jax.numpy as jnp
from jax import lax
import numpy as np

D_MODEL = 1024
BATCH = 8
SEQ = 2048
DEPTH = 4

CHUNK = 64
D_MIX = D_MODEL
D_MLSTM = D_MIX // 2
D_POOL = D_MIX - D_MLSTM
N_MLSTM_HEADS = 4
HEAD_DIM = D_MLSTM // N_MLSTM_HEADS
POOL_WINDOWS = (2, 4, 8, 16)
N_POOL_GROUPS = len(POOL_WINDOWS)
POOL_GROUP_DIM = D_POOL // N_POOL_GROUPS
CONV_WIDTH = 4
D_FF = 4 * D_MODEL
OFF_Q = 0
OFF_K = OFF_Q + D_MLSTM
OFF_V = OFF_K + D_MLSTM
OFF_G = OFF_V + D_MLSTM
OFF_O = OFF_G + 2 * N_MLSTM_HEADS
OFF_P = OFF_O + D_MLSTM
D_IN = OFF_P + D_POOL
ALPHA = (2.0 * DEPTH) ** 0.25
BETA = (8.0 * DEPTH) ** -0.25
LN_EPS = 1e-5

kernel_name = "hymba_mlstm_multiscale_pool_deepnorm"


def layer_norm(x, g, b):
    xf = x.astype(jnp.float32)
    mu = xf.mean(-1, keepdims=True)
    var = jnp.square(xf - mu).mean(-1, keepdims=True)
    return ((xf - mu) * lax.rsqrt(var + LN_EPS) * g.astype(jnp.float32) + b.astype(jnp.float32)).astype(x.dtype)


def causal_conv(x, w):
    S = x.shape[1]
    xp = jnp.pad(x, ((0, 0), (CONV_WIDTH - 1, 0), (0, 0)))
    y = w[0] * xp[:, 0:S]
    for j in range(1, CONV_WIDTH):
        y = y + w[j] * xp[:, j:j + S]
    return y


def mlstm_chunkwise(q, k, v, i_pre, f_pre):
    B, S, H, Dh = q.shape
    NC = S // CHUNK

    def to_chunks(t):
        t = t.reshape((B, NC, CHUNK, H) + t.shape[3:])
        return jnp.moveaxis(t, 3, 1)

    q, k, v = to_chunks(q), to_chunks(k), to_chunks(v)
    i_pre = to_chunks(i_pre)
    logf = to_chunks(jax.nn.log_sigmoid(f_pre))
    a = jnp.cumsum(logf, axis=-1)
    g = a[..., -1]

    w = g[..., None] - a + i_pre
    m_loc = w.max(-1)
    e = jnp.exp(w - m_loc[..., None])
    C_loc = jnp.einsum('bhnl,bhnlv,bhnlk->bhnvk', e, v, k)
    n_loc = jnp.einsum('bhnl,bhnlk->bhnk', e, k)

    def step(carry, inp):
        C, n, m = carry
        g_c, m_c, C_c, n_c = inp
        m_new = jnp.maximum(g_c + m, m_c)
        s_old = jnp.exp(g_c + m - m_new)
        s_new = jnp.exp(m_c - m_new)
        C_new = s_old[..., None, None] * C + s_new[..., None, None] * C_c
        n_new = s_old[..., None] * n + s_new[..., None] * n_c
        return (C_new, n_new, m_new), (C, n, m)

    init = (jnp.zeros((B, H, Dh, Dh), jnp.float32),
            jnp.zeros((B, H, Dh), jnp.float32),
            jnp.zeros((B, H), jnp.float32))
    xs = (jnp.moveaxis(g, 2, 0), jnp.moveaxis(m_loc, 2, 0),
          jnp.moveaxis(C_loc, 2, 0), jnp.moveaxis(n_loc, 2, 0))
    _, (C_prev, n_prev, m_prev) = lax.scan(step, init, xs)
    C_prev = jnp.moveaxis(C_prev, 0, 2)
    n_prev = jnp.moveaxis(n_prev, 0, 2)
    m_prev = jnp.moveaxis(m_prev, 0, 2)

    causal = jnp.tril(jnp.ones((CHUNK, CHUNK), dtype=bool))
    Dlog = a[..., :, None] - a[..., None, :] + i_pre[..., None, :]
    Dlog = jnp.where(causal, Dlog, -jnp.inf)
    m_inter = a + m_prev[..., None]
    m_t = jnp.maximum(m_inter, Dlog.max(-1))
    P = jnp.exp(Dlog - m_t[..., None])
    s_inter = jnp.exp(m_inter - m_t)
    qk = jnp.einsum('bhnld,bhnsd->bhnls', q, k) * P
    num = (jnp.einsum('bhnls,bhnsv->bhnlv', qk, v)
           + s_inter[..., None] * jnp.einsum('bhnvk,bhnlk->bhnlv', C_prev, q))
    den = qk.sum(-1) + s_inter * jnp.einsum('bhnk,bhnlk->bhnl', n_prev, q)
    h = num / jnp.maximum(jnp.abs(den), jnp.exp(-m_t))[..., None]
    return jnp.moveaxis(h, 1, 3).reshape(B, S, H * Dh)


def multiscale_pool(p, w_pool, scale):
    B, S, _ = p.shape
    pf = p.astype(jnp.float32)
    cs = jnp.cumsum(pf, axis=1)
    count = jnp.arange(1, S + 1, dtype=jnp.float32)[:, None]
    outs = []
    for gi, win in enumerate(POOL_WINDOWS):
        lo, hi = gi * POOL_GROUP_DIM, (gi + 1) * POOL_GROUP_DIM
        c = cs[..., lo:hi]
        lagged = jnp.pad(c[:, :S - win], ((0, 0), (win, 0), (0, 0)))
        mean = (c - lagged) / jnp.minimum(count, float(win))
        outs.append(mean - pf[..., lo:hi])
    d = jnp.stack(outs, axis=2)
    y = jnp.einsum('bsgc,gcd->bsgd', d, w_pool.astype(jnp.float32)).reshape(B, S, D_POOL)
    return (y * scale.astype(jnp.float32)).astype(p.dtype)


def hybrid_mixer(x, w_in, b_gate, w_conv, hn_g, w_pool, pool_scale, w_out):
    B, S, _ = x.shape
    u = x @ w_in
    qk = jax.nn.silu(causal_conv(u[..., OFF_Q:OFF_V], w_conv)).astype(jnp.float32)
    q = qk[..., :D_MLSTM].reshape(B, S, N_MLSTM_HEADS, HEAD_DIM)
    k = qk[..., D_MLSTM:].reshape(B, S, N_MLSTM_HEADS, HEAD_DIM) * (HEAD_DIM ** -0.5)
    v = u[..., OFF_V:OFF_G].astype(jnp.float32).reshape(B, S, N_MLSTM_HEADS, HEAD_DIM)
    gates = (u[..., OFF_G:OFF_O] + b_gate).astype(jnp.float32)
    i_pre = gates[..., :N_MLSTM_HEADS]
    f_pre = gates[..., N_MLSTM_HEADS:]
    h = mlstm_chunkwise(q, k, v, i_pre, f_pre).reshape(B, S, N_MLSTM_HEADS, HEAD_DIM)
    mu = h.mean(-1, keepdims=True)
    var = jnp.square(h - mu).mean(-1, keepdims=True)
    h = ((h - mu) * lax.rsqrt(var + LN_EPS)).reshape(B, S, D_MLSTM) * hn_g.astype(jnp.float32)
    y_m = (jax.nn.sigmoid(u[..., OFF_O:OFF_P].astype(jnp.float32)) * h).astype(x.dtype)
    y_p = multiscale_pool(u[..., OFF_P:D_IN], w_pool, pool_scale)
    return jnp.concatenate([y_m, y_p], axis=-1) @ w_out


def squared_relu_mlp(x, w1, w2):
    return jnp.square(jax.nn.relu(x @ w1)) @ w2


def setup_inputs(seed: int = 0) -> dict:
    key = jax.random.key(seed)
    ks = jax.random.split(key, 16)
    L = DEPTH
    nrm = jax.random.normal
    x = nrm(ks[0], (BATCH, SEQ, D_MODEL), jnp.float32)
    w_in = nrm(ks[1], (L, D_MODEL, D_IN), jnp.float32) * D_MODEL ** -0.5
    b_i = 0.1 * nrm(ks[2], (L, N_MLSTM_HEADS), jnp.float32)
    b_f = jnp.linspace(3.0, 6.0, N_MLSTM_HEADS, dtype=jnp.float32)[None] + 0.1 * nrm(ks[3], (L, N_MLSTM_HEADS), jnp.float32)
    b_gate = jnp.concatenate([b_i, b_f], axis=-1)
    w_conv = nrm(ks[4], (L, CONV_WIDTH, 2 * D_MLSTM), jnp.float32) * CONV_WIDTH ** -0.5
    hn_g = 1.0 + 0.02 * nrm(ks[5], (L, D_MLSTM), jnp.float32)
    w_pool = nrm(ks[6], (L, N_POOL_GROUPS, POOL_GROUP_DIM, POOL_GROUP_DIM), jnp.float32) * POOL_GROUP_DIM ** -0.5
    pool_scale = 1.0 + 0.02 * nrm(ks[7], (L, D_POOL), jnp.float32)
    w_out = nrm(ks[8], (L, D_MIX, D_MODEL), jnp.float32) * (D_MIX ** -0.5) * BETA
    ln1_g = 1.0 + 0.02 * nrm(ks[9], (L, D_MODEL), jnp.float32)
    ln1_b = 0.02 * nrm(ks[10], (L, D_MODEL), jnp.float32)
    w_ff1 = nrm(ks[11], (L, D_MODEL, D_FF), jnp.float32) * D_MODEL ** -0.5
    w_ff2 = nrm(ks[12], (L, D_FF, D_MODEL), jnp.float32) * (D_FF ** -0.5) * BETA
    ln2_g = 1.0 + 0.02 * nrm(ks[13], (L, D_MODEL), jnp.float32)
    ln2_b = 0.02 * nrm(ks[14], (L, D_MODEL), jnp.float32)
    return {"x": x, "w_in": w_in, "b_gate": b_gate, "w_conv": w_conv, "hn_g": hn_g,
            "w_pool": w_pool, "pool_scale": pool_scale, "w_out": w_out,
            "ln1_g": ln1_g, "ln1_b": ln1_b, "w_ff1": w_ff1, "w_ff2": w_ff2,
            "ln2_g": ln2_g, "ln2_b": ln2_b}


def reference(x, w_in, b_gate, w_conv, hn_g, w_pool, pool_scale, w_out,
              ln1_g, ln1_b, w_ff1, w_ff2, ln2_g, ln2_b):
    for l in range(DEPTH):
        mix = hybrid_mixer(x, w_in[l], b_gate[l], w_conv[l], hn_g[l], w_pool[l], pool_scale[l], w_out[l])
        x = layer_norm(ALPHA * x + mix, ln1_g[l], ln1_b[l])
        x = layer_norm(ALPHA * x + squared_relu_mlp(x, w_ff1[l], w_ff2[l]), ln2_g[l], ln2_b[l])
    return x
```

```python
import math, os
from contextlib import ExitStack
import numpy as np
import concourse.bass as bass
import concourse.mybir as mybir
from concourse.bass_utils import run_bass_kernel_spmd

F32 = mybir.dt.float32
BF16 = mybir.dt.bfloat16
AF = mybir.ActivationFunctionType
ALU = mybir.AluOpType
AX = mybir.AxisListType

S = 2048
D = 1024
DIN = 2568
DFF = 4096
NSEG = 4
SEG = 512
ALPHA_FULL = (2.0 * 4) ** 0.25
LN_EPS = 1e-5
LNK = math.log(128.0 ** -0.5)
RING = 4


class Trk:
    def __init__(self, nc, es):
        self.nc, self.es = nc, es
        self.E = {}
        self.lw = {}
        self.rd = {}
        self.reg_owner = {}
        self.reg_ev = {}
        self.region_of = lambda k: None
        self.n_wait = 0
        self.snap = {}

    def add_eng(self, name, eng, own=True):
        sem = self.es.enter_context(self.nc.semaphore("s_" + name)) if own else None
        self.E[name] = dict(eng=eng, sem=sem, cnt=0, seen={}, id=name)

    def dsem(self, name):
        return dict(sem=self.es.enter_context(self.nc.semaphore("d_" + name)), cnt=0, id="d_" + name)

    def _deps(self, R, W):
        deps = {}

        def add(ev):
            if ev is None:
                return
            sid, sh, v = ev
            if sid not in deps or deps[sid][1] < v:
                deps[sid] = (sh, v)

        for k in R:
            add(self.lw.get(k))
        for k in W:
            add(self.lw.get(k))
            for sid, (sh, v) in self.rd.get(k, {}).items():
                add((sid, sh, v))
        for k in list(R) + list(W):
            rg = self.region_of(k)
            if rg is not None:
                reg, tag = rg
                if self.reg_owner.get(reg) != tag:
                    for sid, (sh, v) in self.reg_ev.get(reg, {}).items():
                        add((sid, sh, v))
        return deps

    def _wait(self, e, deps, en):
        for sid, (sh, v) in deps.items():
            if sid == en:
                if en == "pe" or v > e["cnt"] or os.environ.get("NO_OWN_WAIT"):
                    continue
            if e["seen"].get(sid, 0) >= v:
                continue
            e["eng"].wait_ge(sh, v)
            e["seen"][sid] = v
            self.n_wait += 1
            for k2, v2 in self.snap.get((sid, v), {}).items():
                if e["seen"].get(k2, 0) < v2:
                    e["seen"][k2] = v2

    def _record(self, ev, R, W, seen=None):
        sid, sh, v = ev
        if seen is not None:
            d0 = self.snap.setdefault((sid, v), {})
            for k2, v2 in seen.items():
                if d0.get(k2, 0) < v2:
                    d0[k2] = v2
        for k in W:
            self.lw[k] = ev
            self.rd[k] = {}
        for k in R:
            d = self.rd.setdefault(k, {})
            if sid not in d or d[sid][1] < v:
                d[sid] = (sh, v)
        for k in list(R) + list(W):
            rg = self.region_of(k)
            if rg is not None:
                reg, tag = rg
                if self.reg_owner.get(reg) != tag:
                    self.reg_owner[reg] = tag
                    self.reg_ev[reg] = {}
                d = self.reg_ev[reg]
                if sid not in d or d[sid][1] < v:
                    d[sid] = (sh, v)

    def op(self, en, fn, R=(), W=(), inc=True):
        e = self.E[en]
        W = list(W) + [k for k in R if k[0] == "PS" and k not in W]
        R = [k for k in R if k[0] != "PS"]
        self._wait(e, self._deps(R, W), en)
        ins = fn(e["eng"])
        if inc:
            ins.then_inc(e["sem"], 1)
            e["cnt"] += 1
            ev = (en, e["sem"], e["cnt"])
        else:
            ev = (en, e["sem"], e["cnt"] + 1)
        self._record(ev, R, W, seen=e["seen"])

    def dma(self, qn, ds, out, in_, R=(), W=(), **kw):
        e = self.E[qn]
        self._wait(e, self._deps(R, W), qn + "_q")
        e["eng"].dma_start(out=out, in_=in_, **kw).then_inc(ds["sem"], 16)
        ds["cnt"] += 16
        self._record((ds["id"], ds["sem"], ds["cnt"]), R, W, seen=e["seen"])

    def wait_all(self, qn, keys):
        e = self.E[qn]
        self._wait(e, self._deps(keys, ()), qn + "_q")


def build(depth=4, last_unscaled=True, dbg=None):
    L = depth
    ALPHA = ALPHA_FULL
    nc = bass.Bass("TRN2", target_bir_lowering=False)
    x_d = nc.dram_tensor("x", [S, D], F32, kind="ExternalInput").ap()
    w_in_d = nc.dram_tensor("w_in", [L, D, DIN], F32, kind="ExternalInput").ap()
    b_gate_d = nc.dram_tensor("b_gate", [L, 8], F32, kind="ExternalInput").ap()
    w_conv_d = nc.dram_tensor("w_conv", [L, 4, D], F32, kind="ExternalInput").ap()
    hn_g_d = nc.dram_tensor("hn_g", [L, 512], F32, kind="ExternalInput").ap()
    w_pool_d = nc.dram_tensor("w_pool", [L, 4, 128, 128], F32, kind="ExternalInput").ap()
    pool_scale_d = nc.dram_tensor("pool_scale", [L, 512], F32, kind="ExternalInput").ap()
    w_out_d = nc.dram_tensor("w_out", [L, D, D], F32, kind="ExternalInput").ap()
    ln_d = [nc.dram_tensor(n, [L, D], F32, kind="ExternalInput").ap() for n in ("ln1_g", "ln1_b", "ln2_g", "ln2_b")]
    w_ff1_d = nc.dram_tensor("w_ff1", [L, D, DFF], F32, kind="ExternalInput").ap()
    w_ff2_d = nc.dram_tensor("w_ff2", [L, DFF, D], F32, kind="ExternalInput").ap()
    y_d = nc.dram_tensor("y", [S, D], F32, kind="ExternalOutput").ap()
    gscr_d = nc.dram_tensor("gscr", [L, NSEG, 8, SEG], F32, kind="Internal").ap()

    es = ExitStack()
    with es:
        def sb(name, shape, dt):
            return es.enter_context(nc.sbuf_tensor(name, shape, dt))

        T = Trk(nc, es)
        T.add_eng("pe", nc.tensor)
        T.add_eng("act", nc.scalar)
        T.add_eng("dve", nc.vector)
        T.add_eng("pool", nc.gpsimd)
        T.add_eng("sp", nc.sync, own=False)

        XF = sb("XF", [128, 8, S], F32)
        XB = sb("XB", [128, 8, S], BF16)
        RG = [sb(f"RG{i}", [128, 4096], BF16) for i in range(RING)]
        ARENA = sb("ARENA", [128, 16384], BF16)
        PS = [es.enter_context(nc.psum_tensor(f"PS{i}", [128, 512], F32)) for i in range(8)]

        def av(c0, n, b):
            return ARENA[:, c0:c0 + n].rearrange("p (a b) -> p a b", b=b)
        QT = av(0, 2048, 512)
        KT = av(2048, 2048, 512)
        KTOK = ARENA[:, 4096:6144].rearrange("p (c h d) -> p c h d", h=4, d=128)
        VP = ARENA[:, 6144:8192].rearrange("p (c h d) -> p c h d", h=4, d=128)
        YM = av(8192, 2048, 512)
        YP = av(10240, 2048, 512)
        PB = av(12288, 2112, 528)
        UQ = [ARENA[:, 14400:14916], ARENA[:, 14916:15432]]
        H = ARENA[:, :].rearrange("p (j t) -> p j t", t=S)
        XS = [ARENA[:, i * 2048:(i + 1) * 2048].bitcast(F32) for i in range(2)]

        AB_NAMES = {"QT", "KT", "KTOK", "VP", "YM", "YP", "PB", "UQ"}
        def region_of(k):
            n = k[0]
            if n in AB_NAMES:
                return ("AR", "AB")
            if n == "H":
                return ("AR", "FFN")
            if n == "XS":
                return ("AR", "IO")
            return None
        T.region_of = region_of

        RBt = [sb(f"RB{i}", [128, 512], BF16)[:, :] for i in range(3)]
        RSQt = [sb(f"RSQ{i}", [128, 512], BF16)[:, :] for i in range(3)]
        MEANt = [sb(f"MEAN{i}", [128, 512], F32)[:, :] for i in range(2)]
        RSTDt = [sb(f"RSTD{i}", [128, 512], F32)[:, :] for i in range(2)]
        DG = [sb(f"DG{i}", [128, 4, 128], BF16) for i in range(2)]
        G8 = sb("G8", [8, SEG], F32)
        GI = sb("GI", [16, 128], F32)
        GF = sb("GF", [16, 128], F32)
        NA = sb("NA", [16, 128], F32)
        GM = sb("GM", [16, 4], F32)
        ROW = sb("ROW", [1, 32], F32)
        MFULL = sb("MFULL", [1, 4, 5], F32)
        MC = sb("MC", [1, 16], F32)
        TR = sb("TR", [1, 16], F32)
        SCR = sb("SCR", [1, 16], F32)
        NMC = sb("NMC", [16, 2], F32)
        SCB = sb("SCB", [128, 16], F32)
        ET = sb("ET", [128, 16], F32)
        ETB = sb("ETB", [128, 16], BF16)
        THRT = sb("THRT", [128, 16], F32)
        C = sb("C", [128, 4, 129], F32)
        CB = sb("CB", [128, 4, 129], BF16)
        ATs = [sb(f"AT{i}", [128, 4, 128], BF16) for i in range(2)]
        HN = sb("HN", [128, 4, 4, 128], BF16)
        SQs = [sb(f"SQ{i}", [128, 4, 128], F32) for i in range(2)]
        SM = sb("SM", [128, 12, 16], F32)
        HALO = sb("HALO", [128, 8, 3], BF16)
        PHALO = sb("PHALO", [128, 4, 16], BF16)
        CS = sb("CS", [128, 16], F32)
        DFB = sb("DFB", [128, 16], BF16)
        HR = [sb(f"HR{i}", [128, 512], BF16) for i in range(2)]
        IDENTB = sb("IDENTB", [128, 128], BF16)
        IDENTF = sb("IDENTF", [128, 128], F32)
        MASK = sb("MASK", [128, 128], BF16)
        ONESM = sb("ONESM", [128, 128], BF16)
        ONESF = sb("ONESF", [128, 128], F32)
        INVC = sb("INVC", [128, 16], F32)
        PRMS = sb("PRMS", [128, 128], F32)
        PRMT = sb("PRMT", [128, 128], F32)
        PRMA = sb("PRMA", [128, 128], F32)
        PRM2T = sb("PRM2T", [128, 32], F32)
        WCVT = sb("WCVT", [128, 128], F32)
        BG = sb("BG", [8, 4], F32)
        WPF = sb("WPF", [128, 4, 128], F32)
        WPA = sb("WPA", [128, 4, 128], BF16)
        WPBt = sb("WPBt", [128, 4, 128], BF16)
        WPC = sb("WPC", [128, 4, 128], BF16)
        GW = [sb(f"GW{i}", [128, 8, 8], BF16) for i in range(L)]

        d_x = [T.dsem("x0"), T.dsem("x1")]
        d_y = [T.dsem("y0"), T.dsem("y1")]
        d_prm = T.dsem("prm")
        d_bg = T.dsem("bg")
        d_g = [T.dsem("g0"), T.dsem("g1"), T.dsem("g2")]
        d_gw = T.dsem("gw")
        d_wp = T.dsem("wp")
        d_w = [T.dsem(f"w{i}") for i in range(RING)]

        bank_ctr = [0]
        reserved = set()

        def nb():
            while True:
                b = bank_ctr[0] % 8
                bank_ctr[0] += 1
                if b not in reserved:
                    return b

        def psk(b, c0=0, c1=512):
            return [("PS", b)]

        def psbf(b):
            return PS[b][:, 0:256].bitcast(BF16)

        def mm(out, lhsT, rhs, start, stop, R, W, inc=None):
            if inc is None:
                inc = stop
            T.op("pe", lambda e: e.matmul(out, lhsT=lhsT, rhs=rhs, start=start, stop=stop), R=R, W=W, inc=inc)

        def tp(out, in_, ident, R, W, inc=True):
            T.op("pe", lambda e: e.transpose(out=out, in_=in_, identity=ident), R=R, W=W, inc=inc)

        def act(out, in_, func, R, W, bias=None, scale=None):
            kw = {}
            if bias is not None:
                kw["bias"] = bias
            if scale is not None:
                kw["scale"] = scale
            T.op("act", lambda e: e.activation(out=out, in_=in_, func=func, **kw), R=R, W=W)

        def dve(fn, R, W):
            T.op("dve", fn, R=R, W=W)

        plan = []
        for l in range(L):
            for s in range(NSEG):
                for kind, c0 in (("q", 0), ("k", 512), ("o", 1544), ("v", 1024), ("p", 2056)):
                    plan.append((kind, w_in_d[l][:, c0:c0 + 512].rearrange("(k p) n -> p k n", p=128), "kn"))
                for i in range(2):
                    plan.append((f"wo{i}", w_out_d[l][:, i * 512:(i + 1) * 512].rearrange("(k p) n -> p k n", p=128), "kn"))
            for q in range(4):
                for i in range(2):
                    c0 = q * 1024 + i * 512
                    plan.append((f"w1{i}", w_ff1_d[l][:, c0:c0 + 512].rearrange("(k p) n -> p k n", p=128), "kn"))
                for i in range(2):
                    r0 = q * 1024 + i * 512
                    plan.append((f"w2{i}", w_ff2_d[l][r0:r0 + 512, :].rearrange("(j p) n -> p j n", p=128), "jn"))
        ring = dict(acq=0, rel=0)

        def slot_view(slot, lay):
            if lay == "kn":
                return RG[slot][:, :].rearrange("p (k n) -> p k n", n=512)
            return RG[slot][:, :].rearrange("p (j n) -> p j n", n=1024)

        def issue_fill(i):
            kind, src, lay = plan[i]
            slot = i % RING
            T.dma("pool", d_w[slot], slot_view(slot, lay), src, R=(), W=[("W", slot)])

        def acquire(kind):
            i = ring["acq"]
            assert plan[i][0] == kind, (plan[i][0], kind)
            ring["acq"] += 1
            slot = i % RING
            return slot_view(slot, plan[i][2]), ("W", slot)

        def release():
            i = ring["rel"]
            ring["rel"] += 1
            if i + RING < len(plan):
                issue_fill(i + RING)

        T.op("pool", lambda e: e.memset(ONESF[:], 1.0), W=[("ONESF",)])
        T.op("pool", lambda e: e.memset(ONESM[:], 1.0 / 1024.0), W=[("ONESM",)])
        T.op("pool", lambda e: e.affine_select(out=IDENTF[:], in_=ONESF[:], pattern=[[1, 128]], compare_op=ALU.is_equal,
                                               fill=0.0, base=0, channel_multiplier=-1), R=[("ONESF",)], W=[("IDENTF",)])
        T.op("pool", lambda e: e.affine_select(out=IDENTB[:], in_=ONESF[:], pattern=[[1, 128]], compare_op=ALU.is_equal,
                                               fill=0.0, base=0, channel_multiplier=-1), R=[("ONESF",)], W=[("IDENTB",)])
        T.op("pool", lambda e: e.affine_select(out=MASK[:], in_=ONESF[:], pattern=[[1, 128]], compare_op=ALU.is_ge,
                                               fill=0.0, base=0, channel_multiplier=-1), R=[("ONESF",)], W=[("MASK",)])
        for t in range(16):
            T.op("pool", lambda e, t=t: e.memset(INVC[:, t:t + 1], 1.0 / (t + 1)), W=[("INVC",)])

        d_gws = [T.dsem(f"gw{i}") for i in range(L)]
        for l_ in range(L):
            with nc.allow_non_contiguous_dma(reason="gate weights, 32B rows"):
                T.dma("pool", d_gws[l_], GW[l_][:, :, :], w_in_d[l_][:, 1536:1544].rearrange("(k p) n -> p k n", p=128),
                      R=(), W=[("GW", l_)])
        for i in range(min(RING, len(plan))):
            if not os.environ.get("SKIP_PREFETCH"):
                issue_fill(i)

        def load_T(rows_list, dst, ncols, key):
            r = 0
            for src in rows_list:
                n = src.shape[0]
                T.dma("sp", d_prm, PRMS[r:r + n, :], src, R=(), W=[("PRMS",)])
                r += n
            b = nb()
            tp(PS[b][:, 0:r], PRMS[0:r, :], IDENTF[0:r, 0:r], R=[("PRMS",), ("IDENTF",)], W=psk(b, 0, r))
            dve(lambda e: e.tensor_copy(out=dst[:, 0:r], in_=PS[b][:, 0:r]), R=psk(b, 0, r), W=[key])

        if os.environ.get("SKIP_PARAMS"):
            load_T = lambda *a, **k: None
        load_T([a.rearrange("l (k c) -> (l k) c", c=128) for a in ln_d], PRMT, 128, ("PRMT",))
        dve(lambda e: e.tensor_scalar(out=PRMA[:, 0:32 * L], in0=PRMT[:, 0:32 * L], scalar1=ALPHA, scalar2=None, op0=ALU.mult),
            R=[("PRMT",)], W=[("PRMA",)])
        load_T([hn_g_d.rearrange("l (h c) -> (l h) c", c=128), pool_scale_d.rearrange("l (h c) -> (l h) c", c=128)],
               PRM2T, 32, ("PRM2T",))
        load_T([w_conv_d.rearrange("l j (k c) -> (l j k) c", c=128)], WCVT, 128, ("WCVT",))
        with nc.allow_non_contiguous_dma(reason="tiny bias"):
          if not os.environ.get("SKIP_BG"):
            T.dma("sp", d_bg, BG[0:8, 0:L], b_gate_d.rearrange("l g -> g l"), R=(), W=[("BG",)])

        def lncol(arr, l, k):
            return arr * 8 * L + l * 8 + k

        XM = int(os.environ.get('XSMOD', 2))
        for tt in range(int(os.environ.get('NX', 16))):
            xs = XS[tt % XM]
            T.dma("sp", d_x[tt % XM], xs, x_d[tt * 128:(tt + 1) * 128, :], R=(), W=[("XS", tt % XM)])
            for dg in range(2):
                b = nb()
                for di in range(4):
                    d = dg * 4 + di
                    tp(PS[b][:, di * 128:(di + 1) * 128], xs[:, d * 128:(d + 1) * 128], IDENTF[:],
                       R=[("XS", tt % XM), ("IDENTF",)], W=psk(b, di * 128, di * 128 + 128), inc=(di == 3))
                pv = PS[b][:, :].rearrange("p (a b) -> p a b", b=128)
                tb = tt // 4
                T.op("act", lambda e, pv=pv, dg=dg, tt=tt: e.mul(out=XF[:, dg * 4:dg * 4 + 4, tt * 128:(tt + 1) * 128], in_=pv, mul=ALPHA),
                     R=psk(b), W=[("XF", d, tb) for d in range(dg * 4, dg * 4 + 4)])
                dve(lambda e, pv=pv, dg=dg, tt=tt: e.tensor_copy(out=XB[:, dg * 4:dg * 4 + 4, tt * 128:(tt + 1) * 128], in_=pv),
                    R=psk(b), W=[("XB", d, tb) for d in range(dg * 4, dg * 4 + 4)])

        gate_tails = {}

        def phase_gates(l, s):
            cols = slice(s * SEG, (s + 1) * SEG)
            gw = GW[l]
            if s == 0:
                dve(lambda e: e.memset(MFULL[0:1, :, :], 0.0), R=(), W=[("MFULL",)])
            b = nb()
            for k in range(8):
                mm(PS[b][0:8, 0:512], gw[:, k, :], XB[:, k, cols], k == 0, k == 7,
                   R=[("GW", l), ("XB", k, s)], W=psk(b))
            act(G8[0:8, :], PS[b][0:8, 0:512], AF.Identity, R=psk(b) + [("BG",)], W=[("G8",)], bias=BG[0:8, l:l + 1])
            T.dma("sp", d_g[0], gscr_d[l, s], G8[0:8, :], R=[("G8",)], W=[("gscr",)])
            T.dma("sp", d_g[1], GI[0:16, :], gscr_d[l, s, 0:4, :].rearrange("g (c t) -> (g c) t", t=128), R=[("gscr",)], W=[("GI",)])
            T.dma("sp", d_g[2], GF[0:16, :], gscr_d[l, s, 4:8, :].rearrange("g (c t) -> (g c) t", t=128), R=[("gscr",)], W=[("GF",)])
            def t0():
                act(GF[:, :], GF[:, :], AF.Exp, R=[("GF",)], W=[("GF",)], scale=-1.0)
                act(GF[:, :], GF[:, :], AF.Ln, R=[("GF",)], W=[("GF",)], bias=1.0)
                dve(lambda e: e.tensor_tensor_scan(out=NA[:, :], data0=ONESF[0:16, :], data1=GF[:, :], initial=0.0,
                                                   op0=ALU.mult, op1=ALU.add), R=[("GF",), ("ONESF",)], W=[("NA",)])
                dve(lambda e: e.tensor_tensor(out=GI[:, :], in0=GI[:, :], in1=NA[:, :], op=ALU.add), R=[("GI",), ("NA",)], W=[("GI",)])
                dve(lambda e: e.tensor_reduce(out=GM[:, 2:3], in_=GI[:, :], axis=AX.X, op=ALU.max), R=[("GI",)], W=[("GM", 2)])
                dve(lambda e: e.tensor_scalar(out=GM[:, 0:1], in0=NA[:, 127:128], scalar1=-1.0, scalar2=None, op0=ALU.mult),
                    R=[("NA",)], W=[("GM", 0)])
                dve(lambda e: e.tensor_tensor(out=GM[:, 1:2], in0=GM[:, 2:3], in1=GM[:, 0:1], op=ALU.add),
                    R=[("GM", 2), ("GM", 0)], W=[("GM", 1)])

            def t1():
                b2 = nb()
                tp(PS[b2][0:1, 0:16], GM[0:16, 0:1], IDENTF[0:16, 0:16], R=[("GM", 0), ("IDENTF",)], W=psk(b2, 0, 16), inc=False)
                tp(PS[b2][0:1, 16:32], GM[0:16, 1:2], IDENTF[0:16, 0:16], R=[("GM", 1), ("IDENTF",)], W=psk(b2, 16, 32))
                dve(lambda e: e.tensor_copy(out=ROW[0:1, 0:32], in_=PS[b2][0:1, 0:32]), R=psk(b2, 0, 32), W=[("ROW",)])
                for h in range(4):
                    dve(lambda e, h=h: e.tensor_tensor_scan(out=MFULL[0:1, h, 1:5], data0=ROW[0:1, h * 4:(h + 1) * 4],
                                                            data1=ROW[0:1, 16 + h * 4:16 + (h + 1) * 4], initial=MFULL[0:1, h, 0:1],
                                                            op0=ALU.add, op1=ALU.max), R=[("ROW",), ("MFULL",)], W=[("MFULL",)])
                mc3 = MC[0:1, :].rearrange("p (h c) -> p h c", c=4)
                tr3 = TR[0:1, :].rearrange("p (h c) -> p h c", c=4)
                row3 = ROW[0:1, 0:16].rearrange("p (h c) -> p h c", c=4)
                dve(lambda e: e.tensor_tensor(out=mc3, in0=MFULL[0:1, :, 1:5], in1=row3, op=ALU.subtract), R=[("MFULL",), ("ROW",)], W=[("MC",)])
                dve(lambda e: e.tensor_tensor(out=tr3, in0=MFULL[0:1, :, 0:4], in1=mc3, op=ALU.subtract), R=[("MFULL",), ("MC",)], W=[("TR",)])
                act(SCR[0:1, :], TR[0:1, :], AF.Exp, R=[("TR",)], W=[("SCR",)])
                dve(lambda e: e.tensor_copy(out=MFULL[0:1, :, 0:1], in_=MFULL[0:1, :, 4:5]), R=[("MFULL",)], W=[("MFULL",)])

            def t2():
                b3 = nb()
                mm(PS[b3][:, 0:16], ONESF[0:1, 0:128], SCR[0:1, 0:16], True, True, R=[("ONESF",), ("SCR",)], W=psk(b3, 0, 16))
                dve(lambda e: e.tensor_copy(out=SCB[:, :], in_=PS[b3][:, 0:16]), R=psk(b3, 0, 16), W=[("SCB",)])
                b4 = nb()
                tp(PS[b4][0:16, 0:1], MC[0:1, 0:16], IDENTF[0:1, 0:1], R=[("MC",), ("IDENTF",)], W=psk(b4, 0, 1))
                dve(lambda e: e.tensor_scalar(out=NMC[:, 0:1], in0=PS[b4][0:16, 0:1], scalar1=-1.0, scalar2=None, op0=ALU.mult),
                    R=psk(b4, 0, 1), W=[("NMC", 0)])
                dve(lambda e: e.tensor_scalar(out=NMC[:, 1:2], in0=PS[b4][0:16, 0:1], scalar1=-1.0, scalar2=LNK, op0=ALU.mult, op1=ALU.add),
                    R=psk(b4, 0, 1), W=[("NMC", 1)])
                act(GI[:, :], GI[:, :], AF.Exp, R=[("GI",), ("NMC", 1)], W=[("GI",)], bias=NMC[:, 1:2])
                act(NA[:, :], NA[:, :], AF.Exp, R=[("NA",), ("NMC", 0)], W=[("NA",)], bias=NMC[:, 0:1])

            def t3():
                b5 = nb()
                tp(PS[b5][:, 0:16], GI[0:16, :], IDENTF[0:16, 0:16], R=[("GI",), ("IDENTF",)], W=psk(b5, 0, 16), inc=False)
                tp(PS[b5][:, 16:32], NA[0:16, :], IDENTF[0:16, 0:16], R=[("NA",), ("IDENTF",)], W=psk(b5, 16, 32))
                dve(lambda e: e.tensor_copy(out=ET[:, :], in_=PS[b5][:, 0:16]), R=psk(b5, 0, 16), W=[("ET",)])
                dve(lambda e: e.tensor_copy(out=ETB[:, :], in_=PS[b5][:, 0:16]), R=psk(b5, 0, 16), W=[("ETB",)])
                dve(lambda e: e.tensor_copy(out=THRT[:, :], in_=PS[b5][:, 16:32]), R=psk(b5, 16, 32), W=[("THRT",)])


            gate_tails[(l, s)] = [t0, t1, t2, t3]

        def phase_qk(l, s, which):
            cols = slice(s * SEG, (s + 1) * SEG)
            W, wk = acquire(which)
            DST, dname = (QT, "QT") if which == "q" else (KT, "KT")
            inject = gate_tails.get((l, s), [])
            sched = {}

            def inj(hm):
                for _ in range(sched.get(hm, 0)):
                    if inject:
                        inject.pop(0)()

            def main(h):
                tile = (0 if which == "q" else 4) + h
                uq = UQ[tile % 2]
                uk = ("UQ", tile % 2)
                b = nb()
                for k in range(8):
                    mm(PS[b][:, :], W[:, k, h * 128:(h + 1) * 128], XB[:, k, cols], k == 0, k == 7, R=[wk, ("XB", k, s)], W=psk(b))
                if s == 0:
                    dve(lambda e, uq=uq: e.memset(uq[:, 0:3], 0.0), R=(), W=[uk])
                else:
                    dve(lambda e, uq=uq, tile=tile: e.tensor_copy(out=uq[:, 0:3], in_=HALO[:, tile, :]), R=[("HALO", tile)], W=[uk])
                act(uq[:, 3:515], PS[b][:, :], AF.Identity, R=psk(b), W=[uk])
                if s < NSEG - 1:
                    dve(lambda e, uq=uq, tile=tile: e.tensor_copy(out=HALO[:, tile, :], in_=uq[:, 512:515]), R=[uk], W=[("HALO", tile)])
                dg = DG[tile % 2]
                for j in range(4):
                    c = l * 32 + j * 8 + tile
                    dve(lambda e, dg=dg, j=j, c=c: e.tensor_scalar(out=dg[:, j, :], in0=IDENTB[:, :], scalar1=WCVT[:, c:c + 1],
                                                                   scalar2=None, op0=ALU.mult),
                        R=[("IDENTB",), ("WCVT",)], W=[("DG", tile % 2, j)])

            def conv(h):
                tile = (0 if which == "q" else 4) + h
                uq = UQ[tile % 2]
                uk = ("UQ", tile % 2)
                dg = DG[tile % 2]
                b2 = nb()
                for j in range(4):
                    mm(PS[b2][:, :], dg[:, j, :], uq[:, j:j + 512], j == 0, j == 3, R=[("DG", tile % 2, j), uk], W=psk(b2))
                act(DST[:, h, :], PS[b2][:, :], AF.Silu, R=psk(b2), W=[(dname, h)])

            main(0)
            inj(0)
            for h in range(4):
                if h + 1 < 4:
                    main(h + 1)
                    inj(h + 1)
                conv(h)
            release()

        def phase_ktok(l, s):
            for h in range(4):
                b3 = nb()
                pb = psbf(b3)
                for c in range(4):
                    tp(pb[:, c * 128:(c + 1) * 128], KT[:, h, c * 128:(c + 1) * 128], IDENTB[:, :],
                       R=[("KT", h), ("IDENTB",)], W=psk(b3, 0, 256), inc=(c == 3))
                dve(lambda e, h=h, pb=pb: e.tensor_copy(out=KTOK[:, :, h, :], in_=pb[:, :].rearrange("p (c d) -> p c d", d=128)),
                    R=psk(b3, 0, 256), W=[("KTOK", h)])

        def phase_o(l, s):
            cols = slice(s * SEG, (s + 1) * SEG)
            W, wk = acquire("o")
            for h in range(4):
                b = nb()
                for k in range(8):
                    mm(PS[b][:, :], W[:, k, h * 128:(h + 1) * 128], XB[:, k, cols], k == 0, k == 7, R=[wk, ("XB", k, s)], W=psk(b))
                act(YM[:, h, :], PS[b][:, :], AF.Sigmoid, R=psk(b), W=[("YM", h)])
                if h == 3:
                    tl = gate_tails.get((l, s), [])
                    if tl:
                        tl.pop(0)()

            release()
            phase_ktok(l, s)

        pool_w = {}

        def pool_prep(l, s):
            if s == 0:
                T.dma("sp", d_wp, WPF[:, :, :], w_pool_d[l].rearrange("g c d -> c g d"), R=(), W=[("WPF",)])
                for g in range(4):
                    win = 2 ** (g + 1)
                    dve(lambda e, g=g, win=win: e.tensor_scalar(out=WPA[:, g, :], in0=WPF[:, g, :], scalar1=(1.0 / win - 1.0),
                                                                 scalar2=None, op0=ALU.mult), R=[("WPF",)], W=[("WPA", g)])
                    dve(lambda e, g=g, win=win: e.tensor_scalar(out=WPBt[:, g, :], in0=WPF[:, g, :], scalar1=1.0 / win,
                                                                 scalar2=None, op0=ALU.mult), R=[("WPF",)], W=[("WPB", g)])
                dve(lambda e: e.tensor_copy(out=WPC[:, :, :], in_=WPF[:, :, :]), R=[("WPF",)], W=[("WPC",)])

        def pool_inproj(l, s, g, bank=None):
            cols = slice(s * SEG, (s + 1) * SEG)
            if g == 0:
                pool_w["w"] = acquire("p")
            W, wk = pool_w["w"]
            b = nb() if bank is None else bank
            for k in range(8):
                mm(PS[b][:, :], W[:, k, g * 128:(g + 1) * 128], XB[:, k, cols], k == 0, k == 7, R=[wk, ("XB", k, s)], W=psk(b))
            if s > 0:
                act(PB[:, g, 0:16], PHALO[:, g, :], AF.Identity, R=[("PHALO", g)], W=[("PB", g)])
            act(PB[:, g, 16:528], PS[b][:, :], AF.Identity, R=psk(b), W=[("PB", g)])
            if s < NSEG - 1:
                act(PHALO[:, g, :], PB[:, g, 512:528], AF.Identity, R=[("PB", g)], W=[("PHALO", g)])
            if g == 3:
                release()

        def pool_group(l, s, g):
            win = 2 ** (g + 1)
            c0 = win - 1 if s == 0 else 0
            b = nb()
            mm(PS[b][:, c0:512], WPA[:, g, :], PB[:, g, 16 + c0:528], True, False, R=[("WPA", g), ("PB", g)], W=psk(b), inc=False)
            for j in range(1, win):
                mm(PS[b][:, c0:512], WPBt[:, g, :], PB[:, g, 16 + c0 - j:528 - j], False, j == win - 1,
                   R=[("WPB", g), ("PB", g)], W=psk(b))
            if s == 0:
                dve(lambda e, g=g, c0=c0: e.tensor_tensor_scan(out=CS[:, 0:c0], data0=ONESF[:, 0:c0], data1=PB[:, g, 16:16 + c0],
                                                               initial=0.0, op0=ALU.mult, op1=ALU.add),
                    R=[("PB", g), ("ONESF",)], W=[("CS",)])
                dve(lambda e, c0=c0: e.tensor_tensor(out=CS[:, 0:c0], in0=CS[:, 0:c0], in1=INVC[:, 0:c0], op=ALU.mult),
                    R=[("CS",), ("INVC",)], W=[("CS",)])
                dve(lambda e, g=g, c0=c0: e.tensor_tensor(out=DFB[:, 0:c0], in0=CS[:, 0:c0], in1=PB[:, g, 16:16 + c0], op=ALU.subtract),
                    R=[("CS",), ("PB", g)], W=[("DFB",)])
                mm(PS[b][:, 0:c0], WPC[:, g, :], DFB[:, 0:c0], True, True, R=[("WPC",), ("DFB",)], W=psk(b))
            c = 4 * L + l * 4 + g
            act(YP[:, g, :], PS[b][:, :], AF.Identity, R=psk(b) + [("PRM2T",)], W=[("YP", g)], scale=PRM2T[:, c:c + 1])
            pop_side(1)

        def phase_v(l, s):
            while reserved:
                pop_side(1)
            W, wk = acquire("v")
            et3 = ET[:, :].rearrange("p (h c) -> p h c", c=4)
            tl = gate_tails.get((l, s), [])
            if len(tl) == 4:
                tl.pop(0)()
            banks = []
            for c in range(4):
                tt = s * 4 + c
                b = nb()
                banks.append(b)
                for k in range(8):
                    mm(PS[b][:, :], XB[:, k, tt * 128:(tt + 1) * 128], W[:, k, :], k == 0, k == 7, R=[wk, ("XB", k, s)], W=psk(b))
                if c >= 1 and tl:
                    tl.pop(0)()
            while tl:
                tl.pop(0)()
            for c in range(4):
                b = banks[c]
                pv = PS[b][:, :].rearrange("p (h d) -> p h d", d=128)
                dve(lambda e, c=c, pv=pv: e.tensor_tensor(out=VP[:, c, :, :], in0=pv,
                                                          in1=et3[:, :, c].unsqueeze(2).to_broadcast([128, 4, 128]), op=ALU.mult),
                    R=psk(b) + [("ET",)], W=[("VP", c)])
            release()

        def phase_mlstm(l, s):
            while reserved:
                pop_side(1)
            pool_prep(l, s)
            scb3 = SCB[:, :].rearrange("p (h c) -> p h c", c=4)
            bO = [0, 1, 2, 3]
            bX = 4
            rot = [5, 6, 7]
            pX = PS[bX]
            kX = [("PS", bX)]
            def smv(i):
                return SM[:, i, :]

            def smc(i, c):
                return SM[:, i, :].rearrange("p (h c) -> p h c", c=4)[:, :, c]
            DNA, DN, RDN, SUM_, SSQ, MEAN_, EX2, VAR, R2, TT, SS, NBB = range(12)

            def stats(c):
                dve(lambda e: e.tensor_reduce(out=smc(SUM_, c), in_=pOs[c], axis=AX.X, op=ALU.add), R=psk(bO[c]), W=[("SM", SUM_)])
                act(SQs[c % 2][:, :, :], pOs[c], AF.Square, R=psk(bO[c]), W=[("SQ", c % 2)])
                dve(lambda e: e.tensor_reduce(out=smc(SSQ, c), in_=SQs[c % 2][:, :, :], axis=AX.X, op=ALU.add), R=[("SQ", c % 2)], W=[("SM", SSQ)])

            ri = 0
            pOs = []
            for c in range(4):
                first = (s == 0 and c == 0)
                tc_ = slice(c * 128, (c + 1) * 128)
                bS = rot[ri % 3]
                bDC = rot[(ri + 1) % 3]
                ri += 2
                pS = PS[bS][:, :].rearrange("p (h d) -> p h d", d=128)
                pDC = PS[bDC][:, :].rearrange("p (h d) -> p h d", d=128)
                pO = PS[bO[c]][:, :].rearrange("p (h d) -> p h d", d=128)
                pOs.append(pO)
                at = ATs[c % 2]
                ak = ("AT", c % 2)
                for h in range(4):
                    mm(pS[:, h, :], KT[:, h, tc_], QT[:, h, tc_], True, True, R=[("KT", h), ("QT", h)], W=psk(bS), inc=(h == 3))
                dve(lambda e, pS=pS, at=at: e.tensor_tensor(out=at[:, :, :], in0=pS, in1=MASK[:, :].unsqueeze(1).to_broadcast([128, 4, 128]),
                                                            op=ALU.mult), R=psk(bS) + [("MASK",)], W=[ak])
                for h in range(4):
                    col = h * 4 + c
                    mm(pDC[:, h, :], KTOK[:, c, h, :], VP[:, c, h, :], True, True, R=[("KTOK", h), ("VP", c)], W=psk(bDC), inc=False)
                    mm(pX[:, 16 + h:17 + h], KTOK[:, c, h, :], ETB[:, col:col + 1], True, True, R=[("KTOK", h), ("ETB",)], W=kX, inc=(h == 3))
                if not first:
                    dve(lambda e, c=c: e.tensor_tensor(out=C[:, :, :], in0=C[:, :, :],
                                                       in1=scb3[:, :, c].unsqueeze(2).to_broadcast([128, 4, 129]), op=ALU.mult),
                        R=[("C",), ("SCB",)], W=[("C",)])
                    act(CB[:, :, :], C[:, :, :], AF.Identity, R=[("C",)], W=[("CB",)])
                if c >= 1:
                    stats(c - 1)
                for h in range(4):
                    col = h * 4 + c
                    mm(pO[:, h, :], at[:, h, :], VP[:, c, h, :], True, first, R=[ak, ("VP", c)], W=psk(bO[c]), inc=False)
                    if not first:
                        mm(pO[:, h, :], QT[:, h, tc_], CB[:, h, 0:128], False, True, R=[("QT", h), ("CB",)], W=psk(bO[c]), inc=False)
                    mm(pX[:, col:col + 1], at[:, h, :], ETB[:, col:col + 1], True, first, R=[ak, ("ETB",)], W=kX, inc=(first and h == 3))
                    if not first:
                        mm(pX[:, col:col + 1], QT[:, h, tc_], CB[:, h, 128:129], False, True, R=[("QT", h), ("CB",)], W=kX, inc=(h == 3))
                if first:
                    dve(lambda e, pDC=pDC: e.tensor_copy(out=C[:, :, 0:128], in_=pDC), R=psk(bDC), W=[("C",)])
                    dve(lambda e: e.tensor_copy(out=C[:, :, 128:129], in_=pX[:, 16:20].unsqueeze(2)), R=kX, W=[("C",)])
                else:
                    dve(lambda e, pDC=pDC: e.tensor_tensor(out=C[:, :, 0:128], in0=C[:, :, 0:128], in1=pDC, op=ALU.add),
                        R=psk(bDC) + [("C",)], W=[("C",)])
                    dve(lambda e: e.tensor_tensor(out=C[:, :, 128:129], in0=C[:, :, 128:129], in1=pX[:, 16:20].unsqueeze(2), op=ALU.add),
                        R=kX + [("C",)], W=[("C",)])
            reserved.update((0, 1, 2, 3, 4))
            dve(lambda e: e.tensor_tensor(out=smv(DNA), in0=pX[:, 0:16], in1=THRT[:, 0:16], op=ALU.max), R=kX + [("THRT",)], W=[("SM", DNA)])
            dve(lambda e: e.scalar_tensor_tensor(out=smv(DN), in0=pX[:, 0:16], scalar=-1.0, in1=smv(DNA), op0=ALU.mult, op1=ALU.max),
                R=kX + [("SM", DNA)], W=[("SM", DN)])
            dve(lambda e: e.reciprocal(out=smv(RDN), in_=smv(DN)), R=[("SM", DN)], W=[("SM", RDN)])
            stats(3)
            pool_inproj(l, s, 0, bank=5)
            pool_inproj(l, s, 1, bank=7)
            dve(lambda e: e.tensor_scalar(out=smv(MEAN_), in0=smv(SUM_), scalar1=1.0 / 128, scalar2=None, op0=ALU.mult),
                R=[("SM", SUM_)], W=[("SM", MEAN_)])
            dve(lambda e: e.tensor_tensor(out=smv(EX2), in0=smv(MEAN_), in1=smv(MEAN_), op=ALU.mult), R=[("SM", MEAN_)], W=[("SM", EX2)])
            dve(lambda e: e.scalar_tensor_tensor(out=smv(VAR), in0=smv(SSQ), scalar=1.0 / 128, in1=smv(EX2), op0=ALU.mult, op1=ALU.subtract),
                R=[("SM", SSQ), ("SM", EX2)], W=[("SM", VAR)])
            dve(lambda e: e.tensor_tensor(out=smv(R2), in0=smv(RDN), in1=smv(RDN), op=ALU.mult), R=[("SM", RDN)], W=[("SM", R2)])
            dve(lambda e: e.tensor_tensor(out=smv(TT), in0=smv(R2), in1=smv(VAR), op=ALU.mult), R=[("SM", R2), ("SM", VAR)], W=[("SM", TT)])
            act(smv(TT), smv(TT), AF.Ln, R=[("SM", TT)], W=[("SM", TT)], bias=LN_EPS)
            act(smv(TT), smv(TT), AF.Exp, R=[("SM", TT)], W=[("SM", TT)], scale=-0.5)
            dve(lambda e: e.tensor_tensor(out=smv(SS), in0=smv(TT), in1=smv(RDN), op=ALU.mult), R=[("SM", TT), ("SM", RDN)], W=[("SM", SS)])
            dve(lambda e: e.scalar_tensor_tensor(out=smv(NBB), in0=smv(MEAN_), scalar=-1.0, in1=smv(SS), op0=ALU.mult, op1=ALU.mult),
                R=[("SM", MEAN_), ("SM", SS)], W=[("SM", NBB)])
            for c in (0, 1, 2, 3):
                for h in range(4):
                    col = h * 4 + c
                    if c < 1:
                        act(HN[:, c, h, :], pOs[c][:, h, :], AF.Identity, R=psk(bO[c]) + [("SM", SS), ("SM", NBB)], W=[("HN", c)],
                            scale=SM[:, SS, col:col + 1], bias=SM[:, NBB, col:col + 1])
                    else:
                        dve(lambda e, c=c, h=h, col=col: e.tensor_scalar(out=HN[:, c, h, :], in0=pOs[c][:, h, :], scalar1=SM[:, SS, col:col + 1],
                                                                         scalar2=SM[:, NBB, col:col + 1], op0=ALU.mult, op1=ALU.add),
                            R=psk(bO[c]) + [("SM", SS), ("SM", NBB)], W=[("HN", c)])
            pool_inproj(l, s, 2, bank=6)
            pool_inproj(l, s, 3, bank=5)
            for bb in (0, 1, 2, 3, 4):
                reserved.discard(bb)
            pool_group(l, s, 0)
            pool_group(l, s, 1)
            for pr in range(2):
                bT = nb()
                pHT = PS[bT][:, :].bitcast(BF16).rearrange("p (c h d) -> p c h d", h=4, d=128)
                for cc in range(2):
                    for h in range(4):
                        tp(pHT[:, cc, h, :], HN[:, 2 * pr + cc, h, :], IDENTB[:, :], R=[("HN", 2 * pr + cc), ("IDENTB",)], W=psk(bT),
                           inc=(cc == 1 and h == 3))
                for h in range(4):
                    cc_ = l * 4 + h
                    ymv = YM[:, h, pr * 256:(pr + 1) * 256].rearrange("p (c d) -> p c d", d=128)
                    dve(lambda e, h=h, cc_=cc_, ymv=ymv, pHT=pHT: e.scalar_tensor_tensor(out=ymv, in0=pHT[:, :, h, :], scalar=PRM2T[:, cc_:cc_ + 1],
                                                                                     in1=ymv, op0=ALU.mult, op1=ALU.mult),
                        R=psk(bT) + [("YM", h), ("PRM2T",)], W=[("YM", h)])

            pool_group(l, s, 2)
            pool_group(l, s, 3)

        def phase_outproj(l, s):
            cols = slice(s * SEG, (s + 1) * SEG)
            Ws = [acquire("wo0"), acquire("wo1")]
            for m in range(8):
                W, wk = Ws[m // 4]
                b = nb()
                for k in range(8):
                    rhs = YM[:, k, :] if k < 4 else YP[:, k - 4, :]
                    rk = ("YM", k) if k < 4 else ("YP", k - 4)
                    mm(PS[b][:, :], W[:, k, (m % 4) * 128:(m % 4 + 1) * 128], rhs, k == 0, k == 7, R=[wk, rk], W=psk(b))
                dve(lambda e, m=m, b=b: e.tensor_tensor(out=XF[:, m, cols], in0=XF[:, m, cols], in1=PS[b][:, :], op=ALU.add),
                    R=psk(b) + [("XF", m, s)], W=[("XF", m, s)])
                if m % 2 == 1:
                    pop_side(1)
                if m == 3:
                    release()
            release()

        ln_ctr = [0]

        def ln_a(l, which, tb, st, part):
            cols = slice(tb * SEG, (tb + 1) * SEG)
            if part == 0:
                st["j"] = ln_ctr[0] % 2
                ln_ctr[0] += 1
                st["b1"], st["b2"] = nb(), nb()
                reserved.update((st["b1"], st["b2"]))
            j, b1, b2 = st["j"], st["b1"], st["b2"]
            if part < 2:
                for d in range(part * 4, part * 4 + 4):
                    i = d % 3
                    act(RBt[i], XF[:, d, cols], AF.Identity, R=[("XF", d, tb)], W=[("RB", i)])
                    act(RSQt[i], XF[:, d, cols], AF.Square, R=[("XF", d, tb)], W=[("RSQ", i)])
                    mm(PS[b1][:, :], ONESM[:, :], RBt[i], d == 0, d == 7, R=[("ONESM",), ("RB", i)], W=psk(b1), inc=True)
                    mm(PS[b2][:, :], ONESM[:, :], RSQt[i], d == 0, d == 7, R=[("ONESM",), ("RSQ", i)], W=psk(b2), inc=True)
                return
            mean, rstd = MEANt[j], RSTDt[j]
            dve(lambda e: e.tensor_copy(out=mean, in_=PS[b1][:, :]), R=psk(b1), W=[("MEAN", j)])
            dve(lambda e: e.tensor_tensor(out=rstd, in0=mean, in1=mean, op=ALU.mult), R=[("MEAN", j)], W=[("RSTD", j)])
            dve(lambda e: e.tensor_tensor(out=rstd, in0=PS[b2][:, :], in1=rstd, op=ALU.subtract), R=psk(b2) + [("RSTD", j)], W=[("RSTD", j)])
            act(rstd, rstd, AF.Ln, R=[("RSTD", j)], W=[("RSTD", j)], bias=LN_EPS)
            act(rstd, rstd, AF.Exp, R=[("RSTD", j)], W=[("RSTD", j)], scale=-0.5)
            reserved.discard(b1)
            reserved.discard(b2)

        def ln_b(l, which, tb, j, d0, d1, scaled):
            ga, ba = (0, 1) if which == 1 else (2, 3)
            PA = PRMA if scaled else PRMT
            cols = slice(tb * SEG, (tb + 1) * SEG)
            mean, rstd = MEANt[j], RSTDt[j]
            for d in range(d0, d1):
                xf = XF[:, d, cols]
                dve(lambda e, xf=xf: e.tensor_tensor(out=xf, in0=xf, in1=mean, op=ALU.subtract), R=[("XF", d, tb), ("MEAN", j)], W=[("XF", d, tb)])
                dve(lambda e, xf=xf: e.tensor_tensor(out=xf, in0=xf, in1=rstd, op=ALU.mult), R=[("XF", d, tb), ("RSTD", j)], W=[("XF", d, tb)])
                cg, cb = lncol(ga, l, d), lncol(ba, l, d)
                act(XB[:, d, cols], xf, AF.Identity, R=[("XF", d, tb), ("PRMT",)], W=[("XB", d, tb)],
                    scale=PRMT[:, cg:cg + 1], bias=PRMT[:, cb:cb + 1])
                act(xf, xf, AF.Identity, R=[("XF", d, tb), ("PRMA",), ("PRMT",)], W=[("XF", d, tb)],
                    scale=PA[:, cg:cg + 1], bias=PA[:, cb:cb + 1])

        ffn_w = {}

        def ffn1(l, q, tb):
            if tb == 0:
                ffn_w["w1"] = [acquire("w10"), acquire("w11")]
            W1 = ffn_w["w1"]
            cols = slice(tb * SEG, (tb + 1) * SEG)
            for j in range(8):
                W, wk = W1[j // 4]
                b = nb()
                for k in range(8):
                    mm(PS[b][:, :], W[:, k, (j % 4) * 128:(j % 4 + 1) * 128], XB[:, k, cols], k == 0, k == 7,
                       R=[wk, ("XB", k, tb)], W=psk(b))
                hr = HR[j % 2]
                act(hr[:, :], PS[b][:, :], AF.Relu, R=psk(b), W=[("HR", j % 2)])
                dve(lambda e, hr=hr, j=j: e.tensor_tensor(out=H[:, j, cols], in0=hr[:, :], in1=hr[:, :], op=ALU.mult),
                    R=[("HR", j % 2)], W=[("H", j, tb)])
                if j % 2 == 1:
                    pop_side(1)
            if tb == 3:
                release()
                release()

        def ffn2(l, q, tb):
            if tb == 0:
                ffn_w["w2"] = [acquire("w20"), acquire("w21")]
            W2 = ffn_w["w2"]
            cols = slice(tb * SEG, (tb + 1) * SEG)
            for grp in ((0, 1, 2), (3, 4, 5), (6, 7)):
                banks = [nb() for _ in grp]
                for j in range(8):
                    W, wk = W2[j // 4]
                    for mi, m in enumerate(grp):
                        mm(PS[banks[mi]][:, :], W[:, j % 4, m * 128:(m + 1) * 128], H[:, j, cols], j == 0, j == 7,
                           R=[wk, ("H", j, tb)], W=psk(banks[mi]))
                for mi, m in enumerate(grp):
                    b = banks[mi]
                    dve(lambda e, m=m, b=b: e.tensor_tensor(out=XF[:, m, cols], in0=XF[:, m, cols], in1=PS[b][:, :], op=ALU.add),
                        R=psk(b) + [("XF", m, tb)], W=[("XF", m, tb)])
                pop_side(1)
            if tb == 3:
                release()
                release()

        from collections import deque
        side = deque()
        NOSIDE = bool(os.environ.get("NOSIDE"))

        def enqueue_ln(l, which, tb, scaled=True):
            tag = (l, which, tb)
            st = {}
            for part in range(3):
                side.append((tag, lambda part=part: ln_a(l, which, tb, st, part)))
            for d0 in range(0, 8, 2):
                side.append((tag, lambda d0=d0: ln_b(l, which, tb, st["j"], d0, d0 + 2, scaled)))
            if NOSIDE:
                drain(tag)

        def drain(tag=None):
            if tag is not None and not any(t == tag for t, _ in side):
                return
            while side:
                t, fn = side.popleft()
                fn()
                if tag is not None and not any(tt == tag for tt, _ in side):
                    break

        def pop_side(n=1):
            for _ in range(n):
                if side:
                    side.popleft()[1]()

        steps = []
        for l in range(L):
            for s in range(NSEG):
                need = (l - 1, 2, s) if l > 0 else None
                steps.append((need, lambda l=l, s=s: phase_gates(l, s)))
                steps.append((None, lambda l=l, s=s: phase_qk(l, s, "q")))
                steps.append((None, lambda l=l, s=s: phase_qk(l, s, "k")))
                steps.append((None, lambda l=l, s=s: phase_o(l, s)))
                steps.append((None, lambda l=l, s=s: phase_v(l, s)))
                steps.append((None, lambda l=l, s=s: phase_mlstm(l, s)))
                steps.append((None, lambda l=l, s=s: (phase_outproj(l, s), enqueue_ln(l, 1, s))))
            last_scaled = not (l == L - 1 and last_unscaled)
            for q in range(4):
                for tb in range(4):
                    steps.append(((l, 1, tb), lambda l=l, q=q, tb=tb: ffn1(l, q, tb)))
                for tb in range(4):
                    if q < 3:
                        steps.append((None, lambda l=l, q=q, tb=tb: ffn2(l, q, tb)))
                    else:
                        steps.append((None, lambda l=l, q=q, tb=tb, sc=last_scaled: (ffn2(l, q, tb), enqueue_ln(l, 2, tb, sc))))
        for i, (need, st) in enumerate(steps):
            if dbg is not None and i >= dbg:
                break
            if need is not None:
                drain(need)
            st()
        if dbg is not None:
            drain(None)

        for tt in range(int(os.environ.get('NOUT', 16))):
            xs = XS[tt % 2]
            tb = tt // 4
            if dbg is None and tt % 4 == 0:
                drain((L - 1, 2, tb))
            for dg in range(2):
                b = nb()
                for di in range(4):
                    d = dg * 4 + di
                    tp(PS[b][:, di * 128:(di + 1) * 128], XF[:, d, tt * 128:(tt + 1) * 128], IDENTF[:, :],
                       R=[("XF", d, tb), ("IDENTF",)], W=psk(b, di * 128, di * 128 + 128), inc=(di == 3))
                if dg == 0:
                    act(xs[:, 0:512], PS[b][:, :], AF.Identity, R=psk(b), W=[("XS", tt % 2)])
                else:
                    dve(lambda e, xs=xs, b=b: e.tensor_copy(out=xs[:, 512:1024], in_=PS[b][:, :]), R=psk(b), W=[("XS", tt % 2)])
            T.dma("sp", d_y[tt % 2], y_d[tt * 128:(tt + 1) * 128, :], xs, R=[("XS", tt % 2)], W=[("Y", tt)])
        drain(None)
        for dd in d_y:
            nc.sync.wait_ge(dd["sem"], dd["cnt"])
        build.stats = dict(n_wait=T.n_wait, cnt={k: v["cnt"] for k, v in T.E.items()})
    return nc


_NAMES = ["x", "w_in", "b_gate", "w_conv", "hn_g", "w_pool", "pool_scale", "w_out",
          "ln1_g", "ln1_b", "w_ff1", "w_ff2", "ln2_g", "ln2_b"]


def kernel(**inputs):
    arrs = {k: np.ascontiguousarray(np.asarray(inputs[k], dtype=np.float32)) for k in _NAMES}
    B = arrs["x"].shape[0]
    L = arrs["w_in"].shape[0]
    nc = build(depth=L)
    in_maps = []
    for b in range(B):
        m = {k: arrs[k] for k in _NAMES if k != "x"}
        m["x"] = np.ascontiguousarray(arrs["x"][b])
        in_maps.append(m)
    res = run_bass_kernel_spmd(nc, in_maps, core_ids=list(range(B)))
    return np.stack([res.results[b]["y"] for b in range(B)], axis=0).astype(np.float32)
```

```python
import math, os
from contextlib import ExitStack
import numpy as np
import concourse.bass as bass
import concourse.mybir as mybir
from concourse.bass_utils import run_bass_kernel_spmd

F32 = mybir.dt.float32
BF16 = mybir.dt.bfloat16
AF = mybir.ActivationFunctionType
ALU = mybir.AluOpType
AX = mybir.AxisListType

S = 2048
D = 1024
DIN = 2568
DFF = 4096
NSEG = 4
SEG = 512
ALPHA_FULL = (2.0 * 4) ** 0.25
LN_EPS = 1e-5
LNK = math.log(128.0 ** -0.5)
RING = 4


class Trk:
    def __init__(self, nc, es):
        self.nc, self.es = nc, es
        self.E = {}
        self.lw = {}
        self.rd = {}
        self.reg_owner = {}
        self.reg_ev = {}
        self.region_of = lambda k: None
        self.n_wait = 0
        self.snap = {}

    def add_eng(self, name, eng, own=True):
        sem = self.es.enter_context(self.nc.semaphore("s_" + name)) if own else None
        self.E[name] = dict(eng=eng, sem=sem, cnt=0, seen={}, id=name)

    def dsem(self, name):
        return dict(sem=self.es.enter_context(self.nc.semaphore("d_" + name)), cnt=0, id="d_" + name)

    def _deps(self, R, W):
        deps = {}

        def add(ev):
            if ev is None:
                return
            sid, sh, v = ev
            if sid not in deps or deps[sid][1] < v:
                deps[sid] = (sh, v)

        for k in R:
            add(self.lw.get(k))
        for k in W:
            add(self.lw.get(k))
            for sid, (sh, v) in self.rd.get(k, {}).items():
                add((sid, sh, v))
        for k in list(R) + list(W):
            rg = self.region_of(k)
            if rg is not None:
                reg, tag = rg
                if self.reg_owner.get(reg) != tag:
                    for sid, (sh, v) in self.reg_ev.get(reg, {}).items():
                        add((sid, sh, v))
        return deps

    def _wait(self, e, deps, en):
        for sid, (sh, v) in deps.items():
            if sid == en:
                if en == "pe" or v > e["cnt"] or os.environ.get("NO_OWN_WAIT"):
                    continue
            if e["seen"].get(sid, 0) >= v:
                continue
            e["eng"].wait_ge(sh, v)
            e["seen"][sid] = v
            self.n_wait += 1
            for k2, v2 in self.snap.get((sid, v), {}).items():
                if e["seen"].get(k2, 0) < v2:
                    e["seen"][k2] = v2

    def _record(self, ev, R, W, seen=None):
        sid, sh, v = ev
        if seen is not None:
            d0 = self.snap.setdefault((sid, v), {})
            for k2, v2 in seen.items():
                if d0.get(k2, 0) < v2:
                    d0[k2] = v2
        for k in W:
            self.lw[k] = ev
            self.rd[k] = {}
        for k in R:
            d = self.rd.setdefault(k, {})
            if sid not in d or d[sid][1] < v:
                d[sid] = (sh, v)
        for k in list(R) + list(W):
            rg = self.region_of(k)
            if rg is not None:
                reg, tag = rg
                if self.reg_owner.get(reg) != tag:
                    self.reg_owner[reg] = tag
                    self.reg_ev[reg] = {}
                d = self.reg_ev[reg]
                if sid not in d or d[sid][1] < v:
                    d[sid] = (sh, v)

    def op(self, en, fn, R=(), W=(), inc=True):
        e = self.E[en]
        W = list(W) + [k for k in R if k[0] == "PS" and k not in W]
        R = [k for k in R if k[0] != "PS"]
        self._wait(e, self._deps(R, W), en)
        ins = fn(e["eng"])
        if inc:
            ins.then_inc(e["sem"], 1)
            e["cnt"] += 1
            ev = (en, e["sem"], e["cnt"])
        else:
            ev = (en, e["sem"], e["cnt"] + 1)
        self._record(ev, R, W, seen=e["seen"])

    def dma(self, qn, ds, out, in_, R=(), W=(), **kw):
        e = self.E[qn]
        self._wait(e, self._deps(R, W), qn + "_q")
        e["eng"].dma_start(out=out, in_=in_, **kw).then_inc(ds["sem"], 16)
        ds["cnt"] += 16
        self._record((ds["id"], ds["sem"], ds["cnt"]), R, W, seen=e["seen"])

    def wait_all(self, qn, keys):
        e = self.E[qn]
        self._wait(e, self._deps(keys, ()), qn + "_q")


def build(depth=4, last_unscaled=True, dbg=None):
    L = depth
    ALPHA = ALPHA_FULL
    nc = bass.Bass("TRN2", target_bir_lowering=False)
    x_d = nc.dram_tensor("x", [S, D], F32, kind="ExternalInput").ap()
    w_in_d = nc.dram_tensor("w_in", [L, D, DIN], F32, kind="ExternalInput").ap()
    b_gate_d = nc.dram_tensor("b_gate", [L, 8], F32, kind="ExternalInput").ap()
    w_conv_d = nc.dram_tensor("w_conv", [L, 4, D], F32, kind="ExternalInput").ap()
    hn_g_d = nc.dram_tensor("hn_g", [L, 512], F32, kind="ExternalInput").ap()
    w_pool_d = nc.dram_tensor("w_pool", [L, 4, 128, 128], F32, kind="ExternalInput").ap()
    pool_scale_d = nc.dram_tensor("pool_scale", [L, 512], F32, kind="ExternalInput").ap()
    w_out_d = nc.dram_tensor("w_out", [L, D, D], F32, kind="ExternalInput").ap()
    ln_d = [nc.dram_tensor(n, [L, D], F32, kind="ExternalInput").ap() for n in ("ln1_g", "ln1_b", "ln2_g", "ln2_b")]
    w_ff1_d = nc.dram_tensor("w_ff1", [L, D, DFF], F32, kind="ExternalInput").ap()
    w_ff2_d = nc.dram_tensor("w_ff2", [L, DFF, D], F32, kind="ExternalInput").ap()
    y_d = nc.dram_tensor("y", [S, D], F32, kind="ExternalOutput").ap()
    gscr_d = nc.dram_tensor("gscr", [L, NSEG, 8, SEG], F32, kind="Internal").ap()

    es = ExitStack()
    with es:
        def sb(name, shape, dt):
            return es.enter_context(nc.sbuf_tensor(name, shape, dt))

        T = Trk(nc, es)
        T.add_eng("pe", nc.tensor)
        T.add_eng("act", nc.scalar)
        T.add_eng("dve", nc.vector)
        T.add_eng("pool", nc.gpsimd)
        T.add_eng("sp", nc.sync, own=False)

        XF = sb("XF", [128, 8, S], F32)
        XB = sb("XB", [128, 8, S], BF16)
        RG = [sb(f"RG{i}", [128, 4096], BF16) for i in range(RING)]
        ARENA = sb("ARENA", [128, 16384], BF16)
        PS = [es.enter_context(nc.psum_tensor(f"PS{i}", [128, 512], F32)) for i in range(8)]

        def av(c0, n, b):
            return ARENA[:, c0:c0 + n].rearrange("p (a b) -> p a b", b=b)
        QT = av(0, 2048, 512)
        KT = av(2048, 2048, 512)
        KTOK = ARENA[:, 4096:6144].rearrange("p (c h d) -> p c h d", h=4, d=128)
        VP = ARENA[:, 6144:8192].rearrange("p (c h d) -> p c h d", h=4, d=128)
        YM = av(8192, 2048, 512)
        YP = av(10240, 2048, 512)
        PB = av(12288, 2112, 528)
        UQ = [ARENA[:, 14400:14916], ARENA[:, 14916:15432]]
        H = ARENA[:, :].rearrange("p (j t) -> p j t", t=S)
        XS = [ARENA[:, i * 2048:(i + 1) * 2048].bitcast(F32) for i in range(4)]

        AB_NAMES = {"QT", "KT", "KTOK", "VP", "YM", "YP", "PB", "UQ"}
        def region_of(k):
            n = k[0]
            if n in AB_NAMES:
                return ("AR", "AB")
            if n == "H":
                return ("AR", "FFN")
            if n == "XS":
                return ("AR", "IO")
            return None
        T.region_of = region_of

        RBt = [sb(f"RB{i}", [128, 512], BF16)[:, :] for i in range(3)]
        RSQt = [sb(f"RSQ{i}", [128, 512], BF16)[:, :] for i in range(3)]
        MEANt = [sb(f"MEAN{i}", [128, 512], F32)[:, :] for i in range(2)]
        RSTDt = [sb(f"RSTD{i}", [128, 512], F32)[:, :] for i in range(2)]
        DG = [sb(f"DG{i}", [128, 4, 128], BF16) for i in range(2)]
        G8 = sb("G8", [8, SEG], F32)
        GI = sb("GI", [16, 128], F32)
        GF = sb("GF", [16, 128], F32)
        NA = sb("NA", [16, 128], F32)
        GM = sb("GM", [16, 4], F32)
        ROW = sb("ROW", [1, 32], F32)
        MFULL = sb("MFULL", [1, 4, 5], F32)
        MC = sb("MC", [1, 16], F32)
        TR = sb("TR", [1, 16], F32)
        SCR = sb("SCR", [1, 16], F32)
        NMC = sb("NMC", [16, 2], F32)
        SCB = sb("SCB", [128, 16], F32)
        ET = sb("ET", [128, 16], F32)
        ETB = sb("ETB", [128, 16], BF16)
        THRT = sb("THRT", [128, 16], F32)
        C = sb("C", [128, 4, 129], F32)
        CB = sb("CB", [128, 4, 129], BF16)
        ATs = [sb(f"AT{i}", [128, 4, 128], BF16) for i in range(2)]
        HN = sb("HN", [128, 4, 4, 128], BF16)
        SQs = [sb(f"SQ{i}", [128, 4, 128], F32) for i in range(2)]
        SM = sb("SM", [128, 12, 16], F32)
        HALO = sb("HALO", [128, 8, 3], BF16)
        PHALO = sb("PHALO", [128, 4, 16], BF16)
        CS = sb("CS", [128, 16], F32)
        DFB = sb("DFB", [128, 16], BF16)
        HR = [sb(f"HR{i}", [128, 512], BF16) for i in range(2)]
        IDENTB = sb("IDENTB", [128, 128], BF16)
        IDENTF = sb("IDENTF", [128, 128], F32)
        MASK = sb("MASK", [128, 128], BF16)
        ONESM = sb("ONESM", [128, 128], BF16)
        ONESF = sb("ONESF", [128, 128], F32)
        INVC = sb("INVC", [128, 16], F32)
        PRMS = sb("PRMS", [128, 128], F32)
        PRMT = sb("PRMT", [128, 128], F32)
        PRMA = sb("PRMA", [128, 128], F32)
        PRM2T = sb("PRM2T", [128, 32], F32)
        WCVT = sb("WCVT", [128, 128], F32)
        BG = sb("BG", [8, 4], F32)
        WPF = sb("WPF", [128, 4, 128], F32)
        WPA = sb("WPA", [128, 4, 128], BF16)
        WPBt = sb("WPBt", [128, 4, 128], BF16)
        WPC = sb("WPC", [128, 4, 128], BF16)
        GW = [sb(f"GW{i}", [128, 8, 8], BF16) for i in range(L)]

        d_x = [T.dsem(f"x{i}") for i in range(4)]
        d_y = [T.dsem("y0"), T.dsem("y1")]
        d_prm = T.dsem("prm")
        d_bg = T.dsem("bg")
        d_g = [T.dsem("g0"), T.dsem("g1"), T.dsem("g2")]
        d_gw = T.dsem("gw")
        d_wp = T.dsem("wp")
        d_w = [T.dsem(f"w{i}") for i in range(RING)]

        bank_ctr = [0]
        reserved = set()

        def nb():
            while True:
                b = bank_ctr[0] % 8
                bank_ctr[0] += 1
                if b not in reserved:
                    return b

        def psk(b, c0=0, c1=512):
            return [("PS", b)]

        def psbf(b):
            return PS[b][:, 0:256].bitcast(BF16)

        def mm(out, lhsT, rhs, start, stop, R, W, inc=None):
            if inc is None:
                inc = stop
            T.op("pe", lambda e: e.matmul(out, lhsT=lhsT, rhs=rhs, start=start, stop=stop), R=R, W=W, inc=inc)

        def tp(out, in_, ident, R, W, inc=True):
            T.op("pe", lambda e: e.transpose(out=out, in_=in_, identity=ident), R=R, W=W, inc=inc)

        def act(out, in_, func, R, W, bias=None, scale=None):
            kw = {}
            if bias is not None:
                kw["bias"] = bias
            if scale is not None:
                kw["scale"] = scale
            T.op("act", lambda e: e.activation(out=out, in_=in_, func=func, **kw), R=R, W=W)

        def dve(fn, R, W):
            T.op("dve", fn, R=R, W=W)

        plan = []
        for l in range(L):
            for s in range(NSEG):
                for kind, c0 in (("q", 0), ("k", 512), ("o", 1544), ("v", 1024), ("p", 2056)):
                    plan.append((kind, w_in_d[l][:, c0:c0 + 512].rearrange("(k p) n -> p k n", p=128), "kn"))
                for i in range(2):
                    plan.append((f"wo{i}", w_out_d[l][:, i * 512:(i + 1) * 512].rearrange("(k p) n -> p k n", p=128), "kn"))
            for q in range(4):
                for i in range(2):
                    c0 = q * 1024 + i * 512
                    plan.append((f"w1{i}", w_ff1_d[l][:, c0:c0 + 512].rearrange("(k p) n -> p k n", p=128), "kn"))
                for i in range(2):
                    r0 = q * 1024 + i * 512
                    plan.append((f"w2{i}", w_ff2_d[l][r0:r0 + 512, :].rearrange("(j p) n -> p j n", p=128), "jn"))
        ring = dict(acq=0, rel=0)

        def slot_view(slot, lay):
            if lay == "kn":
                return RG[slot][:, :].rearrange("p (k n) -> p k n", n=512)
            return RG[slot][:, :].rearrange("p (j n) -> p j n", n=1024)

        def issue_fill(i):
            kind, src, lay = plan[i]
            slot = i % RING
            T.dma("pool", d_w[slot], slot_view(slot, lay), src, R=(), W=[("W", slot)])

        def acquire(kind):
            i = ring["acq"]
            assert plan[i][0] == kind, (plan[i][0], kind)
            ring["acq"] += 1
            slot = i % RING
            return slot_view(slot, plan[i][2]), ("W", slot)

        def release():
            i = ring["rel"]
            ring["rel"] += 1
            if i + RING < len(plan):
                issue_fill(i + RING)

        T.op("pool", lambda e: e.memset(ONESF[:], 1.0), W=[("ONESF",)])
        T.op("pool", lambda e: e.memset(ONESM[:], 1.0 / 1024.0), W=[("ONESM",)])
        T.op("pool", lambda e: e.affine_select(out=IDENTF[:], in_=ONESF[:], pattern=[[1, 128]], compare_op=ALU.is_equal,
                                               fill=0.0, base=0, channel_multiplier=-1), R=[("ONESF",)], W=[("IDENTF",)])
        T.op("pool", lambda e: e.affine_select(out=IDENTB[:], in_=ONESF[:], pattern=[[1, 128]], compare_op=ALU.is_equal,
                                               fill=0.0, base=0, channel_multiplier=-1), R=[("ONESF",)], W=[("IDENTB",)])
        T.op("pool", lambda e: e.affine_select(out=MASK[:], in_=ONESF[:], pattern=[[1, 128]], compare_op=ALU.is_ge,
                                               fill=0.0, base=0, channel_multiplier=-1), R=[("ONESF",)], W=[("MASK",)])
        for t in range(16):
            T.op("pool", lambda e, t=t: e.memset(INVC[:, t:t + 1], 1.0 / (t + 1)), W=[("INVC",)])

        d_gws = [T.dsem(f"gw{i}") for i in range(L)]
        def _gw(l_):
            with nc.allow_non_contiguous_dma(reason="gate weights, 32B rows"):
                T.dma("pool", d_gws[l_], GW[l_][:, :, :], w_in_d[l_][:, 1536:1544].rearrange("(k p) n -> p k n", p=128),
                      R=(), W=[("GW", l_)])
        _gw(0)
        for i in range(min(RING, len(plan))):
            if not os.environ.get("SKIP_PREFETCH"):
                issue_fill(i)
        for l_ in range(1, L):
            _gw(l_)

        def load_T(rows_list, dst, ncols, key):
            r = 0
            for src in rows_list:
                n = src.shape[0]
                T.dma("sp", d_prm, PRMS[r:r + n, :], src, R=(), W=[("PRMS",)])
                r += n
            b = nb()
            tp(PS[b][:, 0:r], PRMS[0:r, :], IDENTF[0:r, 0:r], R=[("PRMS",), ("IDENTF",)], W=psk(b, 0, r))
            dve(lambda e: e.tensor_copy(out=dst[:, 0:r], in_=PS[b][:, 0:r]), R=psk(b, 0, r), W=[key])

        if os.environ.get("SKIP_PARAMS"):
            load_T = lambda *a, **k: None
        load_T([a.rearrange("l (k c) -> (l k) c", c=128) for a in ln_d], PRMT, 128, ("PRMT",))
        dve(lambda e: e.tensor_scalar(out=PRMA[:, 0:32 * L], in0=PRMT[:, 0:32 * L], scalar1=ALPHA, scalar2=None, op0=ALU.mult),
            R=[("PRMT",)], W=[("PRMA",)])
        load_T([hn_g_d.rearrange("l (h c) -> (l h) c", c=128), pool_scale_d.rearrange("l (h c) -> (l h) c", c=128)],
               PRM2T, 32, ("PRM2T",))
        load_T([w_conv_d.rearrange("l j (k c) -> (l j k) c", c=128)], WCVT, 128, ("WCVT",))
        with nc.allow_non_contiguous_dma(reason="tiny bias"):
          if not os.environ.get("SKIP_BG"):
            T.dma("sp", d_bg, BG[0:8, 0:L], b_gate_d.rearrange("l g -> g l"), R=(), W=[("BG",)])

        def lncol(arr, l, k):
            return arr * 8 * L + l * 8 + k

        XM = int(os.environ.get('XSMOD', 4))
        for tt in range(int(os.environ.get('NX', 16))):
            xs = XS[tt % XM]
            T.dma("sp", d_x[tt % XM], xs, x_d[tt * 128:(tt + 1) * 128, :], R=(), W=[("XS", tt % XM)])
            for dg in range(2):
                b = nb()
                for di in range(4):
                    d = dg * 4 + di
                    tp(PS[b][:, di * 128:(di + 1) * 128], xs[:, d * 128:(d + 1) * 128], IDENTF[:],
                       R=[("XS", tt % XM), ("IDENTF",)], W=psk(b, di * 128, di * 128 + 128), inc=(di == 3))
                pv = PS[b][:, :].rearrange("p (a b) -> p a b", b=128)
                tb = tt // 4
                T.op("act", lambda e, pv=pv, dg=dg, tt=tt: e.mul(out=XF[:, dg * 4:dg * 4 + 4, tt * 128:(tt + 1) * 128], in_=pv, mul=ALPHA),
                     R=psk(b), W=[("XF", d, tb) for d in range(dg * 4, dg * 4 + 4)])
                dve(lambda e, pv=pv, dg=dg, tt=tt: e.tensor_copy(out=XB[:, dg * 4:dg * 4 + 4, tt * 128:(tt + 1) * 128], in_=pv),
                    R=psk(b), W=[("XB", d, tb) for d in range(dg * 4, dg * 4 + 4)])

        gate_tails = {}

        def phase_gates(l, s):
            cols = slice(s * SEG, (s + 1) * SEG)
            gw = GW[l]
            if s == 0:
                dve(lambda e: e.memset(MFULL[0:1, :, :], 0.0), R=(), W=[("MFULL",)])
            b = nb()
            for k in range(8):
                mm(PS[b][0:8, 0:512], gw[:, k, :], XB[:, k, cols], k == 0, k == 7,
                   R=[("GW", l), ("XB", k, s)], W=psk(b))
            act(G8[0:8, :], PS[b][0:8, 0:512], AF.Identity, R=psk(b) + [("BG",)], W=[("G8",)], bias=BG[0:8, l:l + 1])
            T.dma("sp", d_g[0], gscr_d[l, s], G8[0:8, :], R=[("G8",)], W=[("gscr",)])
            T.dma("sp", d_g[1], GI[0:16, :], gscr_d[l, s, 0:4, :].rearrange("g (c t) -> (g c) t", t=128), R=[("gscr",)], W=[("GI",)])
            T.dma("sp", d_g[2], GF[0:16, :], gscr_d[l, s, 4:8, :].rearrange("g (c t) -> (g c) t", t=128), R=[("gscr",)], W=[("GF",)])
            def t0():
                act(GF[:, :], GF[:, :], AF.Exp, R=[("GF",)], W=[("GF",)], scale=-1.0)
                act(GF[:, :], GF[:, :], AF.Ln, R=[("GF",)], W=[("GF",)], bias=1.0)
                dve(lambda e: e.tensor_tensor_scan(out=NA[:, :], data0=ONESF[0:16, :], data1=GF[:, :], initial=0.0,
                                                   op0=ALU.mult, op1=ALU.add), R=[("GF",), ("ONESF",)], W=[("NA",)])
                dve(lambda e: e.tensor_tensor(out=GI[:, :], in0=GI[:, :], in1=NA[:, :], op=ALU.add), R=[("GI",), ("NA",)], W=[("GI",)])
                dve(lambda e: e.tensor_reduce(out=GM[:, 2:3], in_=GI[:, :], axis=AX.X, op=ALU.max), R=[("GI",)], W=[("GM", 2)])
                dve(lambda e: e.tensor_scalar(out=GM[:, 0:1], in0=NA[:, 127:128], scalar1=-1.0, scalar2=None, op0=ALU.mult),
                    R=[("NA",)], W=[("GM", 0)])
                dve(lambda e: e.tensor_tensor(out=GM[:, 1:2], in0=GM[:, 2:3], in1=GM[:, 0:1], op=ALU.add),
                    R=[("GM", 2), ("GM", 0)], W=[("GM", 1)])

            def t1():
                b2 = nb()
                tp(PS[b2][0:1, 0:16], GM[0:16, 0:1], IDENTF[0:16, 0:16], R=[("GM", 0), ("IDENTF",)], W=psk(b2, 0, 16), inc=False)
                tp(PS[b2][0:1, 16:32], GM[0:16, 1:2], IDENTF[0:16, 0:16], R=[("GM", 1), ("IDENTF",)], W=psk(b2, 16, 32))
                dve(lambda e: e.tensor_copy(out=ROW[0:1, 0:32], in_=PS[b2][0:1, 0:32]), R=psk(b2, 0, 32), W=[("ROW",)])
                for h in range(4):
                    dve(lambda e, h=h: e.tensor_tensor_scan(out=MFULL[0:1, h, 1:5], data0=ROW[0:1, h * 4:(h + 1) * 4],
                                                            data1=ROW[0:1, 16 + h * 4:16 + (h + 1) * 4], initial=MFULL[0:1, h, 0:1],
                                                            op0=ALU.add, op1=ALU.max), R=[("ROW",), ("MFULL",)], W=[("MFULL",)])
                mc3 = MC[0:1, :].rearrange("p (h c) -> p h c", c=4)
                tr3 = TR[0:1, :].rearrange("p (h c) -> p h c", c=4)
                row3 = ROW[0:1, 0:16].rearrange("p (h c) -> p h c", c=4)
                dve(lambda e: e.tensor_tensor(out=mc3, in0=MFULL[0:1, :, 1:5], in1=row3, op=ALU.subtract), R=[("MFULL",), ("ROW",)], W=[("MC",)])
                dve(lambda e: e.tensor_tensor(out=tr3, in0=MFULL[0:1, :, 0:4], in1=mc3, op=ALU.subtract), R=[("MFULL",), ("MC",)], W=[("TR",)])
                act(SCR[0:1, :], TR[0:1, :], AF.Exp, R=[("TR",)], W=[("SCR",)])
                dve(lambda e: e.tensor_copy(out=MFULL[0:1, :, 0:1], in_=MFULL[0:1, :, 4:5]), R=[("MFULL",)], W=[("MFULL",)])

            def t2():
                b3 = nb()
                mm(PS[b3][:, 0:16], ONESF[0:1, 0:128], SCR[0:1, 0:16], True, True, R=[("ONESF",), ("SCR",)], W=psk(b3, 0, 16))
                dve(lambda e: e.tensor_copy(out=SCB[:, :], in_=PS[b3][:, 0:16]), R=psk(b3, 0, 16), W=[("SCB",)])
                b4 = nb()
                tp(PS[b4][0:16, 0:1], MC[0:1, 0:16], IDENTF[0:1, 0:1], R=[("MC",), ("IDENTF",)], W=psk(b4, 0, 1))
                dve(lambda e: e.tensor_scalar(out=NMC[:, 0:1], in0=PS[b4][0:16, 0:1], scalar1=-1.0, scalar2=None, op0=ALU.mult),
                    R=psk(b4, 0, 1), W=[("NMC", 0)])
                dve(lambda e: e.tensor_scalar(out=NMC[:, 1:2], in0=PS[b4][0:16, 0:1], scalar1=-1.0, scalar2=LNK, op0=ALU.mult, op1=ALU.add),
                    R=psk(b4, 0, 1), W=[("NMC", 1)])
                act(GI[:, :], GI[:, :], AF.Exp, R=[("GI",), ("NMC", 1)], W=[("GI",)], bias=NMC[:, 1:2])
                act(NA[:, :], NA[:, :], AF.Exp, R=[("NA",), ("NMC", 0)], W=[("NA",)], bias=NMC[:, 0:1])

            def t3():
                b5 = nb()
                tp(PS[b5][:, 0:16], GI[0:16, :], IDENTF[0:16, 0:16], R=[("GI",), ("IDENTF",)], W=psk(b5, 0, 16), inc=False)
                tp(PS[b5][:, 16:32], NA[0:16, :], IDENTF[0:16, 0:16], R=[("NA",), ("IDENTF",)], W=psk(b5, 16, 32))
                dve(lambda e: e.tensor_copy(out=ET[:, :], in_=PS[b5][:, 0:16]), R=psk(b5, 0, 16), W=[("ET",)])
                dve(lambda e: e.tensor_copy(out=ETB[:, :], in_=PS[b5][:, 0:16]), R=psk(b5, 0, 16), W=[("ETB",)])
                dve(lambda e: e.tensor_copy(out=THRT[:, :], in_=PS[b5][:, 16:32]), R=psk(b5, 16, 32), W=[("THRT",)])


            gate_tails[(l, s)] = [t0, t1, t2, t3]

        def phase_qk(l, s, which):
            cols = slice(s * SEG, (s + 1) * SEG)
            W, wk = acquire(which)
            DST, dname = (QT, "QT") if which == "q" else (KT, "KT")
            inject = gate_tails.get((l, s), [])
            sched = {}

            def inj(hm):
                for _ in range(sched.get(hm, 0)):
                    if inject:
                        inject.pop(0)()

            def main(h):
                tile = (0 if which == "q" else 4) + h
                uq = UQ[tile % 2]
                uk = ("UQ", tile % 2)
                b = nb()
                for k in range(8):
                    mm(PS[b][:, :], W[:, k, h * 128:(h + 1) * 128], XB[:, k, cols], k == 0, k == 7, R=[wk, ("XB", k, s)], W=psk(b))
                if s == 0:
                    dve(lambda e, uq=uq: e.memset(uq[:, 0:3], 0.0), R=(), W=[uk])
                else:
                    dve(lambda e, uq=uq, tile=tile: e.tensor_copy(out=uq[:, 0:3], in_=HALO[:, tile, :]), R=[("HALO", tile)], W=[uk])
                act(uq[:, 3:515], PS[b][:, :], AF.Identity, R=psk(b), W=[uk])
                if s < NSEG - 1:
                    dve(lambda e, uq=uq, tile=tile: e.tensor_copy(out=HALO[:, tile, :], in_=uq[:, 512:515]), R=[uk], W=[("HALO", tile)])
                dg = DG[tile % 2]
                for j in range(4):
                    c = l * 32 + j * 8 + tile
                    dve(lambda e, dg=dg, j=j, c=c: e.tensor_scalar(out=dg[:, j, :], in0=IDENTB[:, :], scalar1=WCVT[:, c:c + 1],
                                                                   scalar2=None, op0=ALU.mult),
                        R=[("IDENTB",), ("WCVT",)], W=[("DG", tile % 2, j)])

            def conv(h):
                tile = (0 if which == "q" else 4) + h
                uq = UQ[tile % 2]
                uk = ("UQ", tile % 2)
                dg = DG[tile % 2]
                b2 = nb()
                for j in range(4):
                    mm(PS[b2][:, :], dg[:, j, :], uq[:, j:j + 512], j == 0, j == 3, R=[("DG", tile % 2, j), uk], W=psk(b2))
                act(DST[:, h, :], PS[b2][:, :], AF.Silu, R=psk(b2), W=[(dname, h)])

            main(0)
            inj(0)
            for h in range(4):
                if h + 1 < 4:
                    main(h + 1)
                    inj(h + 1)
                conv(h)
            release()

        def phase_ktok(l, s):
            for h in range(4):
                b3 = nb()
                pb = psbf(b3)
                for c in range(4):
                    tp(pb[:, c * 128:(c + 1) * 128], KT[:, h, c * 128:(c + 1) * 128], IDENTB[:, :],
                       R=[("KT", h), ("IDENTB",)], W=psk(b3, 0, 256), inc=(c == 3))
                dve(lambda e, h=h, pb=pb: e.tensor_copy(out=KTOK[:, :, h, :], in_=pb[:, :].rearrange("p (c d) -> p c d", d=128)),
                    R=psk(b3, 0, 256), W=[("KTOK", h)])

        def phase_o(l, s):
            cols = slice(s * SEG, (s + 1) * SEG)
            W, wk = acquire("o")
            for h in range(4):
                b = nb()
                for k in range(8):
                    mm(PS[b][:, :], W[:, k, h * 128:(h + 1) * 128], XB[:, k, cols], k == 0, k == 7, R=[wk, ("XB", k, s)], W=psk(b))
                act(YM[:, h, :], PS[b][:, :], AF.Sigmoid, R=psk(b), W=[("YM", h)])
                if h == 3:
                    tl = gate_tails.get((l, s), [])
                    if tl:
                        tl.pop(0)()

            release()
            phase_ktok(l, s)

        pool_w = {}

        def pool_prep(l, s):
            if s == 0:
                T.dma("sp", d_wp, WPF[:, :, :], w_pool_d[l].rearrange("g c d -> c g d"), R=(), W=[("WPF",)])
                for g in range(4):
                    win = 2 ** (g + 1)
                    dve(lambda e, g=g, win=win: e.tensor_scalar(out=WPA[:, g, :], in0=WPF[:, g, :], scalar1=(1.0 / win - 1.0),
                                                                 scalar2=None, op0=ALU.mult), R=[("WPF",)], W=[("WPA", g)])
                    dve(lambda e, g=g, win=win: e.tensor_scalar(out=WPBt[:, g, :], in0=WPF[:, g, :], scalar1=1.0 / win,
                                                                 scalar2=None, op0=ALU.mult), R=[("WPF",)], W=[("WPB", g)])
                dve(lambda e: e.tensor_copy(out=WPC[:, :, :], in_=WPF[:, :, :]), R=[("WPF",)], W=[("WPC",)])

        def pool_inproj(l, s, g, bank=None):
            cols = slice(s * SEG, (s + 1) * SEG)
            if g == 0:
                pool_w["w"] = acquire("p")
            W, wk = pool_w["w"]
            b = nb() if bank is None else bank
            for k in range(8):
                mm(PS[b][:, :], W[:, k, g * 128:(g + 1) * 128], XB[:, k, cols], k == 0, k == 7, R=[wk, ("XB", k, s)], W=psk(b))
            if s > 0:
                act(PB[:, g, 0:16], PHALO[:, g, :], AF.Identity, R=[("PHALO", g)], W=[("PB", g)])
            act(PB[:, g, 16:528], PS[b][:, :], AF.Identity, R=psk(b), W=[("PB", g)])
            if s < NSEG - 1:
                act(PHALO[:, g, :], PB[:, g, 512:528], AF.Identity, R=[("PB", g)], W=[("PHALO", g)])
            if g == 3:
                release()

        def pool_group(l, s, g):
            win = 2 ** (g + 1)
            c0 = win - 1 if s == 0 else 0
            b = nb()
            mm(PS[b][:, c0:512], WPA[:, g, :], PB[:, g, 16 + c0:528], True, False, R=[("WPA", g), ("PB", g)], W=psk(b), inc=False)
            for j in range(1, win):
                mm(PS[b][:, c0:512], WPBt[:, g, :], PB[:, g, 16 + c0 - j:528 - j], False, j == win - 1,
                   R=[("WPB", g), ("PB", g)], W=psk(b))
            if s == 0:
                dve(lambda e, g=g, c0=c0: e.tensor_tensor_scan(out=CS[:, 0:c0], data0=ONESF[:, 0:c0], data1=PB[:, g, 16:16 + c0],
                                                               initial=0.0, op0=ALU.mult, op1=ALU.add),
                    R=[("PB", g), ("ONESF",)], W=[("CS",)])
                dve(lambda e, c0=c0: e.tensor_tensor(out=CS[:, 0:c0], in0=CS[:, 0:c0], in1=INVC[:, 0:c0], op=ALU.mult),
                    R=[("CS",), ("INVC",)], W=[("CS",)])
                dve(lambda e, g=g, c0=c0: e.tensor_tensor(out=DFB[:, 0:c0], in0=CS[:, 0:c0], in1=PB[:, g, 16:16 + c0], op=ALU.subtract),
                    R=[("CS",), ("PB", g)], W=[("DFB",)])
                mm(PS[b][:, 0:c0], WPC[:, g, :], DFB[:, 0:c0], True, True, R=[("WPC",), ("DFB",)], W=psk(b))
            c = 4 * L + l * 4 + g
            act(YP[:, g, :], PS[b][:, :], AF.Identity, R=psk(b) + [("PRM2T",)], W=[("YP", g)], scale=PRM2T[:, c:c + 1])
            pop_side(1)

        def phase_v(l, s):
            while reserved:
                pop_side(1)
            W, wk = acquire("v")
            et3 = ET[:, :].rearrange("p (h c) -> p h c", c=4)
            tl = gate_tails.get((l, s), [])
            if len(tl) == 4:
                tl.pop(0)()
            banks = []
            for c in range(4):
                tt = s * 4 + c
                b = nb()
                banks.append(b)
                for k in range(8):
                    mm(PS[b][:, :], XB[:, k, tt * 128:(tt + 1) * 128], W[:, k, :], k == 0, k == 7, R=[wk, ("XB", k, s)], W=psk(b))
                if c >= 1 and tl:
                    tl.pop(0)()
            while tl:
                tl.pop(0)()
            for c in range(4):
                b = banks[c]
                pv = PS[b][:, :].rearrange("p (h d) -> p h d", d=128)
                dve(lambda e, c=c, pv=pv: e.tensor_tensor(out=VP[:, c, :, :], in0=pv,
                                                          in1=et3[:, :, c].unsqueeze(2).to_broadcast([128, 4, 128]), op=ALU.mult),
                    R=psk(b) + [("ET",)], W=[("VP", c)])
            release()

        def phase_mlstm(l, s):
            while reserved:
                pop_side(1)
            pool_prep(l, s)
            scb3 = SCB[:, :].rearrange("p (h c) -> p h c", c=4)
            bO = [0, 1, 2, 3]
            bX = 4
            rot = [5, 6, 7]
            pX = PS[bX]
            kX = [("PS", bX)]
            def smv(i):
                return SM[:, i, :]

            def smc(i, c):
                return SM[:, i, :].rearrange("p (h c) -> p h c", c=4)[:, :, c]
            DNA, DN, RDN, SUM_, SSQ, MEAN_, EX2, VAR, R2, TT, SS, NBB = range(12)

            def stats(c):
                dve(lambda e: e.tensor_reduce(out=smc(SUM_, c), in_=pOs[c], axis=AX.X, op=ALU.add), R=psk(bO[c]), W=[("SM", SUM_)])
                act(SQs[c % 2][:, :, :], pOs[c], AF.Square, R=psk(bO[c]), W=[("SQ", c % 2)])
                dve(lambda e: e.tensor_reduce(out=smc(SSQ, c), in_=SQs[c % 2][:, :, :], axis=AX.X, op=ALU.add), R=[("SQ", c % 2)], W=[("SM", SSQ)])

            ri = 0
            pOs = []
            for c in range(4):
                first = (s == 0 and c == 0)
                tc_ = slice(c * 128, (c + 1) * 128)
                bS = rot[ri % 3]
                bDC = rot[(ri + 1) % 3]
                ri += 2
                pS = PS[bS][:, :].rearrange("p (h d) -> p h d", d=128)
                pDC = PS[bDC][:, :].rearrange("p (h d) -> p h d", d=128)
                pO = PS[bO[c]][:, :].rearrange("p (h d) -> p h d", d=128)
                pOs.append(pO)
                at = ATs[c % 2]
                ak = ("AT", c % 2)
                for h in range(4):
                    mm(pS[:, h, :], KT[:, h, tc_], QT[:, h, tc_], True, True, R=[("KT", h), ("QT", h)], W=psk(bS), inc=(h == 3))
                dve(lambda e, pS=pS, at=at: e.tensor_tensor(out=at[:, :, :], in0=pS, in1=MASK[:, :].unsqueeze(1).to_broadcast([128, 4, 128]),
                                                            op=ALU.mult), R=psk(bS) + [("MASK",)], W=[ak])
                for h in range(4):
                    col = h * 4 + c
                    mm(pDC[:, h, :], KTOK[:, c, h, :], VP[:, c, h, :], True, True, R=[("KTOK", h), ("VP", c)], W=psk(bDC), inc=False)
                    mm(pX[:, 16 + h:17 + h], KTOK[:, c, h, :], ETB[:, col:col + 1], True, True, R=[("KTOK", h), ("ETB",)], W=kX, inc=(h == 3))
                if not first:
                    dve(lambda e, c=c: e.tensor_tensor(out=C[:, :, :], in0=C[:, :, :],
                                                       in1=scb3[:, :, c].unsqueeze(2).to_broadcast([128, 4, 129]), op=ALU.mult),
                        R=[("C",), ("SCB",)], W=[("C",)])
                    act(CB[:, :, :], C[:, :, :], AF.Identity, R=[("C",)], W=[("CB",)])
                if c >= 1:
                    stats(c - 1)
                for h in range(4):
                    col = h * 4 + c
                    mm(pO[:, h, :], at[:, h, :], VP[:, c, h, :], True, first, R=[ak, ("VP", c)], W=psk(bO[c]), inc=False)
                    if not first:
                        mm(pO[:, h, :], QT[:, h, tc_], CB[:, h, 0:128], False, True, R=[("QT", h), ("CB",)], W=psk(bO[c]), inc=False)
                    mm(pX[:, col:col + 1], at[:, h, :], ETB[:, col:col + 1], True, first, R=[ak, ("ETB",)], W=kX, inc=(first and h == 3))
                    if not first:
                        mm(pX[:, col:col + 1], QT[:, h, tc_], CB[:, h, 128:129], False, True, R=[("QT", h), ("CB",)], W=kX, inc=(h == 3))
                if first:
                    dve(lambda e, pDC=pDC: e.tensor_copy(out=C[:, :, 0:128], in_=pDC), R=psk(bDC), W=[("C",)])
                    dve(lambda e: e.tensor_copy(out=C[:, :, 128:129], in_=pX[:, 16:20].unsqueeze(2)), R=kX, W=[("C",)])
                else:
                    dve(lambda e, pDC=pDC: e.tensor_tensor(out=C[:, :, 0:128], in0=C[:, :, 0:128], in1=pDC, op=ALU.add),
                        R=psk(bDC) + [("C",)], W=[("C",)])
                    dve(lambda e: e.tensor_tensor(out=C[:, :, 128:129], in0=C[:, :, 128:129], in1=pX[:, 16:20].unsqueeze(2), op=ALU.add),
                        R=kX + [("C",)], W=[("C",)])
            reserved.update((0, 1, 2, 3, 4))
            dve(lambda e: e.tensor_tensor(out=smv(DNA), in0=pX[:, 0:16], in1=THRT[:, 0:16], op=ALU.max), R=kX + [("THRT",)], W=[("SM", DNA)])
            dve(lambda e: e.scalar_tensor_tensor(out=smv(DN), in0=pX[:, 0:16], scalar=-1.0, in1=smv(DNA), op0=ALU.mult, op1=ALU.max),
                R=kX + [("SM", DNA)], W=[("SM", DN)])
            dve(lambda e: e.reciprocal(out=smv(RDN), in_=smv(DN)), R=[("SM", DN)], W=[("SM", RDN)])
            stats(3)
            pool_inproj(l, s, 0, bank=5)
            pool_inproj(l, s, 1, bank=7)
            dve(lambda e: e.tensor_scalar(out=smv(MEAN_), in0=smv(SUM_), scalar1=1.0 / 128, scalar2=None, op0=ALU.mult),
                R=[("SM", SUM_)], W=[("SM", MEAN_)])
            dve(lambda e: e.tensor_tensor(out=smv(EX2), in0=smv(MEAN_), in1=smv(MEAN_), op=ALU.mult), R=[("SM", MEAN_)], W=[("SM", EX2)])
            dve(lambda e: e.scalar_tensor_tensor(out=smv(VAR), in0=smv(SSQ), scalar=1.0 / 128, in1=smv(EX2), op0=ALU.mult, op1=ALU.subtract),
                R=[("SM", SSQ), ("SM", EX2)], W=[("SM", VAR)])
            dve(lambda e: e.tensor_tensor(out=smv(R2), in0=smv(RDN), in1=smv(RDN), op=ALU.mult), R=[("SM", RDN)], W=[("SM", R2)])
            dve(lambda e: e.tensor_tensor(out=smv(TT), in0=smv(R2), in1=smv(VAR), op=ALU.mult), R=[("SM", R2), ("SM", VAR)], W=[("SM", TT)])
            act(smv(TT), smv(TT), AF.Ln, R=[("SM", TT)], W=[("SM", TT)], bias=LN_EPS)
            act(smv(TT), smv(TT), AF.Exp, R=[("SM", TT)], W=[("SM", TT)], scale=-0.5)
            dve(lambda e: e.tensor_tensor(out=smv(SS), in0=smv(TT), in1=smv(RDN), op=ALU.mult), R=[("SM", TT), ("SM", RDN)], W=[("SM", SS)])
            dve(lambda e: e.scalar_tensor_tensor(out=smv(NBB), in0=smv(MEAN_), scalar=-1.0, in1=smv(SS), op0=ALU.mult, op1=ALU.mult),
                R=[("SM", MEAN_), ("SM", SS)], W=[("SM", NBB)])
            for c in (0, 1, 2, 3):
                for h in range(4):
                    col = h * 4 + c
                    if c < 1:
                        act(HN[:, c, h, :], pOs[c][:, h, :], AF.Identity, R=psk(bO[c]) + [("SM", SS), ("SM", NBB)], W=[("HN", c)],
                            scale=SM[:, SS, col:col + 1], bias=SM[:, NBB, col:col + 1])
                    else:
                        dve(lambda e, c=c, h=h, col=col: e.tensor_scalar(out=HN[:, c, h, :], in0=pOs[c][:, h, :], scalar1=SM[:, SS, col:col + 1],
                                                                         scalar2=SM[:, NBB, col:col + 1], op0=ALU.mult, op1=ALU.add),
                            R=psk(bO[c]) + [("SM", SS), ("SM", NBB)], W=[("HN", c)])
            pool_inproj(l, s, 2, bank=6)
            pool_inproj(l, s, 3, bank=5)
            for bb in (0, 1, 2, 3, 4):
                reserved.discard(bb)
            pool_group(l, s, 0)
            pool_group(l, s, 1)
            for pr in range(2):
                bT = nb()
                pHT = PS[bT][:, :].bitcast(BF16).rearrange("p (c h d) -> p c h d", h=4, d=128)
                for cc in range(2):
                    for h in range(4):
                        tp(pHT[:, cc, h, :], HN[:, 2 * pr + cc, h, :], IDENTB[:, :], R=[("HN", 2 * pr + cc), ("IDENTB",)], W=psk(bT),
                           inc=(cc == 1 and h == 3))
                for h in range(4):
                    cc_ = l * 4 + h
                    ymv = YM[:, h, pr * 256:(pr + 1) * 256].rearrange("p (c d) -> p c d", d=128)
                    dve(lambda e, h=h, cc_=cc_, ymv=ymv, pHT=pHT: e.scalar_tensor_tensor(out=ymv, in0=pHT[:, :, h, :], scalar=PRM2T[:, cc_:cc_ + 1],
                                                                                     in1=ymv, op0=ALU.mult, op1=ALU.mult),
                        R=psk(bT) + [("YM", h), ("PRM2T",)], W=[("YM", h)])

            pool_group(l, s, 2)
            pool_group(l, s, 3)

        def phase_outproj(l, s):
            cols = slice(s * SEG, (s + 1) * SEG)
            Ws = [acquire("wo0"), acquire("wo1")]
            for m in range(8):
                W, wk = Ws[m // 4]
                b = nb()
                for k in range(8):
                    rhs = YM[:, k, :] if k < 4 else YP[:, k - 4, :]
                    rk = ("YM", k) if k < 4 else ("YP", k - 4)
                    mm(PS[b][:, :], W[:, k, (m % 4) * 128:(m % 4 + 1) * 128], rhs, k == 0, k == 7, R=[wk, rk], W=psk(b))
                dve(lambda e, m=m, b=b: e.tensor_tensor(out=XF[:, m, cols], in0=XF[:, m, cols], in1=PS[b][:, :], op=ALU.add),
                    R=psk(b) + [("XF", m, s)], W=[("XF", m, s)])
                if m % 2 == 1:
                    pop_side(1)
                if m == 3:
                    release()
            release()

        ln_ctr = [0]

        def ln_a(l, which, tb, st, part):
            cols = slice(tb * SEG, (tb + 1) * SEG)
            if part == 0:
                st["j"] = ln_ctr[0] % 2
                ln_ctr[0] += 1
                st["b1"], st["b2"] = nb(), nb()
                reserved.update((st["b1"], st["b2"]))
            j, b1, b2 = st["j"], st["b1"], st["b2"]
            if part < 2:
                for d in range(part * 4, part * 4 + 4):
                    i = d % 3
                    act(RBt[i], XF[:, d, cols], AF.Identity, R=[("XF", d, tb)], W=[("RB", i)])
                    act(RSQt[i], XF[:, d, cols], AF.Square, R=[("XF", d, tb)], W=[("RSQ", i)])
                    mm(PS[b1][:, :], ONESM[:, :], RBt[i], d == 0, d == 7, R=[("ONESM",), ("RB", i)], W=psk(b1), inc=True)
                    mm(PS[b2][:, :], ONESM[:, :], RSQt[i], d == 0, d == 7, R=[("ONESM",), ("RSQ", i)], W=psk(b2), inc=True)
                return
            mean, rstd = MEANt[j], RSTDt[j]
            dve(lambda e: e.tensor_copy(out=mean, in_=PS[b1][:, :]), R=psk(b1), W=[("MEAN", j)])
            dve(lambda e: e.tensor_tensor(out=rstd, in0=mean, in1=mean, op=ALU.mult), R=[("MEAN", j)], W=[("RSTD", j)])
            dve(lambda e: e.tensor_tensor(out=rstd, in0=PS[b2][:, :], in1=rstd, op=ALU.subtract), R=psk(b2) + [("RSTD", j)], W=[("RSTD", j)])
            act(rstd, rstd, AF.Ln, R=[("RSTD", j)], W=[("RSTD", j)], bias=LN_EPS)
            act(rstd, rstd, AF.Exp, R=[("RSTD", j)], W=[("RSTD", j)], scale=-0.5)
            reserved.discard(b1)
            reserved.discard(b2)

        def ln_b(l, which, tb, j, d0, d1, scaled):
            ga, ba = (0, 1) if which == 1 else (2, 3)
            PA = PRMA if scaled else PRMT
            cols = slice(tb * SEG, (tb + 1) * SEG)
            mean, rstd = MEANt[j], RSTDt[j]
            for d in range(d0, d1):
                xf = XF[:, d, cols]
                dve(lambda e, xf=xf: e.tensor_tensor(out=xf, in0=xf, in1=mean, op=ALU.subtract), R=[("XF", d, tb), ("MEAN", j)], W=[("XF", d, tb)])
                dve(lambda e, xf=xf: e.tensor_tensor(out=xf, in0=xf, in1=rstd, op=ALU.mult), R=[("XF", d, tb), ("RSTD", j)], W=[("XF", d, tb)])
                cg, cb = lncol(ga, l, d), lncol(ba, l, d)
                act(XB[:, d, cols], xf, AF.Identity, R=[("XF", d, tb), ("PRMT",)], W=[("XB", d, tb)],
                    scale=PRMT[:, cg:cg + 1], bias=PRMT[:, cb:cb + 1])
                act(xf, xf, AF.Identity, R=[("XF", d, tb), ("PRMA",), ("PRMT",)], W=[("XF", d, tb)],
                    scale=PA[:, cg:cg + 1], bias=PA[:, cb:cb + 1])

        ffn_w = {}

        def ffn1(l, q, tb):
            if tb == 0:
                ffn_w["w1"] = [acquire("w10"), acquire("w11")]
            W1 = ffn_w["w1"]
            cols = slice(tb * SEG, (tb + 1) * SEG)
            for j in range(8):
                W, wk = W1[j // 4]
                b = nb()
                for k in range(8):
                    mm(PS[b][:, :], W[:, k, (j % 4) * 128:(j % 4 + 1) * 128], XB[:, k, cols], k == 0, k == 7,
                       R=[wk, ("XB", k, tb)], W=psk(b))
                hr = HR[j % 2]
                act(hr[:, :], PS[b][:, :], AF.Relu, R=psk(b), W=[("HR", j % 2)])
                dve(lambda e, hr=hr, j=j: e.tensor_tensor(out=H[:, j, cols], in0=hr[:, :], in1=hr[:, :], op=ALU.mult),
                    R=[("HR", j % 2)], W=[("H", j, tb)])
                if j % 2 == 1:
                    pop_side(1)
            if tb == 3:
                release()
                release()

        def ffn2(l, q, tb):
            if tb == 0:
                ffn_w["w2"] = [acquire("w20"), acquire("w21")]
            W2 = ffn_w["w2"]
            cols = slice(tb * SEG, (tb + 1) * SEG)
            for grp in ((0, 1, 2), (3, 4, 5), (6, 7)):
                banks = [nb() for _ in grp]
                for j in range(8):
                    W, wk = W2[j // 4]
                    for mi, m in enumerate(grp):
                        mm(PS[banks[mi]][:, :], W[:, j % 4, m * 128:(m + 1) * 128], H[:, j, cols], j == 0, j == 7,
                           R=[wk, ("H", j, tb)], W=psk(banks[mi]))
                for mi, m in enumerate(grp):
                    b = banks[mi]
                    dve(lambda e, m=m, b=b: e.tensor_tensor(out=XF[:, m, cols], in0=XF[:, m, cols], in1=PS[b][:, :], op=ALU.add),
                        R=psk(b) + [("XF", m, tb)], W=[("XF", m, tb)])
                pop_side(1)
            if tb == 3:
                release()
                release()

        from collections import deque
        side = deque()
        NOSIDE = bool(os.environ.get("NOSIDE"))

        def enqueue_ln(l, which, tb, scaled=True):
            tag = (l, which, tb)
            st = {}
            for part in range(3):
                side.append((tag, lambda part=part: ln_a(l, which, tb, st, part)))
            for d0 in range(0, 8, 2):
                side.append((tag, lambda d0=d0: ln_b(l, which, tb, st["j"], d0, d0 + 2, scaled)))
            if NOSIDE:
                drain(tag)

        def drain(tag=None):
            if tag is not None and not any(t == tag for t, _ in side):
                return
            while side:
                t, fn = side.popleft()
                fn()
                if tag is not None and not any(tt == tag for tt, _ in side):
                    break

        def pop_side(n=1):
            for _ in range(n):
                if side:
                    side.popleft()[1]()

        steps = []
        for l in range(L):
            for s in range(NSEG):
                need = (l - 1, 2, s) if l > 0 else None
                steps.append((need, lambda l=l, s=s: phase_gates(l, s)))
                steps.append((None, lambda l=l, s=s: phase_qk(l, s, "q")))
                steps.append((None, lambda l=l, s=s: phase_qk(l, s, "k")))
                steps.append((None, lambda l=l, s=s: phase_o(l, s)))
                steps.append((None, lambda l=l, s=s: phase_v(l, s)))
                steps.append((None, lambda l=l, s=s: phase_mlstm(l, s)))
                steps.append((None, lambda l=l, s=s: (phase_outproj(l, s), enqueue_ln(l, 1, s))))
            last_scaled = not (l == L - 1 and last_unscaled)
            for q in range(4):
                for tb in range(4):
                    steps.append(((l, 1, tb), lambda l=l, q=q, tb=tb: ffn1(l, q, tb)))
                for tb in range(4):
                    if q < 3:
                        steps.append((None, lambda l=l, q=q, tb=tb: ffn2(l, q, tb)))
                    else:
                        steps.append((None, lambda l=l, q=q, tb=tb, sc=last_scaled: (ffn2(l, q, tb), enqueue_ln(l, 2, tb, sc))))
        for i, (need, st) in enumerate(steps):
            if dbg is not None and i >= dbg:
                break
            if need is not None:
                drain(need)
            st()
        if dbg is not None:
            drain(None)

        for tt in range(int(os.environ.get('NOUT', 16))):
            xs = XS[tt % 2]
            tb = tt // 4
            if dbg is None and tt % 4 == 0:
                drain((L - 1, 2, tb))
            for dg in range(2):
                b = nb()
                for di in range(4):
                    d = dg * 4 + di
                    tp(PS[b][:, di * 128:(di + 1) * 128], XF[:, d, tt * 128:(tt + 1) * 128], IDENTF[:, :],
                       R=[("XF", d, tb), ("IDENTF",)], W=psk(b, di * 128, di * 128 + 128), inc=(di == 3))
                if dg == 0:
                    act(xs[:, 0:512], PS[b][:, :], AF.Identity, R=psk(b), W=[("XS", tt % 2)])
                else:
                    dve(lambda e, xs=xs, b=b: e.tensor_copy(out=xs[:, 512:1024], in_=PS[b][:, :]), R=psk(b), W=[("XS", tt % 2)])
            T.dma("sp", d_y[tt % 2], y_d[tt * 128:(tt + 1) * 128, :], xs, R=[("XS", tt % 2)], W=[("Y", tt)])
        drain(None)
        for dd in d_y:
            nc.sync.wait_ge(dd["sem"], dd["cnt"])
        build.stats = dict(n_wait=T.n_wait, cnt={k: v["cnt"] for k, v in T.E.items()})
    return nc


_NAMES = ["x", "w_in", "b_gate", "w_conv", "hn_g", "w_pool", "pool_scale", "w_out",
          "ln1_g", "ln1_b", "w_ff1", "w_ff2", "ln2_g", "ln2_b"]


def kernel(**inputs):
    arrs = {k: np.ascontiguousarray(np.asarray(inputs[k], dtype=np.float32)) for k in _NAMES}
    B = arrs["x"].shape[0]
    L = arrs["w_in"].shape[0]
    nc = build(depth=L)
    in_maps = []
    for b in range(B):
        m = {k: arrs[k] for k in _NAMES if k != "x"}
        m["x"] = np.ascontiguousarray(arrs["x"][b])
        in_maps.append(m)
    res = run_bass_kernel_spmd(nc, in_maps, core_ids=list(range(B)))
    return np.stack([res.results[b]["y"] for b in range(B)], axis=0).astype(np.float32)
```

```python
import math, os
from contextlib import ExitStack
import numpy as np
import concourse.bass as bass
import concourse.mybir as mybir
from concourse.bass_utils import run_bass_kernel_spmd

F32 = mybir.dt.float32
BF16 = mybir.dt.bfloat16
AF = mybir.ActivationFunctionType
ALU = mybir.AluOpType
AX = mybir.AxisListType

S = 2048
D = 1024
DIN = 2568
DFF = 4096
NSEG = 4
SEG = 512
ALPHA_FULL = (2.0 * 4) ** 0.25
LN_EPS = 1e-5
LNK = math.log(128.0 ** -0.5)
RING = 4


class Trk:
    def __init__(self, nc, es):
        self.nc, self.es = nc, es
        self.E = {}
        self.lw = {}
        self.rd = {}
        self.reg_owner = {}
        self.reg_ev = {}
        self.region_of = lambda k: None
        self.n_wait = 0
        self.snap = {}

    def add_eng(self, name, eng, own=True):
        sem = self.es.enter_context(self.nc.semaphore("s_" + name)) if own else None
        self.E[name] = dict(eng=eng, sem=sem, cnt=0, seen={}, id=name)

    def dsem(self, name):
        return dict(sem=self.es.enter_context(self.nc.semaphore("d_" + name)), cnt=0, id="d_" + name)

    def _deps(self, R, W):
        deps = {}

        def add(ev):
            if ev is None:
                return
            sid, sh, v = ev
            if sid not in deps or deps[sid][1] < v:
                deps[sid] = (sh, v)

        for k in R:
            add(self.lw.get(k))
        for k in W:
            add(self.lw.get(k))
            for sid, (sh, v) in self.rd.get(k, {}).items():
                add((sid, sh, v))
        for k in list(R) + list(W):
            rg = self.region_of(k)
            if rg is not None:
                reg, tag = rg
                if self.reg_owner.get(reg) != tag:
                    for sid, (sh, v) in self.reg_ev.get(reg, {}).items():
                        add((sid, sh, v))
        return deps

    def _wait(self, e, deps, en):
        for sid, (sh, v) in deps.items():
            if sid == en:
                if en == "pe" or v > e["cnt"] or os.environ.get("NO_OWN_WAIT"):
                    continue
            if e["seen"].get(sid, 0) >= v:
                continue
            e["eng"].wait_ge(sh, v)
            e["seen"][sid] = v
            self.n_wait += 1
            for k2, v2 in self.snap.get((sid, v), {}).items():
                if e["seen"].get(k2, 0) < v2:
                    e["seen"][k2] = v2

    def _record(self, ev, R, W, seen=None):
        sid, sh, v = ev
        if seen is not None:
            d0 = self.snap.setdefault((sid, v), {})
            for k2, v2 in seen.items():
                if d0.get(k2, 0) < v2:
                    d0[k2] = v2
        for k in W:
            self.lw[k] = ev
            self.rd[k] = {}
        for k in R:
            d = self.rd.setdefault(k, {})
            if sid not in d or d[sid][1] < v:
                d[sid] = (sh, v)
        for k in list(R) + list(W):
            rg = self.region_of(k)
            if rg is not None:
                reg, tag = rg
                if self.reg_owner.get(reg) != tag:
                    self.reg_owner[reg] = tag
                    self.reg_ev[reg] = {}
                d = self.reg_ev[reg]
                if sid not in d or d[sid][1] < v:
                    d[sid] = (sh, v)

    def op(self, en, fn, R=(), W=(), inc=True):
        e = self.E[en]
        W = list(W) + [k for k in R if k[0] == "PS" and k not in W]
        R = [k for k in R if k[0] != "PS"]
        self._wait(e, self._deps(R, W), en)
        ins = fn(e["eng"])
        if inc:
            ins.then_inc(e["sem"], 1)
            e["cnt"] += 1
            ev = (en, e["sem"], e["cnt"])
        else:
            ev = (en, e["sem"], e["cnt"] + 1)
        self._record(ev, R, W, seen=e["seen"])

    def dma(self, qn, ds, out, in_, R=(), W=(), **kw):
        e = self.E[qn]
        self._wait(e, self._deps(R, W), qn + "_q")
        e["eng"].dma_start(out=out, in_=in_, **kw).then_inc(ds["sem"], 16)
        ds["cnt"] += 16
        self._record((ds["id"], ds["sem"], ds["cnt"]), R, W, seen=e["seen"])

    def wait_all(self, qn, keys):
        e = self.E[qn]
        self._wait(e, self._deps(keys, ()), qn + "_q")


def build(depth=4, last_unscaled=True, dbg=None):
    L = depth
    ALPHA = ALPHA_FULL
    nc = bass.Bass("TRN2", target_bir_lowering=False)
    x_d = nc.dram_tensor("x", [S, D], F32, kind="ExternalInput").ap()
    w_in_d = nc.dram_tensor("w_in", [L, D, DIN], F32, kind="ExternalInput").ap()
    b_gate_d = nc.dram_tensor("b_gate", [L, 8], F32, kind="ExternalInput").ap()
    w_conv_d = nc.dram_tensor("w_conv", [L, 4, D], F32, kind="ExternalInput").ap()
    hn_g_d = nc.dram_tensor("hn_g", [L, 512], F32, kind="ExternalInput").ap()
    w_pool_d = nc.dram_tensor("w_pool", [L, 4, 128, 128], F32, kind="ExternalInput").ap()
    pool_scale_d = nc.dram_tensor("pool_scale", [L, 512], F32, kind="ExternalInput").ap()
    w_out_d = nc.dram_tensor("w_out", [L, D, D], F32, kind="ExternalInput").ap()
    ln_d = [nc.dram_tensor(n, [L, D], F32, kind="ExternalInput").ap() for n in ("ln1_g", "ln1_b", "ln2_g", "ln2_b")]
    w_ff1_d = nc.dram_tensor("w_ff1", [L, D, DFF], F32, kind="ExternalInput").ap()
    w_ff2_d = nc.dram_tensor("w_ff2", [L, DFF, D], F32, kind="ExternalInput").ap()
    y_d = nc.dram_tensor("y", [S, D], F32, kind="ExternalOutput").ap()
    gscr_d = nc.dram_tensor("gscr", [L, NSEG, 8, SEG], F32, kind="Internal").ap()

    es = ExitStack()
    with es:
        def sb(name, shape, dt):
            return es.enter_context(nc.sbuf_tensor(name, shape, dt))

        T = Trk(nc, es)
        T.add_eng("pe", nc.tensor)
        T.add_eng("act", nc.scalar)
        T.add_eng("dve", nc.vector)
        T.add_eng("pool", nc.gpsimd)
        T.add_eng("sp", nc.sync, own=False)

        XF = sb("XF", [128, 8, S], F32)
        XB = sb("XB", [128, 8, S], BF16)
        RG = [sb(f"RG{i}", [128, 4096], BF16) for i in range(RING)]
        ARENA = sb("ARENA", [128, 16384], BF16)
        PS = [es.enter_context(nc.psum_tensor(f"PS{i}", [128, 512], F32)) for i in range(8)]

        def av(c0, n, b):
            return ARENA[:, c0:c0 + n].rearrange("p (a b) -> p a b", b=b)
        QT = av(0, 2048, 512)
        KT = av(2048, 2048, 512)
        KTOK = ARENA[:, 4096:6144].rearrange("p (c h d) -> p c h d", h=4, d=128)
        VP = ARENA[:, 6144:8192].rearrange("p (c h d) -> p c h d", h=4, d=128)
        YM = av(8192, 2048, 512)
        YP = av(10240, 2048, 512)
        PB = av(12288, 2112, 528)
        UQ = [ARENA[:, 14400:14916], ARENA[:, 14916:15432]]
        H = ARENA[:, :].rearrange("p (j t) -> p j t", t=S)
        XS = [ARENA[:, i * 2048:(i + 1) * 2048].bitcast(F32) for i in range(4)]

        AB_NAMES = {"QT", "KT", "KTOK", "VP", "YM", "YP", "PB", "UQ"}
        def region_of(k):
            n = k[0]
            if n in AB_NAMES:
                return ("AR", "AB")
            if n == "H":
                return ("AR", "FFN")
            if n == "XS":
                return ("AR", "IO")
            return None
        T.region_of = region_of

        RBt = [sb(f"RB{i}", [128, 512], BF16)[:, :] for i in range(3)]
        RSQt = [sb(f"RSQ{i}", [128, 512], BF16)[:, :] for i in range(3)]
        MEANt = [sb(f"MEAN{i}", [128, 512], F32)[:, :] for i in range(2)]
        RSTDt = [sb(f"RSTD{i}", [128, 512], F32)[:, :] for i in range(2)]
        DG = [sb(f"DG{i}", [128, 4, 128], BF16) for i in range(2)]
        G8 = sb("G8", [8, SEG], F32)
        GI = sb("GI", [16, 128], F32)
        GF = sb("GF", [16, 128], F32)
        NA = sb("NA", [16, 128], F32)
        GM = sb("GM", [16, 4], F32)
        ROW = sb("ROW", [1, 32], F32)
        MFULL = sb("MFULL", [1, 4, 5], F32)
        MC = sb("MC", [1, 16], F32)
        TR = sb("TR", [1, 16], F32)
        SCR = sb("SCR", [1, 16], F32)
        NMC = sb("NMC", [16, 2], F32)
        SCB = sb("SCB", [128, 16], F32)
        ET = sb("ET", [128, 16], F32)
        ETB = sb("ETB", [128, 16], BF16)
        THRT = sb("THRT", [128, 16], F32)
        C = sb("C", [128, 4, 129], F32)
        CB = sb("CB", [128, 4, 129], BF16)
        ATs = [sb(f"AT{i}", [128, 4, 128], BF16) for i in range(2)]
        HN = sb("HN", [128, 4, 4, 128], BF16)
        SQs = [sb(f"SQ{i}", [128, 4, 128], F32) for i in range(2)]
        SM = sb("SM", [128, 12, 16], F32)
        HALO = sb("HALO", [128, 8, 3], BF16)
        PHALO = sb("PHALO", [128, 4, 16], BF16)
        CS = sb("CS", [128, 16], F32)
        DFB = sb("DFB", [128, 16], BF16)
        HR = [sb(f"HR{i}", [128, 512], BF16) for i in range(2)]
        IDENTB = sb("IDENTB", [128, 128], BF16)
        IDENTF = sb("IDENTF", [128, 128], F32)
        MASK = sb("MASK", [128, 128], BF16)
        ONESM = sb("ONESM", [128, 128], BF16)
        ONESF = sb("ONESF", [128, 128], F32)
        INVC = sb("INVC", [128, 16], F32)
        PRMS = sb("PRMS", [128, 128], F32)
        PRMT = sb("PRMT", [128, 128], F32)
        PRMA = sb("PRMA", [128, 128], F32)
        PRM2T = sb("PRM2T", [128, 32], F32)
        WCVT = sb("WCVT", [128, 128], F32)
        BG = sb("BG", [8, 4], F32)
        WPF = sb("WPF", [128, 4, 128], F32)
        WPA = sb("WPA", [128, 4, 128], BF16)
        WPBt = sb("WPBt", [128, 4, 128], BF16)
        WPC = sb("WPC", [128, 4, 128], BF16)
        GW = [sb(f"GW{i}", [128, 8, 8], BF16) for i in range(L)]

        d_x = [T.dsem(f"x{i}") for i in range(4)]
        d_y = [T.dsem(f"y{i}") for i in range(4)]
        d_prm = T.dsem("prm")
        d_bg = T.dsem("bg")
        d_g = [T.dsem("g0"), T.dsem("g1"), T.dsem("g2")]
        d_gw = T.dsem("gw")
        d_wp = T.dsem("wp")
        d_w = [T.dsem(f"w{i}") for i in range(RING)]

        bank_ctr = [0]
        reserved = set()

        def nb():
            while True:
                b = bank_ctr[0] % 8
                bank_ctr[0] += 1
                if b not in reserved:
                    return b

        def psk(b, c0=0, c1=512):
            return [("PS", b)]

        def psbf(b):
            return PS[b][:, 0:256].bitcast(BF16)

        def mm(out, lhsT, rhs, start, stop, R, W, inc=None):
            if inc is None:
                inc = stop
            T.op("pe", lambda e: e.matmul(out, lhsT=lhsT, rhs=rhs, start=start, stop=stop), R=R, W=W, inc=inc)

        def tp(out, in_, ident, R, W, inc=True):
            T.op("pe", lambda e: e.transpose(out=out, in_=in_, identity=ident), R=R, W=W, inc=inc)

        def act(out, in_, func, R, W, bias=None, scale=None):
            kw = {}
            if bias is not None:
                kw["bias"] = bias
            if scale is not None:
                kw["scale"] = scale
            T.op("act", lambda e: e.activation(out=out, in_=in_, func=func, **kw), R=R, W=W)

        def dve(fn, R, W):
            T.op("dve", fn, R=R, W=W)

        plan = []
        for l in range(L):
            for s in range(NSEG):
                for kind, c0 in (("q", 0), ("k", 512), ("o", 1544), ("v", 1024), ("p", 2056)):
                    plan.append((kind, w_in_d[l][:, c0:c0 + 512].rearrange("(k p) n -> p k n", p=128), "kn"))
                for i in range(2):
                    plan.append((f"wo{i}", w_out_d[l][:, i * 512:(i + 1) * 512].rearrange("(k p) n -> p k n", p=128), "kn"))
            for q in range(4):
                for i in range(2):
                    c0 = q * 1024 + i * 512
                    plan.append((f"w1{i}", w_ff1_d[l][:, c0:c0 + 512].rearrange("(k p) n -> p k n", p=128), "kn"))
                for i in range(2):
                    r0 = q * 1024 + i * 512
                    plan.append((f"w2{i}", w_ff2_d[l][r0:r0 + 512, :].rearrange("(j p) n -> p j n", p=128), "jn"))
        ring = dict(acq=0, rel=0)

        def slot_view(slot, lay):
            if lay == "kn":
                return RG[slot][:, :].rearrange("p (k n) -> p k n", n=512)
            return RG[slot][:, :].rearrange("p (j n) -> p j n", n=1024)

        def issue_fill(i):
            kind, src, lay = plan[i]
            slot = i % RING
            T.dma("pool", d_w[slot], slot_view(slot, lay), src, R=(), W=[("W", slot)])

        def acquire(kind):
            i = ring["acq"]
            assert plan[i][0] == kind, (plan[i][0], kind)
            ring["acq"] += 1
            slot = i % RING
            return slot_view(slot, plan[i][2]), ("W", slot)

        def release():
            i = ring["rel"]
            ring["rel"] += 1
            if i + RING < len(plan):
                issue_fill(i + RING)

        T.op("pool", lambda e: e.memset(ONESF[:], 1.0), W=[("ONESF",)])
        T.op("pool", lambda e: e.memset(ONESM[:], 1.0 / 1024.0), W=[("ONESM",)])
        T.op("pool", lambda e: e.affine_select(out=IDENTF[:], in_=ONESF[:], pattern=[[1, 128]], compare_op=ALU.is_equal,
                                               fill=0.0, base=0, channel_multiplier=-1), R=[("ONESF",)], W=[("IDENTF",)])
        T.op("pool", lambda e: e.affine_select(out=IDENTB[:], in_=ONESF[:], pattern=[[1, 128]], compare_op=ALU.is_equal,
                                               fill=0.0, base=0, channel_multiplier=-1), R=[("ONESF",)], W=[("IDENTB",)])
        T.op("pool", lambda e: e.affine_select(out=MASK[:], in_=ONESF[:], pattern=[[1, 128]], compare_op=ALU.is_ge,
                                               fill=0.0, base=0, channel_multiplier=-1), R=[("ONESF",)], W=[("MASK",)])
        for t in range(16):
            T.op("pool", lambda e, t=t: e.memset(INVC[:, t:t + 1], 1.0 / (t + 1)), W=[("INVC",)])

        d_gws = [T.dsem(f"gw{i}") for i in range(L)]
        def _gw(l_):
            with nc.allow_non_contiguous_dma(reason="gate weights, 32B rows"):
                T.dma("pool", d_gws[l_], GW[l_][:, :, :], w_in_d[l_][:, 1536:1544].rearrange("(k p) n -> p k n", p=128),
                      R=(), W=[("GW", l_)])
        _gw(0)
        for i in range(min(RING, len(plan))):
            if not os.environ.get("SKIP_PREFETCH"):
                issue_fill(i)
        for l_ in range(1, L):
            _gw(l_)

        def load_T(rows_list, dst, ncols, key):
            r = 0
            for src in rows_list:
                n = src.shape[0]
                T.dma("sp", d_prm, PRMS[r:r + n, :], src, R=(), W=[("PRMS",)])
                r += n
            b = nb()
            tp(PS[b][:, 0:r], PRMS[0:r, :], IDENTF[0:r, 0:r], R=[("PRMS",), ("IDENTF",)], W=psk(b, 0, r))
            dve(lambda e: e.tensor_copy(out=dst[:, 0:r], in_=PS[b][:, 0:r]), R=psk(b, 0, r), W=[key])

        if os.environ.get("SKIP_PARAMS"):
            load_T = lambda *a, **k: None
        load_T([a.rearrange("l (k c) -> (l k) c", c=128) for a in ln_d], PRMT, 128, ("PRMT",))
        dve(lambda e: e.tensor_scalar(out=PRMA[:, 0:32 * L], in0=PRMT[:, 0:32 * L], scalar1=ALPHA, scalar2=None, op0=ALU.mult),
            R=[("PRMT",)], W=[("PRMA",)])
        load_T([hn_g_d.rearrange("l (h c) -> (l h) c", c=128), pool_scale_d.rearrange("l (h c) -> (l h) c", c=128)],
               PRM2T, 32, ("PRM2T",))
        load_T([w_conv_d.rearrange("l j (k c) -> (l j k) c", c=128)], WCVT, 128, ("WCVT",))
        with nc.allow_non_contiguous_dma(reason="tiny bias"):
          if not os.environ.get("SKIP_BG"):
            T.dma("sp", d_bg, BG[0:8, 0:L], b_gate_d.rearrange("l g -> g l"), R=(), W=[("BG",)])

        def lncol(arr, l, k):
            return arr * 8 * L + l * 8 + k

        XM = int(os.environ.get('XSMOD', 4))
        for tt in range(int(os.environ.get('NX', 16))):
            xs = XS[tt % XM]
            T.dma("sp", d_x[tt % XM], xs, x_d[tt * 128:(tt + 1) * 128, :], R=(), W=[("XS", tt % XM)])
            for dg in range(2):
                b = nb()
                for di in range(4):
                    d = dg * 4 + di
                    tp(PS[b][:, di * 128:(di + 1) * 128], xs[:, d * 128:(d + 1) * 128], IDENTF[:],
                       R=[("XS", tt % XM), ("IDENTF",)], W=psk(b, di * 128, di * 128 + 128), inc=(di == 3))
                pv = PS[b][:, :].rearrange("p (a b) -> p a b", b=128)
                tb = tt // 4
                T.op("act", lambda e, pv=pv, dg=dg, tt=tt: e.mul(out=XF[:, dg * 4:dg * 4 + 4, tt * 128:(tt + 1) * 128], in_=pv, mul=ALPHA),
                     R=psk(b), W=[("XF", d, tb) for d in range(dg * 4, dg * 4 + 4)])
                dve(lambda e, pv=pv, dg=dg, tt=tt: e.tensor_copy(out=XB[:, dg * 4:dg * 4 + 4, tt * 128:(tt + 1) * 128], in_=pv),
                    R=psk(b), W=[("XB", d, tb) for d in range(dg * 4, dg * 4 + 4)])

        gate_tails = {}

        def phase_gates(l, s):
            cols = slice(s * SEG, (s + 1) * SEG)
            gw = GW[l]
            if s == 0:
                dve(lambda e: e.memset(MFULL[0:1, :, :], 0.0), R=(), W=[("MFULL",)])
            b = nb()
            for k in range(8):
                mm(PS[b][0:8, 0:512], gw[:, k, :], XB[:, k, cols], k == 0, k == 7,
                   R=[("GW", l), ("XB", k, s)], W=psk(b))
            act(G8[0:8, :], PS[b][0:8, 0:512], AF.Identity, R=psk(b) + [("BG",)], W=[("G8",)], bias=BG[0:8, l:l + 1])
            T.dma("sp", d_g[0], gscr_d[l, s], G8[0:8, :], R=[("G8",)], W=[("gscr",)])
            T.dma("sp", d_g[1], GI[0:16, :], gscr_d[l, s, 0:4, :].rearrange("g (c t) -> (g c) t", t=128), R=[("gscr",)], W=[("GI",)])
            T.dma("sp", d_g[2], GF[0:16, :], gscr_d[l, s, 4:8, :].rearrange("g (c t) -> (g c) t", t=128), R=[("gscr",)], W=[("GF",)])
            def t0():
                act(GF[:, :], GF[:, :], AF.Exp, R=[("GF",)], W=[("GF",)], scale=-1.0)
                act(GF[:, :], GF[:, :], AF.Ln, R=[("GF",)], W=[("GF",)], bias=1.0)
                dve(lambda e: e.tensor_tensor_scan(out=NA[:, :], data0=ONESF[0:16, :], data1=GF[:, :], initial=0.0,
                                                   op0=ALU.mult, op1=ALU.add), R=[("GF",), ("ONESF",)], W=[("NA",)])
                dve(lambda e: e.tensor_tensor(out=GI[:, :], in0=GI[:, :], in1=NA[:, :], op=ALU.add), R=[("GI",), ("NA",)], W=[("GI",)])
                dve(lambda e: e.tensor_reduce(out=GM[:, 2:3], in_=GI[:, :], axis=AX.X, op=ALU.max), R=[("GI",)], W=[("GM", 2)])
                dve(lambda e: e.tensor_scalar(out=GM[:, 0:1], in0=NA[:, 127:128], scalar1=-1.0, scalar2=None, op0=ALU.mult),
                    R=[("NA",)], W=[("GM", 0)])
                dve(lambda e: e.tensor_tensor(out=GM[:, 1:2], in0=GM[:, 2:3], in1=GM[:, 0:1], op=ALU.add),
                    R=[("GM", 2), ("GM", 0)], W=[("GM", 1)])

            def t1():
                b2 = nb()
                tp(PS[b2][0:1, 0:16], GM[0:16, 0:1], IDENTF[0:16, 0:16], R=[("GM", 0), ("IDENTF",)], W=psk(b2, 0, 16), inc=False)
                tp(PS[b2][0:1, 16:32], GM[0:16, 1:2], IDENTF[0:16, 0:16], R=[("GM", 1), ("IDENTF",)], W=psk(b2, 16, 32))
                dve(lambda e: e.tensor_copy(out=ROW[0:1, 0:32], in_=PS[b2][0:1, 0:32]), R=psk(b2, 0, 32), W=[("ROW",)])
                for h in range(4):
                    dve(lambda e, h=h: e.tensor_tensor_scan(out=MFULL[0:1, h, 1:5], data0=ROW[0:1, h * 4:(h + 1) * 4],
                                                            data1=ROW[0:1, 16 + h * 4:16 + (h + 1) * 4], initial=MFULL[0:1, h, 0:1],
                                                            op0=ALU.add, op1=ALU.max), R=[("ROW",), ("MFULL",)], W=[("MFULL",)])
                mc3 = MC[0:1, :].rearrange("p (h c) -> p h c", c=4)
                tr3 = TR[0:1, :].rearrange("p (h c) -> p h c", c=4)
                row3 = ROW[0:1, 0:16].rearrange("p (h c) -> p h c", c=4)
                dve(lambda e: e.tensor_tensor(out=mc3, in0=MFULL[0:1, :, 1:5], in1=row3, op=ALU.subtract), R=[("MFULL",), ("ROW",)], W=[("MC",)])
                dve(lambda e: e.tensor_tensor(out=tr3, in0=MFULL[0:1, :, 0:4], in1=mc3, op=ALU.subtract), R=[("MFULL",), ("MC",)], W=[("TR",)])
                act(SCR[0:1, :], TR[0:1, :], AF.Exp, R=[("TR",)], W=[("SCR",)])
                dve(lambda e: e.tensor_copy(out=MFULL[0:1, :, 0:1], in_=MFULL[0:1, :, 4:5]), R=[("MFULL",)], W=[("MFULL",)])

            def t2():
                b3 = nb()
                mm(PS[b3][:, 0:16], ONESF[0:1, 0:128], SCR[0:1, 0:16], True, True, R=[("ONESF",), ("SCR",)], W=psk(b3, 0, 16))
                dve(lambda e: e.tensor_copy(out=SCB[:, :], in_=PS[b3][:, 0:16]), R=psk(b3, 0, 16), W=[("SCB",)])
                b4 = nb()
                tp(PS[b4][0:16, 0:1], MC[0:1, 0:16], IDENTF[0:1, 0:1], R=[("MC",), ("IDENTF",)], W=psk(b4, 0, 1))
                dve(lambda e: e.tensor_scalar(out=NMC[:, 0:1], in0=PS[b4][0:16, 0:1], scalar1=-1.0, scalar2=None, op0=ALU.mult),
                    R=psk(b4, 0, 1), W=[("NMC", 0)])
                dve(lambda e: e.tensor_scalar(out=NMC[:, 1:2], in0=PS[b4][0:16, 0:1], scalar1=-1.0, scalar2=LNK, op0=ALU.mult, op1=ALU.add),
                    R=psk(b4, 0, 1), W=[("NMC", 1)])
                act(GI[:, :], GI[:, :], AF.Exp, R=[("GI",), ("NMC", 1)], W=[("GI",)], bias=NMC[:, 1:2])
                act(NA[:, :], NA[:, :], AF.Exp, R=[("NA",), ("NMC", 0)], W=[("NA",)], bias=NMC[:, 0:1])

            def t3():
                b5 = nb()
                tp(PS[b5][:, 0:16], GI[0:16, :], IDENTF[0:16, 0:16], R=[("GI",), ("IDENTF",)], W=psk(b5, 0, 16), inc=False)
                tp(PS[b5][:, 16:32], NA[0:16, :], IDENTF[0:16, 0:16], R=[("NA",), ("IDENTF",)], W=psk(b5, 16, 32))
                dve(lambda e: e.tensor_copy(out=ET[:, :], in_=PS[b5][:, 0:16]), R=psk(b5, 0, 16), W=[("ET",)])
                dve(lambda e: e.tensor_copy(out=ETB[:, :], in_=PS[b5][:, 0:16]), R=psk(b5, 0, 16), W=[("ETB",)])
                dve(lambda e: e.tensor_copy(out=THRT[:, :], in_=PS[b5][:, 16:32]), R=psk(b5, 16, 32), W=[("THRT",)])


            gate_tails[(l, s)] = [t0, t1, t2, t3]

        def phase_qk(l, s, which):
            cols = slice(s * SEG, (s + 1) * SEG)
            W, wk = acquire(which)
            DST, dname = (QT, "QT") if which == "q" else (KT, "KT")
            inject = gate_tails.get((l, s), [])
            sched = {}

            def inj(hm):
                for _ in range(sched.get(hm, 0)):
                    if inject:
                        inject.pop(0)()

            def main(h):
                tile = (0 if which == "q" else 4) + h
                uq = UQ[tile % 2]
                uk = ("UQ", tile % 2)
                b = nb()
                for k in range(8):
                    mm(PS[b][:, :], W[:, k, h * 128:(h + 1) * 128], XB[:, k, cols], k == 0, k == 7, R=[wk, ("XB", k, s)], W=psk(b))
                if s == 0:
                    dve(lambda e, uq=uq: e.memset(uq[:, 0:3], 0.0), R=(), W=[uk])
                else:
                    dve(lambda e, uq=uq, tile=tile: e.tensor_copy(out=uq[:, 0:3], in_=HALO[:, tile, :]), R=[("HALO", tile)], W=[uk])
                act(uq[:, 3:515], PS[b][:, :], AF.Identity, R=psk(b), W=[uk])
                if s < NSEG - 1:
                    dve(lambda e, uq=uq, tile=tile: e.tensor_copy(out=HALO[:, tile, :], in_=uq[:, 512:515]), R=[uk], W=[("HALO", tile)])
                dg = DG[tile % 2]
                for j in range(4):
                    c = l * 32 + j * 8 + tile
                    dve(lambda e, dg=dg, j=j, c=c: e.tensor_scalar(out=dg[:, j, :], in0=IDENTB[:, :], scalar1=WCVT[:, c:c + 1],
                                                                   scalar2=None, op0=ALU.mult),
                        R=[("IDENTB",), ("WCVT",)], W=[("DG", tile % 2, j)])

            def conv(h):
                tile = (0 if which == "q" else 4) + h
                uq = UQ[tile % 2]
                uk = ("UQ", tile % 2)
                dg = DG[tile % 2]
                b2 = nb()
                for j in range(4):
                    mm(PS[b2][:, :], dg[:, j, :], uq[:, j:j + 512], j == 0, j == 3, R=[("DG", tile % 2, j), uk], W=psk(b2))
                act(DST[:, h, :], PS[b2][:, :], AF.Silu, R=psk(b2), W=[(dname, h)])

            main(0)
            inj(0)
            for h in range(4):
                if h + 1 < 4:
                    main(h + 1)
                    inj(h + 1)
                conv(h)
            release()

        def phase_ktok(l, s):
            for h in range(4):
                b3 = nb()
                pb = psbf(b3)
                for c in range(4):
                    tp(pb[:, c * 128:(c + 1) * 128], KT[:, h, c * 128:(c + 1) * 128], IDENTB[:, :],
                       R=[("KT", h), ("IDENTB",)], W=psk(b3, 0, 256), inc=(c == 3))
                dve(lambda e, h=h, pb=pb: e.tensor_copy(out=KTOK[:, :, h, :], in_=pb[:, :].rearrange("p (c d) -> p c d", d=128)),
                    R=psk(b3, 0, 256), W=[("KTOK", h)])

        def phase_o(l, s):
            cols = slice(s * SEG, (s + 1) * SEG)
            W, wk = acquire("o")
            for h in range(4):
                b = nb()
                for k in range(8):
                    mm(PS[b][:, :], W[:, k, h * 128:(h + 1) * 128], XB[:, k, cols], k == 0, k == 7, R=[wk, ("XB", k, s)], W=psk(b))
                act(YM[:, h, :], PS[b][:, :], AF.Sigmoid, R=psk(b), W=[("YM", h)])
                if h == 3:
                    tl = gate_tails.get((l, s), [])
                    if tl:
                        tl.pop(0)()

            release()
            phase_ktok(l, s)

        pool_w = {}

        def pool_prep(l, s):
            if s == 0:
                T.dma("sp", d_wp, WPF[:, :, :], w_pool_d[l].rearrange("g c d -> c g d"), R=(), W=[("WPF",)])
                for g in range(4):
                    win = 2 ** (g + 1)
                    dve(lambda e, g=g, win=win: e.tensor_scalar(out=WPA[:, g, :], in0=WPF[:, g, :], scalar1=(1.0 / win - 1.0),
                                                                 scalar2=None, op0=ALU.mult), R=[("WPF",)], W=[("WPA", g)])
                    dve(lambda e, g=g, win=win: e.tensor_scalar(out=WPBt[:, g, :], in0=WPF[:, g, :], scalar1=1.0 / win,
                                                                 scalar2=None, op0=ALU.mult), R=[("WPF",)], W=[("WPB", g)])
                dve(lambda e: e.tensor_copy(out=WPC[:, :, :], in_=WPF[:, :, :]), R=[("WPF",)], W=[("WPC",)])

        def pool_inproj(l, s, g, bank=None):
            cols = slice(s * SEG, (s + 1) * SEG)
            if g == 0:
                pool_w["w"] = acquire("p")
            W, wk = pool_w["w"]
            b = nb() if bank is None else bank
            for k in range(8):
                mm(PS[b][:, :], W[:, k, g * 128:(g + 1) * 128], XB[:, k, cols], k == 0, k == 7, R=[wk, ("XB", k, s)], W=psk(b))
            if s > 0:
                act(PB[:, g, 0:16], PHALO[:, g, :], AF.Identity, R=[("PHALO", g)], W=[("PB", g)])
            act(PB[:, g, 16:528], PS[b][:, :], AF.Identity, R=psk(b), W=[("PB", g)])
            if s < NSEG - 1:
                act(PHALO[:, g, :], PB[:, g, 512:528], AF.Identity, R=[("PB", g)], W=[("PHALO", g)])
            if g == 3:
                release()

        def pool_group(l, s, g):
            win = 2 ** (g + 1)
            c0 = win - 1 if s == 0 else 0
            b = nb()
            mm(PS[b][:, c0:512], WPA[:, g, :], PB[:, g, 16 + c0:528], True, False, R=[("WPA", g), ("PB", g)], W=psk(b), inc=False)
            for j in range(1, win):
                mm(PS[b][:, c0:512], WPBt[:, g, :], PB[:, g, 16 + c0 - j:528 - j], False, j == win - 1,
                   R=[("WPB", g), ("PB", g)], W=psk(b))
            if s == 0:
                dve(lambda e, g=g, c0=c0: e.tensor_tensor_scan(out=CS[:, 0:c0], data0=ONESF[:, 0:c0], data1=PB[:, g, 16:16 + c0],
                                                               initial=0.0, op0=ALU.mult, op1=ALU.add),
                    R=[("PB", g), ("ONESF",)], W=[("CS",)])
                dve(lambda e, c0=c0: e.tensor_tensor(out=CS[:, 0:c0], in0=CS[:, 0:c0], in1=INVC[:, 0:c0], op=ALU.mult),
                    R=[("CS",), ("INVC",)], W=[("CS",)])
                dve(lambda e, g=g, c0=c0: e.tensor_tensor(out=DFB[:, 0:c0], in0=CS[:, 0:c0], in1=PB[:, g, 16:16 + c0], op=ALU.subtract),
                    R=[("CS",), ("PB", g)], W=[("DFB",)])
                mm(PS[b][:, 0:c0], WPC[:, g, :], DFB[:, 0:c0], True, True, R=[("WPC",), ("DFB",)], W=psk(b))
            c = 4 * L + l * 4 + g
            act(YP[:, g, :], PS[b][:, :], AF.Identity, R=psk(b) + [("PRM2T",)], W=[("YP", g)], scale=PRM2T[:, c:c + 1])
            pop_side(1)

        def phase_v(l, s):
            while reserved:
                pop_side(1)
            W, wk = acquire("v")
            et3 = ET[:, :].rearrange("p (h c) -> p h c", c=4)
            tl = gate_tails.get((l, s), [])
            if len(tl) == 4:
                tl.pop(0)()
            banks = []
            for c in range(4):
                tt = s * 4 + c
                b = nb()
                banks.append(b)
                for k in range(8):
                    mm(PS[b][:, :], XB[:, k, tt * 128:(tt + 1) * 128], W[:, k, :], k == 0, k == 7, R=[wk, ("XB", k, s)], W=psk(b))
                if c >= 1 and tl:
                    tl.pop(0)()
            while tl:
                tl.pop(0)()
            for c in range(4):
                b = banks[c]
                pv = PS[b][:, :].rearrange("p (h d) -> p h d", d=128)
                dve(lambda e, c=c, pv=pv: e.tensor_tensor(out=VP[:, c, :, :], in0=pv,
                                                          in1=et3[:, :, c].unsqueeze(2).to_broadcast([128, 4, 128]), op=ALU.mult),
                    R=psk(b) + [("ET",)], W=[("VP", c)])
            release()

        def phase_mlstm(l, s):
            while reserved:
                pop_side(1)
            pool_prep(l, s)
            scb3 = SCB[:, :].rearrange("p (h c) -> p h c", c=4)
            bO = [0, 1, 2, 3]
            bX = 4
            rot = [5, 6, 7]
            pX = PS[bX]
            kX = [("PS", bX)]
            def smv(i):
                return SM[:, i, :]

            def smc(i, c):
                return SM[:, i, :].rearrange("p (h c) -> p h c", c=4)[:, :, c]
            DNA, DN, RDN, SUM_, SSQ, MEAN_, EX2, VAR, R2, TT, SS, NBB = range(12)

            def stats(c):
                dve(lambda e: e.tensor_reduce(out=smc(SUM_, c), in_=pOs[c], axis=AX.X, op=ALU.add), R=psk(bO[c]), W=[("SM", SUM_)])
                act(SQs[c % 2][:, :, :], pOs[c], AF.Square, R=psk(bO[c]), W=[("SQ", c % 2)])
                dve(lambda e: e.tensor_reduce(out=smc(SSQ, c), in_=SQs[c % 2][:, :, :], axis=AX.X, op=ALU.add), R=[("SQ", c % 2)], W=[("SM", SSQ)])

            ri = 0
            pOs = []
            for c in range(4):
                first = (s == 0 and c == 0)
                tc_ = slice(c * 128, (c + 1) * 128)
                bS = rot[ri % 3]
                bDC = rot[(ri + 1) % 3]
                ri += 2
                pS = PS[bS][:, :].rearrange("p (h d) -> p h d", d=128)
                pDC = PS[bDC][:, :].rearrange("p (h d) -> p h d", d=128)
                pO = PS[bO[c]][:, :].rearrange("p (h d) -> p h d", d=128)
                pOs.append(pO)
                at = ATs[c % 2]
                ak = ("AT", c % 2)
                for h in range(4):
                    mm(pS[:, h, :], KT[:, h, tc_], QT[:, h, tc_], True, True, R=[("KT", h), ("QT", h)], W=psk(bS), inc=(h == 3))
                dve(lambda e, pS=pS, at=at: e.tensor_tensor(out=at[:, :, :], in0=pS, in1=MASK[:, :].unsqueeze(1).to_broadcast([128, 4, 128]),
                                                            op=ALU.mult), R=psk(bS) + [("MASK",)], W=[ak])
                for h in range(4):
                    col = h * 4 + c
                    mm(pDC[:, h, :], KTOK[:, c, h, :], VP[:, c, h, :], True, True, R=[("KTOK", h), ("VP", c)], W=psk(bDC), inc=False)
                    mm(pX[:, 16 + h:17 + h], KTOK[:, c, h, :], ETB[:, col:col + 1], True, True, R=[("KTOK", h), ("ETB",)], W=kX, inc=(h == 3))
                if not first:
                    dve(lambda e, c=c: e.tensor_tensor(out=C[:, :, :], in0=C[:, :, :],
                                                       in1=scb3[:, :, c].unsqueeze(2).to_broadcast([128, 4, 129]), op=ALU.mult),
                        R=[("C",), ("SCB",)], W=[("C",)])
                    act(CB[:, :, :], C[:, :, :], AF.Identity, R=[("C",)], W=[("CB",)])
                if c >= 1:
                    stats(c - 1)
                for h in range(4):
                    col = h * 4 + c
                    mm(pO[:, h, :], at[:, h, :], VP[:, c, h, :], True, first, R=[ak, ("VP", c)], W=psk(bO[c]), inc=False)
                    if not first:
                        mm(pO[:, h, :], QT[:, h, tc_], CB[:, h, 0:128], False, True, R=[("QT", h), ("CB",)], W=psk(bO[c]), inc=False)
                    mm(pX[:, col:col + 1], at[:, h, :], ETB[:, col:col + 1], True, first, R=[ak, ("ETB",)], W=kX, inc=(first and h == 3))
                    if not first:
                        mm(pX[:, col:col + 1], QT[:, h, tc_], CB[:, h, 128:129], False, True, R=[("QT", h), ("CB",)], W=kX, inc=(h == 3))
                if first:
                    dve(lambda e, pDC=pDC: e.tensor_copy(out=C[:, :, 0:128], in_=pDC), R=psk(bDC), W=[("C",)])
                    dve(lambda e: e.tensor_copy(out=C[:, :, 128:129], in_=pX[:, 16:20].unsqueeze(2)), R=kX, W=[("C",)])
                else:
                    dve(lambda e, pDC=pDC: e.tensor_tensor(out=C[:, :, 0:128], in0=C[:, :, 0:128], in1=pDC, op=ALU.add),
                        R=psk(bDC) + [("C",)], W=[("C",)])
                    dve(lambda e: e.tensor_tensor(out=C[:, :, 128:129], in0=C[:, :, 128:129], in1=pX[:, 16:20].unsqueeze(2), op=ALU.add),
                        R=kX + [("C",)], W=[("C",)])
            reserved.update((0, 1, 2, 3, 4))
            dve(lambda e: e.tensor_tensor(out=smv(DNA), in0=pX[:, 0:16], in1=THRT[:, 0:16], op=ALU.max), R=kX + [("THRT",)], W=[("SM", DNA)])
            dve(lambda e: e.scalar_tensor_tensor(out=smv(DN), in0=pX[:, 0:16], scalar=-1.0, in1=smv(DNA), op0=ALU.mult, op1=ALU.max),
                R=kX + [("SM", DNA)], W=[("SM", DN)])
            dve(lambda e: e.reciprocal(out=smv(RDN), in_=smv(DN)), R=[("SM", DN)], W=[("SM", RDN)])
            stats(3)
            pool_inproj(l, s, 0, bank=5)
            pool_inproj(l, s, 1, bank=7)
            dve(lambda e: e.tensor_scalar(out=smv(MEAN_), in0=smv(SUM_), scalar1=1.0 / 128, scalar2=None, op0=ALU.mult),
                R=[("SM", SUM_)], W=[("SM", MEAN_)])
            dve(lambda e: e.tensor_tensor(out=smv(EX2), in0=smv(MEAN_), in1=smv(MEAN_), op=ALU.mult), R=[("SM", MEAN_)], W=[("SM", EX2)])
            dve(lambda e: e.scalar_tensor_tensor(out=smv(VAR), in0=smv(SSQ), scalar=1.0 / 128, in1=smv(EX2), op0=ALU.mult, op1=ALU.subtract),
                R=[("SM", SSQ), ("SM", EX2)], W=[("SM", VAR)])
            dve(lambda e: e.tensor_tensor(out=smv(R2), in0=smv(RDN), in1=smv(RDN), op=ALU.mult), R=[("SM", RDN)], W=[("SM", R2)])
            dve(lambda e: e.tensor_tensor(out=smv(TT), in0=smv(R2), in1=smv(VAR), op=ALU.mult), R=[("SM", R2), ("SM", VAR)], W=[("SM", TT)])
            act(smv(TT), smv(TT), AF.Ln, R=[("SM", TT)], W=[("SM", TT)], bias=LN_EPS)
            act(smv(TT), smv(TT), AF.Exp, R=[("SM", TT)], W=[("SM", TT)], scale=-0.5)
            dve(lambda e: e.tensor_tensor(out=smv(SS), in0=smv(TT), in1=smv(RDN), op=ALU.mult), R=[("SM", TT), ("SM", RDN)], W=[("SM", SS)])
            dve(lambda e: e.scalar_tensor_tensor(out=smv(NBB), in0=smv(MEAN_), scalar=-1.0, in1=smv(SS), op0=ALU.mult, op1=ALU.mult),
                R=[("SM", MEAN_), ("SM", SS)], W=[("SM", NBB)])
            for c in (0, 1, 2, 3):
                for h in range(4):
                    col = h * 4 + c
                    if c < 1:
                        act(HN[:, c, h, :], pOs[c][:, h, :], AF.Identity, R=psk(bO[c]) + [("SM", SS), ("SM", NBB)], W=[("HN", c)],
                            scale=SM[:, SS, col:col + 1], bias=SM[:, NBB, col:col + 1])
                    else:
                        dve(lambda e, c=c, h=h, col=col: e.tensor_scalar(out=HN[:, c, h, :], in0=pOs[c][:, h, :], scalar1=SM[:, SS, col:col + 1],
                                                                         scalar2=SM[:, NBB, col:col + 1], op0=ALU.mult, op1=ALU.add),
                            R=psk(bO[c]) + [("SM", SS), ("SM", NBB)], W=[("HN", c)])
            pool_inproj(l, s, 2, bank=6)
            pool_inproj(l, s, 3, bank=5)
            for bb in (0, 1, 2, 3, 4):
                reserved.discard(bb)
            pool_group(l, s, 0)
            pool_group(l, s, 1)
            for pr in range(2):
                bT = nb()
                pHT = PS[bT][:, :].bitcast(BF16).rearrange("p (c h d) -> p c h d", h=4, d=128)
                for cc in range(2):
                    for h in range(4):
                        tp(pHT[:, cc, h, :], HN[:, 2 * pr + cc, h, :], IDENTB[:, :], R=[("HN", 2 * pr + cc), ("IDENTB",)], W=psk(bT),
                           inc=(cc == 1 and h == 3))
                for h in range(4):
                    cc_ = l * 4 + h
                    ymv = YM[:, h, pr * 256:(pr + 1) * 256].rearrange("p (c d) -> p c d", d=128)
                    dve(lambda e, h=h, cc_=cc_, ymv=ymv, pHT=pHT: e.scalar_tensor_tensor(out=ymv, in0=pHT[:, :, h, :], scalar=PRM2T[:, cc_:cc_ + 1],
                                                                                     in1=ymv, op0=ALU.mult, op1=ALU.mult),
                        R=psk(bT) + [("YM", h), ("PRM2T",)], W=[("YM", h)])

            pool_group(l, s, 2)
            pool_group(l, s, 3)

        def phase_outproj(l, s):
            cols = slice(s * SEG, (s + 1) * SEG)
            Ws = [acquire("wo0"), acquire("wo1")]
            for m in range(8):
                W, wk = Ws[m // 4]
                b = nb()
                for k in range(8):
                    rhs = YM[:, k, :] if k < 4 else YP[:, k - 4, :]
                    rk = ("YM", k) if k < 4 else ("YP", k - 4)
                    mm(PS[b][:, :], W[:, k, (m % 4) * 128:(m % 4 + 1) * 128], rhs, k == 0, k == 7, R=[wk, rk], W=psk(b))
                dve(lambda e, m=m, b=b: e.tensor_tensor(out=XF[:, m, cols], in0=XF[:, m, cols], in1=PS[b][:, :], op=ALU.add),
                    R=psk(b) + [("XF", m, s)], W=[("XF", m, s)])
                if m % 2 == 1:
                    pop_side(1)
                if m == 3:
                    release()
            release()

        ln_ctr = [0]

        def ln_a(l, which, tb, st, part):
            cols = slice(tb * SEG, (tb + 1) * SEG)
            if part == 0:
                st["j"] = ln_ctr[0] % 2
                ln_ctr[0] += 1
                st["b1"], st["b2"] = nb(), nb()
                reserved.update((st["b1"], st["b2"]))
            j, b1, b2 = st["j"], st["b1"], st["b2"]
            if part < 2:
                for d in range(part * 4, part * 4 + 4):
                    i = d % 3
                    act(RBt[i], XF[:, d, cols], AF.Identity, R=[("XF", d, tb)], W=[("RB", i)])
                    act(RSQt[i], XF[:, d, cols], AF.Square, R=[("XF", d, tb)], W=[("RSQ", i)])
                    mm(PS[b1][:, :], ONESM[:, :], RBt[i], d == 0, d == 7, R=[("ONESM",), ("RB", i)], W=psk(b1), inc=True)
                    mm(PS[b2][:, :], ONESM[:, :], RSQt[i], d == 0, d == 7, R=[("ONESM",), ("RSQ", i)], W=psk(b2), inc=True)
                return
            mean, rstd = MEANt[j], RSTDt[j]
            dve(lambda e: e.tensor_copy(out=mean, in_=PS[b1][:, :]), R=psk(b1), W=[("MEAN", j)])
            dve(lambda e: e.tensor_tensor(out=rstd, in0=mean, in1=mean, op=ALU.mult), R=[("MEAN", j)], W=[("RSTD", j)])
            dve(lambda e: e.tensor_tensor(out=rstd, in0=PS[b2][:, :], in1=rstd, op=ALU.subtract), R=psk(b2) + [("RSTD", j)], W=[("RSTD", j)])
            act(rstd, rstd, AF.Ln, R=[("RSTD", j)], W=[("RSTD", j)], bias=LN_EPS)
            act(rstd, rstd, AF.Exp, R=[("RSTD", j)], W=[("RSTD", j)], scale=-0.5)
            reserved.discard(b1)
            reserved.discard(b2)

        def ln_b(l, which, tb, j, d0, d1, scaled):
            ga, ba = (0, 1) if which == 1 else (2, 3)
            PA = PRMA if scaled else PRMT
            cols = slice(tb * SEG, (tb + 1) * SEG)
            mean, rstd = MEANt[j], RSTDt[j]
            for d in range(d0, d1):
                xf = XF[:, d, cols]
                dve(lambda e, xf=xf: e.tensor_tensor(out=xf, in0=xf, in1=mean, op=ALU.subtract), R=[("XF", d, tb), ("MEAN", j)], W=[("XF", d, tb)])
                dve(lambda e, xf=xf: e.tensor_tensor(out=xf, in0=xf, in1=rstd, op=ALU.mult), R=[("XF", d, tb), ("RSTD", j)], W=[("XF", d, tb)])
                cg, cb = lncol(ga, l, d), lncol(ba, l, d)
                act(XB[:, d, cols], xf, AF.Identity, R=[("XF", d, tb), ("PRMT",)], W=[("XB", d, tb)],
                    scale=PRMT[:, cg:cg + 1], bias=PRMT[:, cb:cb + 1])
                act(xf, xf, AF.Identity, R=[("XF", d, tb), ("PRMA",), ("PRMT",)], W=[("XF", d, tb)],
                    scale=PA[:, cg:cg + 1], bias=PA[:, cb:cb + 1])

        ffn_w = {}

        def ffn1(l, q, tb):
            if tb == 0:
                ffn_w["w1"] = [acquire("w10"), acquire("w11")]
            W1 = ffn_w["w1"]
            cols = slice(tb * SEG, (tb + 1) * SEG)
            for j in range(8):
                W, wk = W1[j // 4]
                b = nb()
                for k in range(8):
                    mm(PS[b][:, :], W[:, k, (j % 4) * 128:(j % 4 + 1) * 128], XB[:, k, cols], k == 0, k == 7,
                       R=[wk, ("XB", k, tb)], W=psk(b))
                hr = HR[j % 2]
                act(hr[:, :], PS[b][:, :], AF.Relu, R=psk(b), W=[("HR", j % 2)])
                dve(lambda e, hr=hr, j=j: e.tensor_tensor(out=H[:, j, cols], in0=hr[:, :], in1=hr[:, :], op=ALU.mult),
                    R=[("HR", j % 2)], W=[("H", j, tb)])
                if j % 2 == 1:
                    pop_side(1)
            if tb == 3:
                release()
                release()

        def ffn2(l, q, tb):
            if tb == 0:
                ffn_w["w2"] = [acquire("w20"), acquire("w21")]
            W2 = ffn_w["w2"]
            cols = slice(tb * SEG, (tb + 1) * SEG)
            for grp in ((0, 1, 2), (3, 4, 5), (6, 7)):
                banks = [nb() for _ in grp]
                for j in range(8):
                    W, wk = W2[j // 4]
                    for mi, m in enumerate(grp):
                        mm(PS[banks[mi]][:, :], W[:, j % 4, m * 128:(m + 1) * 128], H[:, j, cols], j == 0, j == 7,
                           R=[wk, ("H", j, tb)], W=psk(banks[mi]))
                for mi, m in enumerate(grp):
                    b = banks[mi]
                    dve(lambda e, m=m, b=b: e.tensor_tensor(out=XF[:, m, cols], in0=XF[:, m, cols], in1=PS[b][:, :], op=ALU.add),
                        R=psk(b) + [("XF", m, tb)], W=[("XF", m, tb)])
                pop_side(1)
            if tb == 3:
                release()
                release()

        from collections import deque
        side = deque()
        NOSIDE = bool(os.environ.get("NOSIDE"))

        def enqueue_ln(l, which, tb, scaled=True):
            tag = (l, which, tb)
            st = {}
            for part in range(3):
                side.append((tag, lambda part=part: ln_a(l, which, tb, st, part)))
            for d0 in range(0, 8, 2):
                side.append((tag, lambda d0=d0: ln_b(l, which, tb, st["j"], d0, d0 + 2, scaled)))
            if NOSIDE:
                drain(tag)

        def drain(tag=None):
            if tag is not None and not any(t == tag for t, _ in side):
                return
            while side:
                t, fn = side.popleft()
                fn()
                if tag is not None and not any(tt == tag for tt, _ in side):
                    break

        def pop_side(n=1):
            for _ in range(n):
                if side:
                    side.popleft()[1]()

        steps = []
        for l in range(L):
            for s in range(NSEG):
                need = (l - 1, 2, s) if l > 0 else None
                steps.append((need, lambda l=l, s=s: phase_gates(l, s)))
                steps.append((None, lambda l=l, s=s: phase_qk(l, s, "q")))
                steps.append((None, lambda l=l, s=s: phase_qk(l, s, "k")))
                steps.append((None, lambda l=l, s=s: phase_o(l, s)))
                steps.append((None, lambda l=l, s=s: phase_v(l, s)))
                steps.append((None, lambda l=l, s=s: phase_mlstm(l, s)))
                steps.append((None, lambda l=l, s=s: (phase_outproj(l, s), enqueue_ln(l, 1, s))))
            last_scaled = not (l == L - 1 and last_unscaled)
            for q in range(4):
                for tb in range(4):
                    steps.append(((l, 1, tb), lambda l=l, q=q, tb=tb: ffn1(l, q, tb)))
                for tb in range(4):
                    if q < 3:
                        steps.append((None, lambda l=l, q=q, tb=tb: ffn2(l, q, tb)))
                    else:
                        steps.append((None, lambda l=l, q=q, tb=tb, sc=last_scaled: (ffn2(l, q, tb), enqueue_ln(l, 2, tb, sc))))
        for i, (need, st) in enumerate(steps):
            if dbg is not None and i >= dbg:
                break
            if need is not None:
                drain(need)
            st()
        if dbg is not None:
            drain(None)

        for tt in range(int(os.environ.get('NOUT', 16))):
            xs = XS[tt % 4]
            tb = tt // 4
            if dbg is None and tt % 4 == 0:
                drain((L - 1, 2, tb))
            for dg in range(2):
                b = nb()
                for di in range(4):
                    d = dg * 4 + di
                    tp(PS[b][:, di * 128:(di + 1) * 128], XF[:, d, tt * 128:(tt + 1) * 128], IDENTF[:, :],
                       R=[("XF", d, tb), ("IDENTF",)], W=psk(b, di * 128, di * 128 + 128), inc=(di == 3))
                if dg == 0:
                    act(xs[:, 0:512], PS[b][:, :], AF.Identity, R=psk(b), W=[("XS", tt % 4)])
                else:
                    dve(lambda e, xs=xs, b=b: e.tensor_copy(out=xs[:, 512:1024], in_=PS[b][:, :]), R=psk(b), W=[("XS", tt % 4)])
            T.dma("sp", d_y[tt % 4], y_d[tt * 128:(tt + 1) * 128, :], xs, R=[("XS", tt % 4)], W=[("Y", tt)])
        drain(None)
        for dd in d_y:
            nc.sync.wait_ge(dd["sem"], dd["cnt"])
        build.stats = dict(n_wait=T.n_wait, cnt={k: v["cnt"] for k, v in T.E.items()})
    return nc


_NAMES = ["x", "w_in", "b_gate", "w_conv", "hn_g", "w_pool", "pool_scale", "w_out",
          "ln1_g", "ln1_b", "w_ff1", "w_ff2", "ln2_g", "ln2_b"]


def kernel(**inputs):
    arrs = {k: np.ascontiguousarray(np.asarray(inputs[k], dtype=np.float32)) for k in _NAMES}
    B = arrs["x"].shape[0]
    L = arrs["w_in"].shape[0]
    nc = build(depth=L)
    in_maps = []
    for b in range(B):
        m = {k: arrs[k] for k in _NAMES if k != "x"}
        m["x"] = np.ascontiguousarray(arrs["x"][b])
        in_maps.append(m)
    res = run_bass_kernel_spmd(nc, in_maps, core_ids=list(range(B)))
    return np.stack([res.results[b]["y"] for b in range(B)], axis=0).astype(np.float32)
```

```python
import math, os
from contextlib import ExitStack
import numpy as np
import concourse.bass as bass
import concourse.mybir as mybir
from concourse.bass_utils import run_bass_kernel_spmd

F32 = mybir.dt.float32
BF16 = mybir.dt.bfloat16
AF = mybir.ActivationFunctionType
ALU = mybir.AluOpType
AX = mybir.AxisListType

S = 2048
D = 1024
DIN = 2568
DFF = 4096
NSEG = 4
SEG = 512
ALPHA_FULL = (2.0 * 4) ** 0.25
LN_EPS = 1e-5
LNK = math.log(128.0 ** -0.5)
RING = 4


class Trk:
    def __init__(self, nc, es):
        self.nc, self.es = nc, es
        self.E = {}
        self.lw = {}
        self.rd = {}
        self.reg_owner = {}
        self.reg_ev = {}
        self.region_of = lambda k: None
        self.n_wait = 0
        self.snap = {}

    def add_eng(self, name, eng, own=True):
        sem = self.es.enter_context(self.nc.semaphore("s_" + name)) if own else None
        self.E[name] = dict(eng=eng, sem=sem, cnt=0, seen={}, id=name)

    def dsem(self, name):
        return dict(sem=self.es.enter_context(self.nc.semaphore("d_" + name)), cnt=0, id="d_" + name)

    def _deps(self, R, W):
        deps = {}

        def add(ev):
            if ev is None:
                return
            sid, sh, v = ev
            if sid not in deps or deps[sid][1] < v:
                deps[sid] = (sh, v)

        for k in R:
            add(self.lw.get(k))
        for k in W:
            add(self.lw.get(k))
            for sid, (sh, v) in self.rd.get(k, {}).items():
                add((sid, sh, v))
        for k in list(R) + list(W):
            rg = self.region_of(k)
            if rg is not None:
                reg, tag = rg
                if self.reg_owner.get(reg) != tag:
                    for sid, (sh, v) in self.reg_ev.get(reg, {}).items():
                        add((sid, sh, v))
        return deps

    def _wait(self, e, deps, en):
        for sid, (sh, v) in deps.items():
            if sid == en:
                if en == "pe" or v > e["cnt"] or os.environ.get("NO_OWN_WAIT"):
                    continue
            if e["seen"].get(sid, 0) >= v:
                continue
            e["eng"].wait_ge(sh, v)
            e["seen"][sid] = v
            self.n_wait += 1
            for k2, v2 in self.snap.get((sid, v), {}).items():
                if e["seen"].get(k2, 0) < v2:
                    e["seen"][k2] = v2

    def _record(self, ev, R, W, seen=None):
        sid, sh, v = ev
        if seen is not None:
            d0 = self.snap.setdefault((sid, v), {})
            for k2, v2 in seen.items():
                if d0.get(k2, 0) < v2:
                    d0[k2] = v2
        for k in W:
            self.lw[k] = ev
            self.rd[k] = {}
        for k in R:
            d = self.rd.setdefault(k, {})
            if sid not in d or d[sid][1] < v:
                d[sid] = (sh, v)
        for k in list(R) + list(W):
            rg = self.region_of(k)
            if rg is not None:
                reg, tag = rg
                if self.reg_owner.get(reg) != tag:
                    self.reg_owner[reg] = tag
                    self.reg_ev[reg] = {}
                d = self.reg_ev[reg]
                if sid not in d or d[sid][1] < v:
                    d[sid] = (sh, v)

    def op(self, en, fn, R=(), W=(), inc=True):
        e = self.E[en]
        W = list(W) + [k for k in R if k[0] == "PS" and k not in W]
        R = [k for k in R if k[0] != "PS"]
        self._wait(e, self._deps(R, W), en)
        ins = fn(e["eng"])
        if inc:
            ins.then_inc(e["sem"], 1)
            e["cnt"] += 1
            ev = (en, e["sem"], e["cnt"])
        else:
            ev = (en, e["sem"], e["cnt"] + 1)
        self._record(ev, R, W, seen=e["seen"])

    def dma(self, qn, ds, out, in_, R=(), W=(), **kw):
        e = self.E[qn]
        self._wait(e, self._deps(R, W), qn + "_q")
        e["eng"].dma_start(out=out, in_=in_, **kw).then_inc(ds["sem"], 16)
        ds["cnt"] += 16
        self._record((ds["id"], ds["sem"], ds["cnt"]), R, W, seen=e["seen"])

    def wait_all(self, qn, keys):
        e = self.E[qn]
        self._wait(e, self._deps(keys, ()), qn + "_q")


def build(depth=4, last_unscaled=True, dbg=None):
    L = depth
    ALPHA = ALPHA_FULL
    nc = bass.Bass("TRN2", target_bir_lowering=False)
    x_d = nc.dram_tensor("x", [S, D], F32, kind="ExternalInput").ap()
    w_in_d = nc.dram_tensor("w_in", [L, D, DIN], F32, kind="ExternalInput").ap()
    b_gate_d = nc.dram_tensor("b_gate", [L, 8], F32, kind="ExternalInput").ap()
    w_conv_d = nc.dram_tensor("w_conv", [L, 4, D], F32, kind="ExternalInput").ap()
    hn_g_d = nc.dram_tensor("hn_g", [L, 512], F32, kind="ExternalInput").ap()
    w_pool_d = nc.dram_tensor("w_pool", [L, 4, 128, 128], F32, kind="ExternalInput").ap()
    pool_scale_d = nc.dram_tensor("pool_scale", [L, 512], F32, kind="ExternalInput").ap()
    w_out_d = nc.dram_tensor("w_out", [L, D, D], F32, kind="ExternalInput").ap()
    ln_d = [nc.dram_tensor(n, [L, D], F32, kind="ExternalInput").ap() for n in ("ln1_g", "ln1_b", "ln2_g", "ln2_b")]
    w_ff1_d = nc.dram_tensor("w_ff1", [L, D, DFF], F32, kind="ExternalInput").ap()
    w_ff2_d = nc.dram_tensor("w_ff2", [L, DFF, D], F32, kind="ExternalInput").ap()
    y_d = nc.dram_tensor("y", [S, D], F32, kind="ExternalOutput").ap()
    gscr_d = nc.dram_tensor("gscr", [L, NSEG, 8, SEG], F32, kind="Internal").ap()

    es = ExitStack()
    with es:
        def sb(name, shape, dt):
            return es.enter_context(nc.sbuf_tensor(name, shape, dt))

        T = Trk(nc, es)
        T.add_eng("pe", nc.tensor)
        T.add_eng("act", nc.scalar)
        T.add_eng("dve", nc.vector)
        T.add_eng("pool", nc.gpsimd)
        T.add_eng("sp", nc.sync, own=False)

        XF = sb("XF", [128, 8, S], F32)
        XB = sb("XB", [128, 8, S], BF16)
        RG = [sb(f"RG{i}", [128, 4096], BF16) for i in range(RING)]
        ARENA = sb("ARENA", [128, 16384], BF16)
        PS = [es.enter_context(nc.psum_tensor(f"PS{i}", [128, 512], F32)) for i in range(8)]

        def av(c0, n, b):
            return ARENA[:, c0:c0 + n].rearrange("p (a b) -> p a b", b=b)
        QT = av(0, 2048, 512)
        KT = av(2048, 2048, 512)
        KTOK = ARENA[:, 4096:6144].rearrange("p (c h d) -> p c h d", h=4, d=128)
        VP = ARENA[:, 6144:8192].rearrange("p (c h d) -> p c h d", h=4, d=128)
        YM = av(8192, 2048, 512)
        YP = av(10240, 2048, 512)
        PB = av(12288, 2112, 528)
        UQ = [ARENA[:, 14400:14916], ARENA[:, 14916:15432]]
        H = ARENA[:, :].rearrange("p (j t) -> p j t", t=S)
        XS = [ARENA[:, i * 2048:(i + 1) * 2048].bitcast(F32) for i in range(4)]

        AB_NAMES = {"QT", "KT", "KTOK", "VP", "YM", "YP", "PB", "UQ"}
        def region_of(k):
            n = k[0]
            if n in AB_NAMES:
                return ("AR", "AB")
            if n == "H":
                return ("AR", "FFN")
            if n == "XS":
                return ("AR", "IO")
            return None
        T.region_of = region_of

        RBt = [sb(f"RB{i}", [128, 512], BF16)[:, :] for i in range(3)]
        RSQt = [sb(f"RSQ{i}", [128, 512], BF16)[:, :] for i in range(3)]
        MEANt = [sb(f"MEAN{i}", [128, 512], F32)[:, :] for i in range(2)]
        RSTDt = [sb(f"RSTD{i}", [128, 512], F32)[:, :] for i in range(2)]
        DG = [sb(f"DG{i}", [128, 4, 128], BF16) for i in range(2)]
        G8 = sb("G8", [8, SEG], F32)
        GI = sb("GI", [16, 128], F32)
        GF = sb("GF", [16, 128], F32)
        NA = sb("NA", [16, 128], F32)
        GM = sb("GM", [16, 4], F32)
        ROW = sb("ROW", [1, 32], F32)
        MFULL = sb("MFULL", [1, 4, 5], F32)
        MC = sb("MC", [1, 16], F32)
        TR = sb("TR", [1, 16], F32)
        SCR = sb("SCR", [1, 16], F32)
        NMC = sb("NMC", [16, 2], F32)
        SCB = sb("SCB", [128, 16], F32)
        ET = sb("ET", [128, 16], F32)
        ETB = sb("ETB", [128, 16], BF16)
        THRT = sb("THRT", [128, 16], F32)
        C = sb("C", [128, 4, 129], F32)
        CB = sb("CB", [128, 4, 129], BF16)
        ATs = [sb(f"AT{i}", [128, 4, 128], BF16) for i in range(2)]
        HN = sb("HN", [128, 4, 4, 128], BF16)
        SQs = [sb(f"SQ{i}", [128, 4, 128], F32) for i in range(2)]
        SM = sb("SM", [128, 12, 16], F32)
        HALO = sb("HALO", [128, 8, 3], BF16)
        PHALO = sb("PHALO", [128, 4, 16], BF16)
        CS = sb("CS", [128, 16], F32)
        DFB = sb("DFB", [128, 16], BF16)
        HR = [sb(f"HR{i}", [128, 512], BF16) for i in range(2)]
        IDENTB = sb("IDENTB", [128, 128], BF16)
        IDENTF = sb("IDENTF", [128, 128], F32)
        MASK = sb("MASK", [128, 128], BF16)
        ONESM = sb("ONESM", [128, 128], BF16)
        ONESF = sb("ONESF", [128, 128], F32)
        INVC = sb("INVC", [128, 16], F32)
        PRMS = sb("PRMS", [128, 128], F32)
        PRMT = sb("PRMT", [128, 128], F32)
        PRMA = sb("PRMA", [128, 128], F32)
        PRM2T = sb("PRM2T", [128, 32], F32)
        WCVT = sb("WCVT", [128, 128], F32)
        BG = sb("BG", [8, 4], F32)
        WPF = sb("WPF", [128, 4, 128], F32)
        WPA = sb("WPA", [128, 4, 128], BF16)
        WPBt = sb("WPBt", [128, 4, 128], BF16)
        WPC = sb("WPC", [128, 4, 128], BF16)
        GW = [sb(f"GW{i}", [128, 8, 8], BF16) for i in range(L)]

        d_x = [T.dsem(f"x{i}") for i in range(4)]
        d_y = [T.dsem(f"y{i}") for i in range(4)]
        d_prm = T.dsem("prm")
        d_bg = T.dsem("bg")
        d_g = [T.dsem("g0"), T.dsem("g1"), T.dsem("g2")]
        d_gw = T.dsem("gw")
        d_wp = T.dsem("wp")
        d_w = [T.dsem(f"w{i}") for i in range(RING)]

        bank_ctr = [0]
        reserved = set()

        def nb():
            while True:
                b = bank_ctr[0] % 8
                bank_ctr[0] += 1
                if b not in reserved:
                    return b

        def psk(b, c0=0, c1=512):
            return [("PS", b)]

        def psbf(b):
            return PS[b][:, 0:256].bitcast(BF16)

        def mm(out, lhsT, rhs, start, stop, R, W, inc=None):
            if inc is None:
                inc = stop
            T.op("pe", lambda e: e.matmul(out, lhsT=lhsT, rhs=rhs, start=start, stop=stop), R=R, W=W, inc=inc)

        def tp(out, in_, ident, R, W, inc=True):
            T.op("pe", lambda e: e.transpose(out=out, in_=in_, identity=ident), R=R, W=W, inc=inc)

        def act(out, in_, func, R, W, bias=None, scale=None):
            kw = {}
            if bias is not None:
                kw["bias"] = bias
            if scale is not None:
                kw["scale"] = scale
            T.op("act", lambda e: e.activation(out=out, in_=in_, func=func, **kw), R=R, W=W)

        def dve(fn, R, W):
            T.op("dve", fn, R=R, W=W)

        plan = []
        for l in range(L):
            for s in range(NSEG):
                for kind, c0 in (("q", 0), ("k", 512), ("o", 1544), ("v", 1024), ("p", 2056)):
                    plan.append((kind, w_in_d[l][:, c0:c0 + 512].rearrange("(k p) n -> p k n", p=128), "kn"))
                for i in range(2):
                    plan.append((f"wo{i}", w_out_d[l][:, i * 512:(i + 1) * 512].rearrange("(k p) n -> p k n", p=128), "kn"))
            for q in range(4):
                for i in range(2):
                    c0 = q * 1024 + i * 512
                    plan.append((f"w1{i}", w_ff1_d[l][:, c0:c0 + 512].rearrange("(k p) n -> p k n", p=128), "kn"))
                for i in range(2):
                    r0 = q * 1024 + i * 512
                    plan.append((f"w2{i}", w_ff2_d[l][r0:r0 + 512, :].rearrange("(j p) n -> p j n", p=128), "jn"))
        ring = dict(acq=0, rel=0)

        def slot_view(slot, lay):
            if lay == "kn":
                return RG[slot][:, :].rearrange("p (k n) -> p k n", n=512)
            return RG[slot][:, :].rearrange("p (j n) -> p j n", n=1024)

        def issue_fill(i):
            kind, src, lay = plan[i]
            slot = i % RING
            T.dma("pool", d_w[slot], slot_view(slot, lay), src, R=(), W=[("W", slot)])

        def acquire(kind):
            i = ring["acq"]
            assert plan[i][0] == kind, (plan[i][0], kind)
            ring["acq"] += 1
            slot = i % RING
            return slot_view(slot, plan[i][2]), ("W", slot)

        def release():
            i = ring["rel"]
            ring["rel"] += 1
            if i + RING < len(plan):
                issue_fill(i + RING)

        T.op("pool", lambda e: e.memset(ONESF[:], 1.0), W=[("ONESF",)])
        T.op("pool", lambda e: e.memset(ONESM[:], 1.0 / 1024.0), W=[("ONESM",)])
        T.op("pool", lambda e: e.affine_select(out=IDENTF[:], in_=ONESF[:], pattern=[[1, 128]], compare_op=ALU.is_equal,
                                               fill=0.0, base=0, channel_multiplier=-1), R=[("ONESF",)], W=[("IDENTF",)])
        T.op("pool", lambda e: e.affine_select(out=IDENTB[:], in_=ONESF[:], pattern=[[1, 128]], compare_op=ALU.is_equal,
                                               fill=0.0, base=0, channel_multiplier=-1), R=[("ONESF",)], W=[("IDENTB",)])
        T.op("pool", lambda e: e.affine_select(out=MASK[:], in_=ONESF[:], pattern=[[1, 128]], compare_op=ALU.is_ge,
                                               fill=0.0, base=0, channel_multiplier=-1), R=[("ONESF",)], W=[("MASK",)])
        for t in range(16):
            T.op("pool", lambda e, t=t: e.memset(INVC[:, t:t + 1], 1.0 / (t + 1)), W=[("INVC",)])

        d_gws = [T.dsem(f"gw{i}") for i in range(L)]
        def _gw(l_):
            with nc.allow_non_contiguous_dma(reason="gate weights, 32B rows"):
                T.dma("pool", d_gws[l_], GW[l_][:, :, :], w_in_d[l_][:, 1536:1544].rearrange("(k p) n -> p k n", p=128),
                      R=(), W=[("GW", l_)])
        _gw(0)
        for i in range(min(RING, len(plan))):
            if not os.environ.get("SKIP_PREFETCH"):
                issue_fill(i)
        for l_ in range(1, L):
            _gw(l_)

        def load_T(rows_list, dst, ncols, key):
            r = 0
            for src in rows_list:
                n = src.shape[0]
                T.dma("sp", d_prm, PRMS[r:r + n, :], src, R=(), W=[("PRMS",)])
                r += n
            b = nb()
            tp(PS[b][:, 0:r], PRMS[0:r, :], IDENTF[0:r, 0:r], R=[("PRMS",), ("IDENTF",)], W=psk(b, 0, r))
            dve(lambda e: e.tensor_copy(out=dst[:, 0:r], in_=PS[b][:, 0:r]), R=psk(b, 0, r), W=[key])

        if os.environ.get("SKIP_PARAMS"):
            load_T = lambda *a, **k: None
        load_T([a.rearrange("l (k c) -> (l k) c", c=128) for a in ln_d], PRMT, 128, ("PRMT",))
        dve(lambda e: e.tensor_scalar(out=PRMA[:, 0:32 * L], in0=PRMT[:, 0:32 * L], scalar1=ALPHA, scalar2=None, op0=ALU.mult),
            R=[("PRMT",)], W=[("PRMA",)])
        load_T([hn_g_d.rearrange("l (h c) -> (l h) c", c=128), pool_scale_d.rearrange("l (h c) -> (l h) c", c=128)],
               PRM2T, 32, ("PRM2T",))
        load_T([w_conv_d.rearrange("l j (k c) -> (l j k) c", c=128)], WCVT, 128, ("WCVT",))
        with nc.allow_non_contiguous_dma(reason="tiny bias"):
          if not os.environ.get("SKIP_BG"):
            T.dma("sp", d_bg, BG[0:8, 0:L], b_gate_d.rearrange("l g -> g l"), R=(), W=[("BG",)])

        def lncol(arr, l, k):
            return arr * 8 * L + l * 8 + k

        XM = int(os.environ.get('XSMOD', 4))
        for tt in range(int(os.environ.get('NX', 16))):
            xs = XS[tt % XM]
            T.dma("sp", d_x[tt % XM], xs, x_d[tt * 128:(tt + 1) * 128, :], R=(), W=[("XS", tt % XM)])
            for dg in range(2):
                b = nb()
                for di in range(4):
                    d = dg * 4 + di
                    tp(PS[b][:, di * 128:(di + 1) * 128], xs[:, d * 128:(d + 1) * 128], IDENTF[:],
                       R=[("XS", tt % XM), ("IDENTF",)], W=psk(b, di * 128, di * 128 + 128), inc=(di == 3))
                pv = PS[b][:, :].rearrange("p (a b) -> p a b", b=128)
                tb = tt // 4
                T.op("act", lambda e, pv=pv, dg=dg, tt=tt: e.mul(out=XF[:, dg * 4:dg * 4 + 4, tt * 128:(tt + 1) * 128], in_=pv, mul=ALPHA),
                     R=psk(b), W=[("XF", d, tb) for d in range(dg * 4, dg * 4 + 4)])
                dve(lambda e, pv=pv, dg=dg, tt=tt: e.tensor_copy(out=XB[:, dg * 4:dg * 4 + 4, tt * 128:(tt + 1) * 128], in_=pv),
                    R=psk(b), W=[("XB", d, tb) for d in range(dg * 4, dg * 4 + 4)])

        gate_tails = {}

        def phase_gates(l, s):
            cols = slice(s * SEG, (s + 1) * SEG)
            gw = GW[l]
            if s == 0:
                dve(lambda e: e.memset(MFULL[0:1, :, :], 0.0), R=(), W=[("MFULL",)])
            b = nb()
            for k in range(8):
                mm(PS[b][0:8, 0:512], gw[:, k, :], XB[:, k, cols], k == 0, k == 7,
                   R=[("GW", l), ("XB", k, s)], W=psk(b))
            act(G8[0:8, :], PS[b][0:8, 0:512], AF.Identity, R=psk(b) + [("BG",)], W=[("G8",)], bias=BG[0:8, l:l + 1])
            T.dma("sp", d_g[0], gscr_d[l, s], G8[0:8, :], R=[("G8",)], W=[("gscr",)])
            T.dma("sp", d_g[1], GI[0:16, :], gscr_d[l, s, 0:4, :].rearrange("g (c t) -> (g c) t", t=128), R=[("gscr",)], W=[("GI",)])
            T.dma("sp", d_g[2], GF[0:16, :], gscr_d[l, s, 4:8, :].rearrange("g (c t) -> (g c) t", t=128), R=[("gscr",)], W=[("GF",)])
            def t0():
                act(GF[:, :], GF[:, :], AF.Exp, R=[("GF",)], W=[("GF",)], scale=-1.0)
                act(GF[:, :], GF[:, :], AF.Ln, R=[("GF",)], W=[("GF",)], bias=1.0)
                dve(lambda e: e.tensor_tensor_scan(out=NA[:, :], data0=ONESF[0:16, :], data1=GF[:, :], initial=0.0,
                                                   op0=ALU.mult, op1=ALU.add), R=[("GF",), ("ONESF",)], W=[("NA",)])
                dve(lambda e: e.tensor_tensor(out=GI[:, :], in0=GI[:, :], in1=NA[:, :], op=ALU.add), R=[("GI",), ("NA",)], W=[("GI",)])
                dve(lambda e: e.tensor_reduce(out=GM[:, 2:3], in_=GI[:, :], axis=AX.X, op=ALU.max), R=[("GI",)], W=[("GM", 2)])
                dve(lambda e: e.tensor_scalar(out=GM[:, 0:1], in0=NA[:, 127:128], scalar1=-1.0, scalar2=None, op0=ALU.mult),
                    R=[("NA",)], W=[("GM", 0)])
                dve(lambda e: e.tensor_tensor(out=GM[:, 1:2], in0=GM[:, 2:3], in1=GM[:, 0:1], op=ALU.add),
                    R=[("GM", 2), ("GM", 0)], W=[("GM", 1)])

            def t1():
                b2 = nb()
                tp(PS[b2][0:1, 0:16], GM[0:16, 0:1], IDENTF[0:16, 0:16], R=[("GM", 0), ("IDENTF",)], W=psk(b2, 0, 16), inc=False)
                tp(PS[b2][0:1, 16:32], GM[0:16, 1:2], IDENTF[0:16, 0:16], R=[("GM", 1), ("IDENTF",)], W=psk(b2, 16, 32))
                dve(lambda e: e.tensor_copy(out=ROW[0:1, 0:32], in_=PS[b2][0:1, 0:32]), R=psk(b2, 0, 32), W=[("ROW",)])
                for h in range(4):
                    dve(lambda e, h=h: e.tensor_tensor_scan(out=MFULL[0:1, h, 1:5], data0=ROW[0:1, h * 4:(h + 1) * 4],
                                                            data1=ROW[0:1, 16 + h * 4:16 + (h + 1) * 4], initial=MFULL[0:1, h, 0:1],
                                                            op0=ALU.add, op1=ALU.max), R=[("ROW",), ("MFULL",)], W=[("MFULL",)])
                mc3 = MC[0:1, :].rearrange("p (h c) -> p h c", c=4)
                tr3 = TR[0:1, :].rearrange("p (h c) -> p h c", c=4)
                row3 = ROW[0:1, 0:16].rearrange("p (h c) -> p h c", c=4)
                dve(lambda e: e.tensor_tensor(out=mc3, in0=MFULL[0:1, :, 1:5], in1=row3, op=ALU.subtract), R=[("MFULL",), ("ROW",)], W=[("MC",)])
                dve(lambda e: e.tensor_tensor(out=tr3, in0=MFULL[0:1, :, 0:4], in1=mc3, op=ALU.subtract), R=[("MFULL",), ("MC",)], W=[("TR",)])
                act(SCR[0:1, :], TR[0:1, :], AF.Exp, R=[("TR",)], W=[("SCR",)])
                dve(lambda e: e.tensor_copy(out=MFULL[0:1, :, 0:1], in_=MFULL[0:1, :, 4:5]), R=[("MFULL",)], W=[("MFULL",)])

            def t2():
                b3 = nb()
                mm(PS[b3][:, 0:16], ONESF[0:1, 0:128], SCR[0:1, 0:16], True, True, R=[("ONESF",), ("SCR",)], W=psk(b3, 0, 16))
                dve(lambda e: e.tensor_copy(out=SCB[:, :], in_=PS[b3][:, 0:16]), R=psk(b3, 0, 16), W=[("SCB",)])
                b4 = nb()
                tp(PS[b4][0:16, 0:1], MC[0:1, 0:16], IDENTF[0:1, 0:1], R=[("MC",), ("IDENTF",)], W=psk(b4, 0, 1))
                dve(lambda e: e.tensor_scalar(out=NMC[:, 0:1], in0=PS[b4][0:16, 0:1], scalar1=-1.0, scalar2=None, op0=ALU.mult),
                    R=psk(b4, 0, 1), W=[("NMC", 0)])
                dve(lambda e: e.tensor_scalar(out=NMC[:, 1:2], in0=PS[b4][0:16, 0:1], scalar1=-1.0, scalar2=LNK, op0=ALU.mult, op1=ALU.add),
                    R=psk(b4, 0, 1), W=[("NMC", 1)])
                act(GI[:, :], GI[:, :], AF.Exp, R=[("GI",), ("NMC", 1)], W=[("GI",)], bias=NMC[:, 1:2])
                act(NA[:, :], NA[:, :], AF.Exp, R=[("NA",), ("NMC", 0)], W=[("NA",)], bias=NMC[:, 0:1])

            def t3():
                b5 = nb()
                tp(PS[b5][:, 0:16], GI[0:16, :], IDENTF[0:16, 0:16], R=[("GI",), ("IDENTF",)], W=psk(b5, 0, 16), inc=False)
                tp(PS[b5][:, 16:32], NA[0:16, :], IDENTF[0:16, 0:16], R=[("NA",), ("IDENTF",)], W=psk(b5, 16, 32))
                dve(lambda e: e.tensor_copy(out=ET[:, :], in_=PS[b5][:, 0:16]), R=psk(b5, 0, 16), W=[("ET",)])
                dve(lambda e: e.tensor_copy(out=ETB[:, :], in_=PS[b5][:, 0:16]), R=psk(b5, 0, 16), W=[("ETB",)])
                dve(lambda e: e.tensor_copy(out=THRT[:, :], in_=PS[b5][:, 16:32]), R=psk(b5, 16, 32), W=[("THRT",)])


            gate_tails[(l, s)] = [t0, t1, t2, t3]

        def phase_qk(l, s, which):
            cols = slice(s * SEG, (s + 1) * SEG)
            W, wk = acquire(which)
            DST, dname = (QT, "QT") if which == "q" else (KT, "KT")
            inject = gate_tails.get((l, s), [])
            sched = {}

            def inj(hm):
                for _ in range(sched.get(hm, 0)):
                    if inject:
                        inject.pop(0)()

            def main(h):
                tile = (0 if which == "q" else 4) + h
                uq = UQ[tile % 2]
                uk = ("UQ", tile % 2)
                b = nb()
                for k in range(8):
                    mm(PS[b][:, :], W[:, k, h * 128:(h + 1) * 128], XB[:, k, cols], k == 0, k == 7, R=[wk, ("XB", k, s)], W=psk(b))
                if s == 0:
                    dve(lambda e, uq=uq: e.memset(uq[:, 0:3], 0.0), R=(), W=[uk])
                else:
                    dve(lambda e, uq=uq, tile=tile: e.tensor_copy(out=uq[:, 0:3], in_=HALO[:, tile, :]), R=[("HALO", tile)], W=[uk])
                act(uq[:, 3:515], PS[b][:, :], AF.Identity, R=psk(b), W=[uk])
                if s < NSEG - 1:
                    dve(lambda e, uq=uq, tile=tile: e.tensor_copy(out=HALO[:, tile, :], in_=uq[:, 512:515]), R=[uk], W=[("HALO", tile)])
                dg = DG[tile % 2]
                for j in range(4):
                    c = l * 32 + j * 8 + tile
                    dve(lambda e, dg=dg, j=j, c=c: e.tensor_scalar(out=dg[:, j, :], in0=IDENTB[:, :], scalar1=WCVT[:, c:c + 1],
                                                                   scalar2=None, op0=ALU.mult),
                        R=[("IDENTB",), ("WCVT",)], W=[("DG", tile % 2, j)])

            def conv(h):
                tile = (0 if which == "q" else 4) + h
                uq = UQ[tile % 2]
                uk = ("UQ", tile % 2)
                dg = DG[tile % 2]
                b2 = nb()
                for j in range(4):
                    mm(PS[b2][:, :], dg[:, j, :], uq[:, j:j + 512], j == 0, j == 3, R=[("DG", tile % 2, j), uk], W=psk(b2))
                act(DST[:, h, :], PS[b2][:, :], AF.Silu, R=psk(b2), W=[(dname, h)])

            main(0)
            inj(0)
            for h in range(4):
                if h + 1 < 4:
                    main(h + 1)
                    inj(h + 1)
                conv(h)
            release()

        def phase_ktok(l, s):
            for h in range(4):
                b3 = nb()
                pb = psbf(b3)
                for c in range(4):
                    tp(pb[:, c * 128:(c + 1) * 128], KT[:, h, c * 128:(c + 1) * 128], IDENTB[:, :],
                       R=[("KT", h), ("IDENTB",)], W=psk(b3, 0, 256), inc=(c == 3))
                dve(lambda e, h=h, pb=pb: e.tensor_copy(out=KTOK[:, :, h, :], in_=pb[:, :].rearrange("p (c d) -> p c d", d=128)),
                    R=psk(b3, 0, 256), W=[("KTOK", h)])

        def phase_o(l, s):
            cols = slice(s * SEG, (s + 1) * SEG)
            W, wk = acquire("o")
            for h in range(4):
                b = nb()
                for k in range(8):
                    mm(PS[b][:, :], W[:, k, h * 128:(h + 1) * 128], XB[:, k, cols], k == 0, k == 7, R=[wk, ("XB", k, s)], W=psk(b))
                act(YM[:, h, :], PS[b][:, :], AF.Sigmoid, R=psk(b), W=[("YM", h)])
                if h == 3:
                    tl = gate_tails.get((l, s), [])
                    if tl:
                        tl.pop(0)()

            release()
            phase_ktok(l, s)

        pool_w = {}

        def pool_prep(l, s):
            if s == 0:
                T.dma("sp", d_wp, WPF[:, :, :], w_pool_d[l].rearrange("g c d -> c g d"), R=(), W=[("WPF",)])
                for g in range(4):
                    win = 2 ** (g + 1)
                    dve(lambda e, g=g, win=win: e.tensor_scalar(out=WPA[:, g, :], in0=WPF[:, g, :], scalar1=(1.0 / win - 1.0),
                                                                 scalar2=None, op0=ALU.mult), R=[("WPF",)], W=[("WPA", g)])
                    dve(lambda e, g=g, win=win: e.tensor_scalar(out=WPBt[:, g, :], in0=WPF[:, g, :], scalar1=1.0 / win,
                                                                 scalar2=None, op0=ALU.mult), R=[("WPF",)], W=[("WPB", g)])
                dve(lambda e: e.tensor_copy(out=WPC[:, :, :], in_=WPF[:, :, :]), R=[("WPF",)], W=[("WPC",)])

        def pool_inproj(l, s, g, bank=None):
            cols = slice(s * SEG, (s + 1) * SEG)
            if g == 0:
                pool_w["w"] = acquire("p")
            W, wk = pool_w["w"]
            b = nb() if bank is None else bank
            for k in range(8):
                mm(PS[b][:, :], W[:, k, g * 128:(g + 1) * 128], XB[:, k, cols], k == 0, k == 7, R=[wk, ("XB", k, s)], W=psk(b))
            if s > 0:
                act(PB[:, g, 0:16], PHALO[:, g, :], AF.Identity, R=[("PHALO", g)], W=[("PB", g)])
            act(PB[:, g, 16:528], PS[b][:, :], AF.Identity, R=psk(b), W=[("PB", g)])
            if s < NSEG - 1:
                act(PHALO[:, g, :], PB[:, g, 512:528], AF.Identity, R=[("PB", g)], W=[("PHALO", g)])
            if g == 3:
                release()

        def pool_group(l, s, g):
            win = 2 ** (g + 1)
            c0 = win - 1 if s == 0 else 0
            b = nb()
            mm(PS[b][:, c0:512], WPA[:, g, :], PB[:, g, 16 + c0:528], True, False, R=[("WPA", g), ("PB", g)], W=psk(b), inc=False)
            for j in range(1, win):
                mm(PS[b][:, c0:512], WPBt[:, g, :], PB[:, g, 16 + c0 - j:528 - j], False, j == win - 1,
                   R=[("WPB", g), ("PB", g)], W=psk(b))
            if s == 0:
                dve(lambda e, g=g, c0=c0: e.tensor_tensor_scan(out=CS[:, 0:c0], data0=ONESF[:, 0:c0], data1=PB[:, g, 16:16 + c0],
                                                               initial=0.0, op0=ALU.mult, op1=ALU.add),
                    R=[("PB", g), ("ONESF",)], W=[("CS",)])
                dve(lambda e, c0=c0: e.tensor_tensor(out=CS[:, 0:c0], in0=CS[:, 0:c0], in1=INVC[:, 0:c0], op=ALU.mult),
                    R=[("CS",), ("INVC",)], W=[("CS",)])
                dve(lambda e, g=g, c0=c0: e.tensor_tensor(out=DFB[:, 0:c0], in0=CS[:, 0:c0], in1=PB[:, g, 16:16 + c0], op=ALU.subtract),
                    R=[("CS",), ("PB", g)], W=[("DFB",)])
                mm(PS[b][:, 0:c0], WPC[:, g, :], DFB[:, 0:c0], True, True, R=[("WPC",), ("DFB",)], W=psk(b))
            c = 4 * L + l * 4 + g
            act(YP[:, g, :], PS[b][:, :], AF.Identity, R=psk(b) + [("PRM2T",)], W=[("YP", g)], scale=PRM2T[:, c:c + 1])
            pop_side(1)

        def phase_v(l, s):
            while reserved:
                pop_side(1)
            W, wk = acquire("v")
            et3 = ET[:, :].rearrange("p (h c) -> p h c", c=4)
            tl = gate_tails.get((l, s), [])
            if len(tl) == 4:
                tl.pop(0)()
            banks = []
            for c in range(4):
                tt = s * 4 + c
                b = nb()
                banks.append(b)
                for k in range(8):
                    mm(PS[b][:, :], XB[:, k, tt * 128:(tt + 1) * 128], W[:, k, :], k == 0, k == 7, R=[wk, ("XB", k, s)], W=psk(b))
                if c >= 1 and tl:
                    tl.pop(0)()
            while tl:
                tl.pop(0)()
            for c in range(4):
                b = banks[c]
                pv = PS[b][:, :].rearrange("p (h d) -> p h d", d=128)
                dve(lambda e, c=c, pv=pv: e.tensor_tensor(out=VP[:, c, :, :], in0=pv,
                                                          in1=et3[:, :, c].unsqueeze(2).to_broadcast([128, 4, 128]), op=ALU.mult),
                    R=psk(b) + [("ET",)], W=[("VP", c)])
            release()

        def phase_mlstm(l, s):
            while reserved:
                pop_side(1)
            pool_prep(l, s)
            scb3 = SCB[:, :].rearrange("p (h c) -> p h c", c=4)
            bO = [0, 1, 2, 3]
            bX = 4
            rot = [5, 6, 7]
            pX = PS[bX]
            kX = [("PS", bX)]
            def smv(i):
                return SM[:, i, :]

            def smc(i, c):
                return SM[:, i, :].rearrange("p (h c) -> p h c", c=4)[:, :, c]
            DNA, DN, RDN, SUM_, SSQ, MEAN_, EX2, VAR, R2, TT, SS, NBB = range(12)

            def stats(c):
                dve(lambda e: e.tensor_reduce(out=smc(SUM_, c), in_=pOs[c], axis=AX.X, op=ALU.add), R=psk(bO[c]), W=[("SM", SUM_)])
                act(SQs[c % 2][:, :, :], pOs[c], AF.Square, R=psk(bO[c]), W=[("SQ", c % 2)])
                dve(lambda e: e.tensor_reduce(out=smc(SSQ, c), in_=SQs[c % 2][:, :, :], axis=AX.X, op=ALU.add), R=[("SQ", c % 2)], W=[("SM", SSQ)])

            ri = 0
            pOs = []
            for c in range(4):
                first = (s == 0 and c == 0)
                tc_ = slice(c * 128, (c + 1) * 128)
                bS = rot[ri % 3]
                bDC = rot[(ri + 1) % 3]
                ri += 2
                pS = PS[bS][:, :].rearrange("p (h d) -> p h d", d=128)
                pDC = PS[bDC][:, :].rearrange("p (h d) -> p h d", d=128)
                pO = PS[bO[c]][:, :].rearrange("p (h d) -> p h d", d=128)
                pOs.append(pO)
                at = ATs[c % 2]
                ak = ("AT", c % 2)
                for h in range(4):
                    mm(pS[:, h, :], KT[:, h, tc_], QT[:, h, tc_], True, True, R=[("KT", h), ("QT", h)], W=psk(bS), inc=(h == 3))
                dve(lambda e, pS=pS, at=at: e.tensor_tensor(out=at[:, :, :], in0=pS, in1=MASK[:, :].unsqueeze(1).to_broadcast([128, 4, 128]),
                                                            op=ALU.mult), R=psk(bS) + [("MASK",)], W=[ak])
                for h in range(4):
                    col = h * 4 + c
                    mm(pDC[:, h, :], KTOK[:, c, h, :], VP[:, c, h, :], True, True, R=[("KTOK", h), ("VP", c)], W=psk(bDC), inc=False)
                    mm(pX[:, 16 + h:17 + h], KTOK[:, c, h, :], ETB[:, col:col + 1], True, True, R=[("KTOK", h), ("ETB",)], W=kX, inc=(h == 3))
                if not first:
                    dve(lambda e, c=c: e.tensor_tensor(out=C[:, :, :], in0=C[:, :, :],
                                                       in1=scb3[:, :, c].unsqueeze(2).to_broadcast([128, 4, 129]), op=ALU.mult),
                        R=[("C",), ("SCB",)], W=[("C",)])
                    act(CB[:, :, :], C[:, :, :], AF.Identity, R=[("C",)], W=[("CB",)])
                if c >= 1:
                    stats(c - 1)
                for h in range(4):
                    col = h * 4 + c
                    mm(pO[:, h, :], at[:, h, :], VP[:, c, h, :], True, first, R=[ak, ("VP", c)], W=psk(bO[c]), inc=False)
                    if not first:
                        mm(pO[:, h, :], QT[:, h, tc_], CB[:, h, 0:128], False, True, R=[("QT", h), ("CB",)], W=psk(bO[c]), inc=False)
                    mm(pX[:, col:col + 1], at[:, h, :], ETB[:, col:col + 1], True, first, R=[ak, ("ETB",)], W=kX, inc=(first and h == 3))
                    if not first:
                        mm(pX[:, col:col + 1], QT[:, h, tc_], CB[:, h, 128:129], False, True, R=[("QT", h), ("CB",)], W=kX, inc=(h == 3))
                if first:
                    dve(lambda e, pDC=pDC: e.tensor_copy(out=C[:, :, 0:128], in_=pDC), R=psk(bDC), W=[("C",)])
                    dve(lambda e: e.tensor_copy(out=C[:, :, 128:129], in_=pX[:, 16:20].unsqueeze(2)), R=kX, W=[("C",)])
                else:
                    dve(lambda e, pDC=pDC: e.tensor_tensor(out=C[:, :, 0:128], in0=C[:, :, 0:128], in1=pDC, op=ALU.add),
                        R=psk(bDC) + [("C",)], W=[("C",)])
                    dve(lambda e: e.tensor_tensor(out=C[:, :, 128:129], in0=C[:, :, 128:129], in1=pX[:, 16:20].unsqueeze(2), op=ALU.add),
                        R=kX + [("C",)], W=[("C",)])
            reserved.update((0, 1, 2, 3, 4))
            dve(lambda e: e.tensor_tensor(out=smv(DNA), in0=pX[:, 0:16], in1=THRT[:, 0:16], op=ALU.max), R=kX + [("THRT",)], W=[("SM", DNA)])
            dve(lambda e: e.scalar_tensor_tensor(out=smv(DN), in0=pX[:, 0:16], scalar=-1.0, in1=smv(DNA), op0=ALU.mult, op1=ALU.max),
                R=kX + [("SM", DNA)], W=[("SM", DN)])
            dve(lambda e: e.reciprocal(out=smv(RDN), in_=smv(DN)), R=[("SM", DN)], W=[("SM", RDN)])
            stats(3)
            pool_inproj(l, s, 0, bank=5)
            pool_inproj(l, s, 1, bank=7)
            dve(lambda e: e.tensor_scalar(out=smv(MEAN_), in0=smv(SUM_), scalar1=1.0 / 128, scalar2=None, op0=ALU.mult),
                R=[("SM", SUM_)], W=[("SM", MEAN_)])
            dve(lambda e: e.tensor_tensor(out=smv(EX2), in0=smv(MEAN_), in1=smv(MEAN_), op=ALU.mult), R=[("SM", MEAN_)], W=[("SM", EX2)])
            dve(lambda e: e.scalar_tensor_tensor(out=smv(VAR), in0=smv(SSQ), scalar=1.0 / 128, in1=smv(EX2), op0=ALU.mult, op1=ALU.subtract),
                R=[("SM", SSQ), ("SM", EX2)], W=[("SM", VAR)])
            dve(lambda e: e.tensor_tensor(out=smv(R2), in0=smv(RDN), in1=smv(RDN), op=ALU.mult), R=[("SM", RDN)], W=[("SM", R2)])
            dve(lambda e: e.tensor_tensor(out=smv(TT), in0=smv(R2), in1=smv(VAR), op=ALU.mult), R=[("SM", R2), ("SM", VAR)], W=[("SM", TT)])
            act(smv(TT), smv(TT), AF.Ln, R=[("SM", TT)], W=[("SM", TT)], bias=LN_EPS)
            act(smv(TT), smv(TT), AF.Exp, R=[("SM", TT)], W=[("SM", TT)], scale=-0.5)
            dve(lambda e: e.tensor_tensor(out=smv(SS), in0=smv(TT), in1=smv(RDN), op=ALU.mult), R=[("SM", TT), ("SM", RDN)], W=[("SM", SS)])
            dve(lambda e: e.scalar_tensor_tensor(out=smv(NBB), in0=smv(MEAN_), scalar=-1.0, in1=smv(SS), op0=ALU.mult, op1=ALU.mult),
                R=[("SM", MEAN_), ("SM", SS)], W=[("SM", NBB)])
            for c in (0, 1, 2, 3):
                for h in range(4):
                    col = h * 4 + c
                    if c < 1:
                        act(HN[:, c, h, :], pOs[c][:, h, :], AF.Identity, R=psk(bO[c]) + [("SM", SS), ("SM", NBB)], W=[("HN", c)],
                            scale=SM[:, SS, col:col + 1], bias=SM[:, NBB, col:col + 1])
                    else:
                        dve(lambda e, c=c, h=h, col=col: e.tensor_scalar(out=HN[:, c, h, :], in0=pOs[c][:, h, :], scalar1=SM[:, SS, col:col + 1],
                                                                         scalar2=SM[:, NBB, col:col + 1], op0=ALU.mult, op1=ALU.add),
                            R=psk(bO[c]) + [("SM", SS), ("SM", NBB)], W=[("HN", c)])
            pool_inproj(l, s, 2, bank=6)
            pool_inproj(l, s, 3, bank=5)
            for bb in (0, 1, 2, 3, 4):
                reserved.discard(bb)
            pool_group(l, s, 0)
            pool_group(l, s, 1)
            for pr in range(2):
                bT = nb()
                pHT = PS[bT][:, :].bitcast(BF16).rearrange("p (c h d) -> p c h d", h=4, d=128)
                for cc in range(2):
                    for h in range(4):
                        tp(pHT[:, cc, h, :], HN[:, 2 * pr + cc, h, :], IDENTB[:, :], R=[("HN", 2 * pr + cc), ("IDENTB",)], W=psk(bT),
                           inc=(cc == 1 and h == 3))
                for h in range(4):
                    cc_ = l * 4 + h
                    ymv = YM[:, h, pr * 256:(pr + 1) * 256].rearrange("p (c d) -> p c d", d=128)
                    dve(lambda e, h=h, cc_=cc_, ymv=ymv, pHT=pHT: e.scalar_tensor_tensor(out=ymv, in0=pHT[:, :, h, :], scalar=PRM2T[:, cc_:cc_ + 1],
                                                                                     in1=ymv, op0=ALU.mult, op1=ALU.mult),
                        R=psk(bT) + [("YM", h), ("PRM2T",)], W=[("YM", h)])

            pool_group(l, s, 2)
            pool_group(l, s, 3)

        def phase_outproj(l, s):
            cols = slice(s * SEG, (s + 1) * SEG)
            Ws = [acquire("wo0"), acquire("wo1")]
            for m in range(8):
                W, wk = Ws[m // 4]
                b = nb()
                for k in range(8):
                    rhs = YM[:, k, :] if k < 4 else YP[:, k - 4, :]
                    rk = ("YM", k) if k < 4 else ("YP", k - 4)
                    mm(PS[b][:, :], W[:, k, (m % 4) * 128:(m % 4 + 1) * 128], rhs, k == 0, k == 7, R=[wk, rk], W=psk(b))
                dve(lambda e, m=m, b=b: e.tensor_tensor(out=XF[:, m, cols], in0=XF[:, m, cols], in1=PS[b][:, :], op=ALU.add),
                    R=psk(b) + [("XF", m, s)], W=[("XF", m, s)])
                if m % 2 == 1:
                    pop_side(1)
                if m == 3:
                    release()
            release()

        ln_ctr = [0]

        def ln_a(l, which, tb, st, part):
            cols = slice(tb * SEG, (tb + 1) * SEG)
            if part == 0:
                st["j"] = ln_ctr[0] % 2
                ln_ctr[0] += 1
                st["b1"], st["b2"] = nb(), nb()
                reserved.update((st["b1"], st["b2"]))
            j, b1, b2 = st["j"], st["b1"], st["b2"]
            if part < 2:
                for d in range(part * 4, part * 4 + 4):
                    i = d % 3
                    act(RBt[i], XF[:, d, cols], AF.Identity, R=[("XF", d, tb)], W=[("RB", i)])
                    act(RSQt[i], XF[:, d, cols], AF.Square, R=[("XF", d, tb)], W=[("RSQ", i)])
                    mm(PS[b1][:, :], ONESM[:, :], RBt[i], d == 0, d == 7, R=[("ONESM",), ("RB", i)], W=psk(b1), inc=True)
                    mm(PS[b2][:, :], ONESM[:, :], RSQt[i], d == 0, d == 7, R=[("ONESM",), ("RSQ", i)], W=psk(b2), inc=True)
                return
            mean, rstd = MEANt[j], RSTDt[j]
            dve(lambda e: e.tensor_copy(out=mean, in_=PS[b1][:, :]), R=psk(b1), W=[("MEAN", j)])
            dve(lambda e: e.tensor_tensor(out=rstd, in0=mean, in1=mean, op=ALU.mult), R=[("MEAN", j)], W=[("RSTD", j)])
            dve(lambda e: e.tensor_tensor(out=rstd, in0=PS[b2][:, :], in1=rstd, op=ALU.subtract), R=psk(b2) + [("RSTD", j)], W=[("RSTD", j)])
            act(rstd, rstd, AF.Ln, R=[("RSTD", j)], W=[("RSTD", j)], bias=LN_EPS)
            act(rstd, rstd, AF.Exp, R=[("RSTD", j)], W=[("RSTD", j)], scale=-0.5)
            reserved.discard(b1)
            reserved.discard(b2)

        def ln_b(l, which, tb, j, d0, d1, scaled):
            ga, ba = (0, 1) if which == 1 else (2, 3)
            PA = PRMA if scaled else PRMT
            cols = slice(tb * SEG, (tb + 1) * SEG)
            mean, rstd = MEANt[j], RSTDt[j]
            for d in range(d0, d1):
                xf = XF[:, d, cols]
                dve(lambda e, xf=xf: e.tensor_tensor(out=xf, in0=xf, in1=mean, op=ALU.subtract), R=[("XF", d, tb), ("MEAN", j)], W=[("XF", d, tb)])
                dve(lambda e, xf=xf: e.tensor_tensor(out=xf, in0=xf, in1=rstd, op=ALU.mult), R=[("XF", d, tb), ("RSTD", j)], W=[("XF", d, tb)])
                cg, cb = lncol(ga, l, d), lncol(ba, l, d)
                act(XB[:, d, cols], xf, AF.Identity, R=[("XF", d, tb), ("PRMT",)], W=[("XB", d, tb)],
                    scale=PRMT[:, cg:cg + 1], bias=PRMT[:, cb:cb + 1])
                act(xf, xf, AF.Identity, R=[("XF", d, tb), ("PRMA",), ("PRMT",)], W=[("XF", d, tb)],
                    scale=PA[:, cg:cg + 1], bias=PA[:, cb:cb + 1])

        ffn_w = {}

        def ffn1(l, q, tb):
            if tb == 0:
                ffn_w["w1"] = [acquire("w10"), acquire("w11")]
            W1 = ffn_w["w1"]
            cols = slice(tb * SEG, (tb + 1) * SEG)
            for j in range(8):
                W, wk = W1[j // 4]
                b = nb()
                for k in range(8):
                    mm(PS[b][:, :], W[:, k, (j % 4) * 128:(j % 4 + 1) * 128], XB[:, k, cols], k == 0, k == 7,
                       R=[wk, ("XB", k, tb)], W=psk(b))
                hr = HR[j % 2]
                act(hr[:, :], PS[b][:, :], AF.Relu, R=psk(b), W=[("HR", j % 2)])
                dve(lambda e, hr=hr, j=j: e.tensor_tensor(out=H[:, j, cols], in0=hr[:, :], in1=hr[:, :], op=ALU.mult),
                    R=[("HR", j % 2)], W=[("H", j, tb)])
                if j % 2 == 1:
                    pop_side(1)
            if tb == 3:
                release()
                release()

        def ffn2(l, q, tb):
            if tb == 0:
                ffn_w["w2"] = [acquire("w20"), acquire("w21")]
            W2 = ffn_w["w2"]
            cols = slice(tb * SEG, (tb + 1) * SEG)
            for grp in ((0, 1, 2), (3, 4, 5), (6, 7)):
                banks = [nb() for _ in grp]
                for j in range(8):
                    W, wk = W2[j // 4]
                    for mi, m in enumerate(grp):
                        mm(PS[banks[mi]][:, :], W[:, j % 4, m * 128:(m + 1) * 128], H[:, j, cols], j == 0, j == 7,
                           R=[wk, ("H", j, tb)], W=psk(banks[mi]))
                for mi, m in enumerate(grp):
                    b = banks[mi]
                    dve(lambda e, m=m, b=b: e.tensor_tensor(out=XF[:, m, cols], in0=XF[:, m, cols], in1=PS[b][:, :], op=ALU.add),
                        R=psk(b) + [("XF", m, tb)], W=[("XF", m, tb)])
                pop_side(2 if q == 3 else 1)
            if tb == 3:
                release()
                release()

        from collections import deque
        side = deque()
        NOSIDE = bool(os.environ.get("NOSIDE"))

        def enqueue_ln(l, which, tb, scaled=True):
            tag = (l, which, tb)
            st = {}
            for part in range(3):
                side.append((tag, lambda part=part: ln_a(l, which, tb, st, part)))
            for d0 in range(0, 8, 2):
                side.append((tag, lambda d0=d0: ln_b(l, which, tb, st["j"], d0, d0 + 2, scaled)))
            if NOSIDE:
                drain(tag)

        def drain(tag=None):
            if tag is not None and not any(t == tag for t, _ in side):
                return
            while side:
                t, fn = side.popleft()
                fn()
                if tag is not None and not any(tt == tag for tt, _ in side):
                    break

        def pop_side(n=1):
            for _ in range(n):
                if side:
                    side.popleft()[1]()

        steps = []
        for l in range(L):
            for s in range(NSEG):
                need = (l - 1, 2, s) if l > 0 else None
                steps.append((need, lambda l=l, s=s: phase_gates(l, s)))
                steps.append((None, lambda l=l, s=s: phase_qk(l, s, "q")))
                steps.append((None, lambda l=l, s=s: phase_qk(l, s, "k")))
                steps.append((None, lambda l=l, s=s: phase_o(l, s)))
                steps.append((None, lambda l=l, s=s: phase_v(l, s)))
                steps.append((None, lambda l=l, s=s: phase_mlstm(l, s)))
                steps.append((None, lambda l=l, s=s: (phase_outproj(l, s), enqueue_ln(l, 1, s))))
            last_scaled = not (l == L - 1 and last_unscaled)
            for q in range(4):
                for tb in range(4):
                    steps.append(((l, 1, tb), lambda l=l, q=q, tb=tb: ffn1(l, q, tb)))
                for tb in range(4):
                    if q < 3:
                        steps.append((None, lambda l=l, q=q, tb=tb: ffn2(l, q, tb)))
                    else:
                        steps.append((None, lambda l=l, q=q, tb=tb, sc=last_scaled: (ffn2(l, q, tb), enqueue_ln(l, 2, tb, sc))))
        for i, (need, st) in enumerate(steps):
            if dbg is not None and i >= dbg:
                break
            if need is not None:
                drain(need)
            st()
        if dbg is not None:
            drain(None)

        for tt in range(int(os.environ.get('NOUT', 16))):
            xs = XS[tt % 4]
            tb = tt // 4
            if dbg is None and tt % 4 == 0:
                drain((L - 1, 2, tb))
            for dg in range(2):
                b = nb()
                for di in range(4):
                    d = dg * 4 + di
                    tp(PS[b][:, di * 128:(di + 1) * 128], XF[:, d, tt * 128:(tt + 1) * 128], IDENTF[:, :],
                       R=[("XF", d, tb), ("IDENTF",)], W=psk(b, di * 128, di * 128 + 128), inc=(di == 3))
                if dg == 0:
                    act(xs[:, 0:512], PS[b][:, :], AF.Identity, R=psk(b), W=[("XS", tt % 4)])
                else:
                    dve(lambda e, xs=xs, b=b: e.tensor_copy(out=xs[:, 512:1024], in_=PS[b][:, :]), R=psk(b), W=[("XS", tt % 4)])
            T.dma("sp", d_y[tt % 4], y_d[tt * 128:(tt + 1) * 128, :], xs, R=[("XS", tt % 4)], W=[("Y", tt)])
        drain(None)
        for dd in d_y:
            nc.sync.wait_ge(dd["sem"], dd["cnt"])
        build.stats = dict(n_wait=T.n_wait, cnt={k: v["cnt"] for k, v in T.E.items()})
    return nc


_NAMES = ["x", "w_in", "b_gate", "w_conv", "hn_g", "w_pool", "pool_scale", "w_out",
          "ln1_g", "ln1_b", "w_ff1", "w_ff2", "ln2_g", "ln2_b"]


def kernel(**inputs):
    arrs = {k: np.ascontiguousarray(np.asarray(inputs[k], dtype=np.float32)) for k in _NAMES}
    B = arrs["x"].shape[0]
    L = arrs["w_in"].shape[0]
    nc = build(depth=L)
    in_maps = []
    for b in range(B):
        m = {k: arrs[k] for k in _NAMES if k != "x"}
        m["x"] = np.ascontiguousarray(arrs["x"][b])
        in_maps.append(m)
    res = run_bass_kernel_spmd(nc, in_maps, core_ids=list(range(B)))
    return np.stack([res.results[b]["y"] for b in range(B)], axis=0).astype(np.float32)
```

```python
import math, os
from contextlib import ExitStack
import numpy as np
import concourse.bass as bass
import concourse.mybir as mybir
from concourse.bass_utils import run_bass_kernel_spmd

F32 = mybir.dt.float32
BF16 = mybir.dt.bfloat16
AF = mybir.ActivationFunctionType
ALU = mybir.AluOpType
AX = mybir.AxisListType

S = 2048
D = 1024
DIN = 2568
DFF = 4096
NSEG = 4
SEG = 512
ALPHA_FULL = (2.0 * 4) ** 0.25
LN_EPS = 1e-5
LNK = math.log(128.0 ** -0.5)
RING = 4


class Trk:
    def __init__(self, nc, es):
        self.nc, self.es = nc, es
        self.E = {}
        self.lw = {}
        self.rd = {}
        self.reg_owner = {}
        self.reg_ev = {}
        self.region_of = lambda k: None
        self.n_wait = 0
        self.snap = {}

    def add_eng(self, name, eng, own=True):
        sem = self.es.enter_context(self.nc.semaphore("s_" + name)) if own else None
        self.E[name] = dict(eng=eng, sem=sem, cnt=0, seen={}, id=name)

    def dsem(self, name):
        return dict(sem=self.es.enter_context(self.nc.semaphore("d_" + name)), cnt=0, id="d_" + name)

    def _deps(self, R, W):
        deps = {}

        def add(ev):
            if ev is None:
                return
            sid, sh, v = ev
            if sid not in deps or deps[sid][1] < v:
                deps[sid] = (sh, v)

        for k in R:
            add(self.lw.get(k))
        for k in W:
            add(self.lw.get(k))
            for sid, (sh, v) in self.rd.get(k, {}).items():
                add((sid, sh, v))
        for k in list(R) + list(W):
            rg = self.region_of(k)
            if rg is not None:
                reg, tag = rg
                if self.reg_owner.get(reg) != tag:
                    for sid, (sh, v) in self.reg_ev.get(reg, {}).items():
                        add((sid, sh, v))
        return deps

    def _wait(self, e, deps, en):
        for sid, (sh, v) in deps.items():
            if sid == en:
                if en == "pe" or v > e["cnt"] or os.environ.get("NO_OWN_WAIT"):
                    continue
            if e["seen"].get(sid, 0) >= v:
                continue
            e["eng"].wait_ge(sh, v)
            e["seen"][sid] = v
            self.n_wait += 1
            for k2, v2 in self.snap.get((sid, v), {}).items():
                if e["seen"].get(k2, 0) < v2:
                    e["seen"][k2] = v2

    def _record(self, ev, R, W, seen=None):
        sid, sh, v = ev
        if seen is not None:
            d0 = self.snap.setdefault((sid, v), {})
            for k2, v2 in seen.items():
                if d0.get(k2, 0) < v2:
                    d0[k2] = v2
        for k in W:
            self.lw[k] = ev
            self.rd[k] = {}
        for k in R:
            d = self.rd.setdefault(k, {})
            if sid not in d or d[sid][1] < v:
                d[sid] = (sh, v)
        for k in list(R) + list(W):
            rg = self.region_of(k)
            if rg is not None:
                reg, tag = rg
                if self.reg_owner.get(reg) != tag:
                    self.reg_owner[reg] = tag
                    self.reg_ev[reg] = {}
                d = self.reg_ev[reg]
                if sid not in d or d[sid][1] < v:
                    d[sid] = (sh, v)

    def op(self, en, fn, R=(), W=(), inc=True):
        e = self.E[en]
        W = list(W) + [k for k in R if k[0] == "PS" and k not in W]
        R = [k for k in R if k[0] != "PS"]
        self._wait(e, self._deps(R, W), en)
        ins = fn(e["eng"])
        if inc:
            ins.then_inc(e["sem"], 1)
            e["cnt"] += 1
            ev = (en, e["sem"], e["cnt"])
        else:
            ev = (en, e["sem"], e["cnt"] + 1)
        self._record(ev, R, W, seen=e["seen"])

    def dma(self, qn, ds, out, in_, R=(), W=(), **kw):
        e = self.E[qn]
        self._wait(e, self._deps(R, W), qn + "_q")
        e["eng"].dma_start(out=out, in_=in_, **kw).then_inc(ds["sem"], 16)
        ds["cnt"] += 16
        self._record((ds["id"], ds["sem"], ds["cnt"]), R, W, seen=e["seen"])

    def wait_all(self, qn, keys):
        e = self.E[qn]
        self._wait(e, self._deps(keys, ()), qn + "_q")


def build(depth=4, last_unscaled=True, dbg=None):
    L = depth
    ALPHA = ALPHA_FULL
    nc = bass.Bass("TRN2", target_bir_lowering=False)
    x_d = nc.dram_tensor("x", [S, D], F32, kind="ExternalInput").ap()
    w_in_d = nc.dram_tensor("w_in", [L, D, DIN], F32, kind="ExternalInput").ap()
    b_gate_d = nc.dram_tensor("b_gate", [L, 8], F32, kind="ExternalInput").ap()
    w_conv_d = nc.dram_tensor("w_conv", [L, 4, D], F32, kind="ExternalInput").ap()
    hn_g_d = nc.dram_tensor("hn_g", [L, 512], F32, kind="ExternalInput").ap()
    w_pool_d = nc.dram_tensor("w_pool", [L, 4, 128, 128], F32, kind="ExternalInput").ap()
    pool_scale_d = nc.dram_tensor("pool_scale", [L, 512], F32, kind="ExternalInput").ap()
    w_out_d = nc.dram_tensor("w_out", [L, D, D], F32, kind="ExternalInput").ap()
    ln_d = [nc.dram_tensor(n, [L, D], F32, kind="ExternalInput").ap() for n in ("ln1_g", "ln1_b", "ln2_g", "ln2_b")]
    w_ff1_d = nc.dram_tensor("w_ff1", [L, D, DFF], F32, kind="ExternalInput").ap()
    w_ff2_d = nc.dram_tensor("w_ff2", [L, DFF, D], F32, kind="ExternalInput").ap()
    y_d = nc.dram_tensor("y", [S, D], F32, kind="ExternalOutput").ap()
    gscr_d = nc.dram_tensor("gscr", [L, NSEG, 8, SEG], F32, kind="Internal").ap()

    es = ExitStack()
    with es:
        def sb(name, shape, dt):
            return es.enter_context(nc.sbuf_tensor(name, shape, dt))

        T = Trk(nc, es)
        T.add_eng("pe", nc.tensor)
        T.add_eng("act", nc.scalar)
        T.add_eng("dve", nc.vector)
        T.add_eng("pool", nc.gpsimd)
        T.add_eng("sp", nc.sync, own=False)

        XF = sb("XF", [128, 8, S], F32)
        XB = sb("XB", [128, 8, S], BF16)
        RG = [sb(f"RG{i}", [128, 4096], BF16) for i in range(RING)]
        ARENA = sb("ARENA", [128, 16384], BF16)
        PS = [es.enter_context(nc.psum_tensor(f"PS{i}", [128, 512], F32)) for i in range(8)]

        def av(c0, n, b):
            return ARENA[:, c0:c0 + n].rearrange("p (a b) -> p a b", b=b)
        QT = av(0, 2048, 512)
        KT = av(2048, 2048, 512)
        KTOK = ARENA[:, 4096:6144].rearrange("p (c h d) -> p c h d", h=4, d=128)
        VP = ARENA[:, 6144:8192].rearrange("p (c h d) -> p c h d", h=4, d=128)
        YM = av(8192, 2048, 512)
        YP = av(10240, 2048, 512)
        PB = av(12288, 2112, 528)
        UQ = [ARENA[:, 14400:14916], ARENA[:, 14916:15432]]
        H = ARENA[:, :].rearrange("p (j t) -> p j t", t=S)
        XS = [ARENA[:, i * 2048:(i + 1) * 2048].bitcast(F32) for i in range(4)]

        AB_NAMES = {"QT", "KT", "KTOK", "VP", "YM", "YP", "PB", "UQ"}
        def region_of(k):
            n = k[0]
            if n in AB_NAMES:
                return ("AR", "AB")
            if n == "H":
                return ("AR", "FFN")
            if n == "XS":
                return ("AR", "IO")
            return None
        T.region_of = region_of

        RBt = [sb(f"RB{i}", [128, 512], BF16)[:, :] for i in range(3)]
        RSQt = [sb(f"RSQ{i}", [128, 512], BF16)[:, :] for i in range(3)]
        MEANt = [sb(f"MEAN{i}", [128, 512], F32)[:, :] for i in range(2)]
        RSTDt = [sb(f"RSTD{i}", [128, 512], F32)[:, :] for i in range(2)]
        DG = [sb(f"DG{i}", [128, 4, 128], BF16) for i in range(2)]
        G8 = sb("G8", [8, SEG], F32)
        GI = sb("GI", [16, 128], F32)
        GF = sb("GF", [16, 128], F32)
        NA = sb("NA", [16, 128], F32)
        GM = sb("GM", [16, 4], F32)
        ROW = sb("ROW", [1, 32], F32)
        MFULL = sb("MFULL", [1, 4, 5], F32)
        MC = sb("MC", [1, 16], F32)
        TR = sb("TR", [1, 16], F32)
        SCR = sb("SCR", [1, 16], F32)
        NMC = sb("NMC", [16, 2], F32)
        SCB = sb("SCB", [128, 16], F32)
        ET = sb("ET", [128, 16], F32)
        ETB = sb("ETB", [128, 16], BF16)
        THRT = sb("THRT", [128, 16], F32)
        C = sb("C", [128, 4, 129], F32)
        CB = sb("CB", [128, 4, 129], BF16)
        ATs = [sb(f"AT{i}", [128, 4, 128], BF16) for i in range(2)]
        HN = sb("HN", [128, 4, 4, 128], BF16)
        SQs = [sb(f"SQ{i}", [128, 4, 128], F32) for i in range(2)]
        SM = sb("SM", [128, 12, 16], F32)
        HALO = sb("HALO", [128, 8, 3], BF16)
        PHALO = sb("PHALO", [128, 4, 16], BF16)
        CS = sb("CS", [128, 16], F32)
        DFB = sb("DFB", [128, 16], BF16)
        HR = [sb(f"HR{i}", [128, 512], BF16) for i in range(2)]
        IDENTB = sb("IDENTB", [128, 128], BF16)
        IDENTF = sb("IDENTF", [128, 128], F32)
        MASK = sb("MASK", [128, 128], BF16)
        ONESM = sb("ONESM", [128, 128], BF16)
        ONESF = sb("ONESF", [128, 128], F32)
        INVC = sb("INVC", [128, 16], F32)
        PRMS = sb("PRMS", [128, 128], F32)
        PRMT = sb("PRMT", [128, 128], F32)
        PRMA = sb("PRMA", [128, 128], F32)
        PRM2T = sb("PRM2T", [128, 32], F32)
        WCVT = sb("WCVT", [128, 128], F32)
        BG = sb("BG", [8, 4], F32)
        WPF = sb("WPF", [128, 4, 128], F32)
        WPA = sb("WPA", [128, 4, 128], BF16)
        WPBt = sb("WPBt", [128, 4, 128], BF16)
        WPC = sb("WPC", [128, 4, 128], BF16)
        GW = [sb(f"GW{i}", [128, 8, 8], BF16) for i in range(L)]

        d_x = [T.dsem(f"x{i}") for i in range(4)]
        d_y = [T.dsem(f"y{i}") for i in range(4)]
        d_prm = T.dsem("prm")
        d_bg = T.dsem("bg")
        d_g = [T.dsem("g0"), T.dsem("g1"), T.dsem("g2")]
        d_gw = T.dsem("gw")
        d_wp = T.dsem("wp")
        d_w = [T.dsem(f"w{i}") for i in range(RING)]

        bank_ctr = [0]
        reserved = set()

        def nb():
            while True:
                b = bank_ctr[0] % 8
                bank_ctr[0] += 1
                if b not in reserved:
                    return b

        def psk(b, c0=0, c1=512):
            return [("PS", b)]

        def psbf(b):
            return PS[b][:, 0:256].bitcast(BF16)

        def mm(out, lhsT, rhs, start, stop, R, W, inc=None):
            if inc is None:
                inc = stop
            T.op("pe", lambda e: e.matmul(out, lhsT=lhsT, rhs=rhs, start=start, stop=stop), R=R, W=W, inc=inc)

        def tp(out, in_, ident, R, W, inc=True):
            T.op("pe", lambda e: e.transpose(out=out, in_=in_, identity=ident), R=R, W=W, inc=inc)

        def act(out, in_, func, R, W, bias=None, scale=None):
            kw = {}
            if bias is not None:
                kw["bias"] = bias
            if scale is not None:
                kw["scale"] = scale
            T.op("act", lambda e: e.activation(out=out, in_=in_, func=func, **kw), R=R, W=W)

        def dve(fn, R, W):
            T.op("dve", fn, R=R, W=W)

        plan = []
        for l in range(L):
            for s in range(NSEG):
                for kind, c0 in (("q", 0), ("k", 512), ("o", 1544), ("v", 1024), ("p", 2056)):
                    plan.append((kind, w_in_d[l][:, c0:c0 + 512].rearrange("(k p) n -> p k n", p=128), "kn"))
                for i in range(2):
                    plan.append((f"wo{i}", w_out_d[l][:, i * 512:(i + 1) * 512].rearrange("(k p) n -> p k n", p=128), "kn"))
            for q in range(4):
                for i in range(2):
                    c0 = q * 1024 + i * 512
                    plan.append((f"w1{i}", w_ff1_d[l][:, c0:c0 + 512].rearrange("(k p) n -> p k n", p=128), "kn"))
                for i in range(2):
                    r0 = q * 1024 + i * 512
                    plan.append((f"w2{i}", w_ff2_d[l][r0:r0 + 512, :].rearrange("(j p) n -> p j n", p=128), "jn"))
        ring = dict(acq=0, rel=0)

        def slot_view(slot, lay):
            if lay == "kn":
                return RG[slot][:, :].rearrange("p (k n) -> p k n", n=512)
            return RG[slot][:, :].rearrange("p (j n) -> p j n", n=1024)

        def issue_fill(i):
            kind, src, lay = plan[i]
            slot = i % RING
            T.dma("pool", d_w[slot], slot_view(slot, lay), src, R=(), W=[("W", slot)])

        def acquire(kind):
            i = ring["acq"]
            assert plan[i][0] == kind, (plan[i][0], kind)
            ring["acq"] += 1
            slot = i % RING
            return slot_view(slot, plan[i][2]), ("W", slot)

        def release():
            i = ring["rel"]
            ring["rel"] += 1
            if i + RING < len(plan):
                issue_fill(i + RING)

        T.op("pool", lambda e: e.memset(ONESF[:], 1.0), W=[("ONESF",)])
        T.op("pool", lambda e: e.memset(ONESM[:], 1.0 / 1024.0), W=[("ONESM",)])
        T.op("pool", lambda e: e.affine_select(out=IDENTF[:], in_=ONESF[:], pattern=[[1, 128]], compare_op=ALU.is_equal,
                                               fill=0.0, base=0, channel_multiplier=-1), R=[("ONESF",)], W=[("IDENTF",)])
        T.op("pool", lambda e: e.affine_select(out=IDENTB[:], in_=ONESF[:], pattern=[[1, 128]], compare_op=ALU.is_equal,
                                               fill=0.0, base=0, channel_multiplier=-1), R=[("ONESF",)], W=[("IDENTB",)])
        T.op("pool", lambda e: e.affine_select(out=MASK[:], in_=ONESF[:], pattern=[[1, 128]], compare_op=ALU.is_ge,
                                               fill=0.0, base=0, channel_multiplier=-1), R=[("ONESF",)], W=[("MASK",)])
        for t in range(16):
            T.op("pool", lambda e, t=t: e.memset(INVC[:, t:t + 1], 1.0 / (t + 1)), W=[("INVC",)])

        d_gws = [T.dsem(f"gw{i}") for i in range(L)]
        def _gw(l_):
            with nc.allow_non_contiguous_dma(reason="gate weights, 32B rows"):
                T.dma("pool", d_gws[l_], GW[l_][:, :, :], w_in_d[l_][:, 1536:1544].rearrange("(k p) n -> p k n", p=128),
                      R=(), W=[("GW", l_)])
        _gw(0)
        for i in range(min(RING, len(plan))):
            if not os.environ.get("SKIP_PREFETCH"):
                issue_fill(i)
        for l_ in range(1, L):
            _gw(l_)

        def load_T(rows_list, dst, ncols, key):
            r = 0
            for src in rows_list:
                n = src.shape[0]
                T.dma("sp", d_prm, PRMS[r:r + n, :], src, R=(), W=[("PRMS",)])
                r += n
            b = nb()
            tp(PS[b][:, 0:r], PRMS[0:r, :], IDENTF[0:r, 0:r], R=[("PRMS",), ("IDENTF",)], W=psk(b, 0, r))
            dve(lambda e: e.tensor_copy(out=dst[:, 0:r], in_=PS[b][:, 0:r]), R=psk(b, 0, r), W=[key])

        if os.environ.get("SKIP_PARAMS"):
            load_T = lambda *a, **k: None
        load_T([a.rearrange("l (k c) -> (l k) c", c=128) for a in ln_d], PRMT, 128, ("PRMT",))
        dve(lambda e: e.tensor_scalar(out=PRMA[:, 0:32 * L], in0=PRMT[:, 0:32 * L], scalar1=ALPHA, scalar2=None, op0=ALU.mult),
            R=[("PRMT",)], W=[("PRMA",)])
        load_T([hn_g_d.rearrange("l (h c) -> (l h) c", c=128), pool_scale_d.rearrange("l (h c) -> (l h) c", c=128)],
               PRM2T, 32, ("PRM2T",))
        load_T([w_conv_d.rearrange("l j (k c) -> (l j k) c", c=128)], WCVT, 128, ("WCVT",))
        with nc.allow_non_contiguous_dma(reason="tiny bias"):
          if not os.environ.get("SKIP_BG"):
            T.dma("sp", d_bg, BG[0:8, 0:L], b_gate_d.rearrange("l g -> g l"), R=(), W=[("BG",)])

        def lncol(arr, l, k):
            return arr * 8 * L + l * 8 + k

        XM = int(os.environ.get('XSMOD', 4))
        for tt in range(int(os.environ.get('NX', 16))):
            xs = XS[tt % XM]
            T.dma("sp", d_x[tt % XM], xs, x_d[tt * 128:(tt + 1) * 128, :], R=(), W=[("XS", tt % XM)])
            for dg in range(2):
                b = nb()
                for di in range(4):
                    d = dg * 4 + di
                    tp(PS[b][:, di * 128:(di + 1) * 128], xs[:, d * 128:(d + 1) * 128], IDENTF[:],
                       R=[("XS", tt % XM), ("IDENTF",)], W=psk(b, di * 128, di * 128 + 128), inc=(di == 3))
                pv = PS[b][:, :].rearrange("p (a b) -> p a b", b=128)
                tb = tt // 4
                T.op("act", lambda e, pv=pv, dg=dg, tt=tt: e.mul(out=XF[:, dg * 4:dg * 4 + 4, tt * 128:(tt + 1) * 128], in_=pv, mul=ALPHA),
                     R=psk(b), W=[("XF", d, tb) for d in range(dg * 4, dg * 4 + 4)])
                dve(lambda e, pv=pv, dg=dg, tt=tt: e.tensor_copy(out=XB[:, dg * 4:dg * 4 + 4, tt * 128:(tt + 1) * 128], in_=pv),
                    R=psk(b), W=[("XB", d, tb) for d in range(dg * 4, dg * 4 + 4)])

        gate_tails = {}

        def phase_gates(l, s):
            cols = slice(s * SEG, (s + 1) * SEG)
            gw = GW[l]
            if s == 0:
                dve(lambda e: e.memset(MFULL[0:1, :, :], 0.0), R=(), W=[("MFULL",)])
            b = nb()
            for k in range(8):
                mm(PS[b][0:8, 0:512], gw[:, k, :], XB[:, k, cols], k == 0, k == 7,
                   R=[("GW", l), ("XB", k, s)], W=psk(b))
            act(G8[0:8, :], PS[b][0:8, 0:512], AF.Identity, R=psk(b) + [("BG",)], W=[("G8",)], bias=BG[0:8, l:l + 1])
            T.dma("sp", d_g[0], gscr_d[l, s], G8[0:8, :], R=[("G8",)], W=[("gscr",)])
            T.dma("sp", d_g[1], GI[0:16, :], gscr_d[l, s, 0:4, :].rearrange("g (c t) -> (g c) t", t=128), R=[("gscr",)], W=[("GI",)])
            T.dma("sp", d_g[2], GF[0:16, :], gscr_d[l, s, 4:8, :].rearrange("g (c t) -> (g c) t", t=128), R=[("gscr",)], W=[("GF",)])
            def t0():
                act(GF[:, :], GF[:, :], AF.Exp, R=[("GF",)], W=[("GF",)], scale=-1.0)
                act(GF[:, :], GF[:, :], AF.Ln, R=[("GF",)], W=[("GF",)], bias=1.0)
                dve(lambda e: e.tensor_tensor_scan(out=NA[:, :], data0=ONESF[0:16, :], data1=GF[:, :], initial=0.0,
                                                   op0=ALU.mult, op1=ALU.add), R=[("GF",), ("ONESF",)], W=[("NA",)])
                dve(lambda e: e.tensor_tensor(out=GI[:, :], in0=GI[:, :], in1=NA[:, :], op=ALU.add), R=[("GI",), ("NA",)], W=[("GI",)])
                dve(lambda e: e.tensor_reduce(out=GM[:, 2:3], in_=GI[:, :], axis=AX.X, op=ALU.max), R=[("GI",)], W=[("GM", 2)])
                dve(lambda e: e.tensor_scalar(out=GM[:, 0:1], in0=NA[:, 127:128], scalar1=-1.0, scalar2=None, op0=ALU.mult),
                    R=[("NA",)], W=[("GM", 0)])
                dve(lambda e: e.tensor_tensor(out=GM[:, 1:2], in0=GM[:, 2:3], in1=GM[:, 0:1], op=ALU.add),
                    R=[("GM", 2), ("GM", 0)], W=[("GM", 1)])

            def t1():
                b2 = nb()
                tp(PS[b2][0:1, 0:16], GM[0:16, 0:1], IDENTF[0:16, 0:16], R=[("GM", 0), ("IDENTF",)], W=psk(b2, 0, 16), inc=False)
                tp(PS[b2][0:1, 16:32], GM[0:16, 1:2], IDENTF[0:16, 0:16], R=[("GM", 1), ("IDENTF",)], W=psk(b2, 16, 32))
                dve(lambda e: e.tensor_copy(out=ROW[0:1, 0:32], in_=PS[b2][0:1, 0:32]), R=psk(b2, 0, 32), W=[("ROW",)])
                for h in range(4):
                    dve(lambda e, h=h: e.tensor_tensor_scan(out=MFULL[0:1, h, 1:5], data0=ROW[0:1, h * 4:(h + 1) * 4],
                                                            data1=ROW[0:1, 16 + h * 4:16 + (h + 1) * 4], initial=MFULL[0:1, h, 0:1],
                                                            op0=ALU.add, op1=ALU.max), R=[("ROW",), ("MFULL",)], W=[("MFULL",)])
                mc3 = MC[0:1, :].rearrange("p (h c) -> p h c", c=4)
                tr3 = TR[0:1, :].rearrange("p (h c) -> p h c", c=4)
                row3 = ROW[0:1, 0:16].rearrange("p (h c) -> p h c", c=4)
                dve(lambda e: e.tensor_tensor(out=mc3, in0=MFULL[0:1, :, 1:5], in1=row3, op=ALU.subtract), R=[("MFULL",), ("ROW",)], W=[("MC",)])
                dve(lambda e: e.tensor_tensor(out=tr3, in0=MFULL[0:1, :, 0:4], in1=mc3, op=ALU.subtract), R=[("MFULL",), ("MC",)], W=[("TR",)])
                act(SCR[0:1, :], TR[0:1, :], AF.Exp, R=[("TR",)], W=[("SCR",)])
                dve(lambda e: e.tensor_copy(out=MFULL[0:1, :, 0:1], in_=MFULL[0:1, :, 4:5]), R=[("MFULL",)], W=[("MFULL",)])

            def t2():
                b3 = nb()
                mm(PS[b3][:, 0:16], ONESF[0:1, 0:128], SCR[0:1, 0:16], True, True, R=[("ONESF",), ("SCR",)], W=psk(b3, 0, 16))
                dve(lambda e: e.tensor_copy(out=SCB[:, :], in_=PS[b3][:, 0:16]), R=psk(b3, 0, 16), W=[("SCB",)])
                b4 = nb()
                tp(PS[b4][0:16, 0:1], MC[0:1, 0:16], IDENTF[0:1, 0:1], R=[("MC",), ("IDENTF",)], W=psk(b4, 0, 1))
                dve(lambda e: e.tensor_scalar(out=NMC[:, 0:1], in0=PS[b4][0:16, 0:1], scalar1=-1.0, scalar2=None, op0=ALU.mult),
                    R=psk(b4, 0, 1), W=[("NMC", 0)])
                dve(lambda e: e.tensor_scalar(out=NMC[:, 1:2], in0=PS[b4][0:16, 0:1], scalar1=-1.0, scalar2=LNK, op0=ALU.mult, op1=ALU.add),
                    R=psk(b4, 0, 1), W=[("NMC", 1)])
                act(GI[:, :], GI[:, :], AF.Exp, R=[("GI",), ("NMC", 1)], W=[("GI",)], bias=NMC[:, 1:2])
                act(NA[:, :], NA[:, :], AF.Exp, R=[("NA",), ("NMC", 0)], W=[("NA",)], bias=NMC[:, 0:1])

            def t3():
                b5 = nb()
                tp(PS[b5][:, 0:16], GI[0:16, :], IDENTF[0:16, 0:16], R=[("GI",), ("IDENTF",)], W=psk(b5, 0, 16), inc=False)
                tp(PS[b5][:, 16:32], NA[0:16, :], IDENTF[0:16, 0:16], R=[("NA",), ("IDENTF",)], W=psk(b5, 16, 32))
                dve(lambda e: e.tensor_copy(out=ET[:, :], in_=PS[b5][:, 0:16]), R=psk(b5, 0, 16), W=[("ET",)])
                dve(lambda e: e.tensor_copy(out=ETB[:, :], in_=PS[b5][:, 0:16]), R=psk(b5, 0, 16), W=[("ETB",)])
                dve(lambda e: e.tensor_copy(out=THRT[:, :], in_=PS[b5][:, 16:32]), R=psk(b5, 16, 32), W=[("THRT",)])


            gate_tails[(l, s)] = [t0, t1, t2, t3]

        def phase_qk(l, s, which):
            cols = slice(s * SEG, (s + 1) * SEG)
            W, wk = acquire(which)
            DST, dname = (QT, "QT") if which == "q" else (KT, "KT")
            inject = gate_tails.get((l, s), [])
            sched = {}

            def inj(hm):
                for _ in range(sched.get(hm, 0)):
                    if inject:
                        inject.pop(0)()

            def main(h):
                tile = (0 if which == "q" else 4) + h
                uq = UQ[tile % 2]
                uk = ("UQ", tile % 2)
                b = nb()
                for k in range(8):
                    mm(PS[b][:, :], W[:, k, h * 128:(h + 1) * 128], XB[:, k, cols], k == 0, k == 7, R=[wk, ("XB", k, s)], W=psk(b))
                if s == 0:
                    dve(lambda e, uq=uq: e.memset(uq[:, 0:3], 0.0), R=(), W=[uk])
                else:
                    dve(lambda e, uq=uq, tile=tile: e.tensor_copy(out=uq[:, 0:3], in_=HALO[:, tile, :]), R=[("HALO", tile)], W=[uk])
                act(uq[:, 3:515], PS[b][:, :], AF.Identity, R=psk(b), W=[uk])
                if s < NSEG - 1:
                    dve(lambda e, uq=uq, tile=tile: e.tensor_copy(out=HALO[:, tile, :], in_=uq[:, 512:515]), R=[uk], W=[("HALO", tile)])
                dg = DG[tile % 2]
                for j in range(4):
                    c = l * 32 + j * 8 + tile
                    dve(lambda e, dg=dg, j=j, c=c: e.tensor_scalar(out=dg[:, j, :], in0=IDENTB[:, :], scalar1=WCVT[:, c:c + 1],
                                                                   scalar2=None, op0=ALU.mult),
                        R=[("IDENTB",), ("WCVT",)], W=[("DG", tile % 2, j)])

            def conv(h):
                tile = (0 if which == "q" else 4) + h
                uq = UQ[tile % 2]
                uk = ("UQ", tile % 2)
                dg = DG[tile % 2]
                b2 = nb()
                for j in range(4):
                    mm(PS[b2][:, :], dg[:, j, :], uq[:, j:j + 512], j == 0, j == 3, R=[("DG", tile % 2, j), uk], W=psk(b2))
                act(DST[:, h, :], PS[b2][:, :], AF.Silu, R=psk(b2), W=[(dname, h)])

            main(0)
            inj(0)
            for h in range(4):
                if h + 1 < 4:
                    main(h + 1)
                    inj(h + 1)
                conv(h)
            release()

        def phase_ktok(l, s):
            for h in range(4):
                b3 = nb()
                pb = psbf(b3)
                for c in range(4):
                    tp(pb[:, c * 128:(c + 1) * 128], KT[:, h, c * 128:(c + 1) * 128], IDENTB[:, :],
                       R=[("KT", h), ("IDENTB",)], W=psk(b3, 0, 256), inc=(c == 3))
                dve(lambda e, h=h, pb=pb: e.tensor_copy(out=KTOK[:, :, h, :], in_=pb[:, :].rearrange("p (c d) -> p c d", d=128)),
                    R=psk(b3, 0, 256), W=[("KTOK", h)])

        def phase_o(l, s):
            cols = slice(s * SEG, (s + 1) * SEG)
            W, wk = acquire("o")
            for h in range(4):
                b = nb()
                for k in range(8):
                    mm(PS[b][:, :], W[:, k, h * 128:(h + 1) * 128], XB[:, k, cols], k == 0, k == 7, R=[wk, ("XB", k, s)], W=psk(b))
                act(YM[:, h, :], PS[b][:, :], AF.Sigmoid, R=psk(b), W=[("YM", h)])
                if h == 3:
                    tl = gate_tails.get((l, s), [])
                    if tl:
                        tl.pop(0)()

            release()
            phase_ktok(l, s)

        pool_w = {}

        def pool_prep(l, s):
            if s == 0:
                T.dma("sp", d_wp, WPF[:, :, :], w_pool_d[l].rearrange("g c d -> c g d"), R=(), W=[("WPF",)])
                for g in range(4):
                    win = 2 ** (g + 1)
                    dve(lambda e, g=g, win=win: e.tensor_scalar(out=WPA[:, g, :], in0=WPF[:, g, :], scalar1=(1.0 / win - 1.0),
                                                                 scalar2=None, op0=ALU.mult), R=[("WPF",)], W=[("WPA", g)])
                    dve(lambda e, g=g, win=win: e.tensor_scalar(out=WPBt[:, g, :], in0=WPF[:, g, :], scalar1=1.0 / win,
                                                                 scalar2=None, op0=ALU.mult), R=[("WPF",)], W=[("WPB", g)])
                dve(lambda e: e.tensor_copy(out=WPC[:, :, :], in_=WPF[:, :, :]), R=[("WPF",)], W=[("WPC",)])

        def pool_inproj(l, s, g, bank=None):
            cols = slice(s * SEG, (s + 1) * SEG)
            if g == 0:
                pool_w["w"] = acquire("p")
            W, wk = pool_w["w"]
            b = nb() if bank is None else bank
            for k in range(8):
                mm(PS[b][:, :], W[:, k, g * 128:(g + 1) * 128], XB[:, k, cols], k == 0, k == 7, R=[wk, ("XB", k, s)], W=psk(b))
            if s > 0:
                act(PB[:, g, 0:16], PHALO[:, g, :], AF.Identity, R=[("PHALO", g)], W=[("PB", g)])
            act(PB[:, g, 16:528], PS[b][:, :], AF.Identity, R=psk(b), W=[("PB", g)])
            if s < NSEG - 1:
                act(PHALO[:, g, :], PB[:, g, 512:528], AF.Identity, R=[("PB", g)], W=[("PHALO", g)])
            if g == 3:
                release()

        def pool_group(l, s, g):
            win = 2 ** (g + 1)
            c0 = win - 1 if s == 0 else 0
            b = nb()
            mm(PS[b][:, c0:512], WPA[:, g, :], PB[:, g, 16 + c0:528], True, False, R=[("WPA", g), ("PB", g)], W=psk(b), inc=False)
            for j in range(1, win):
                mm(PS[b][:, c0:512], WPBt[:, g, :], PB[:, g, 16 + c0 - j:528 - j], False, j == win - 1,
                   R=[("WPB", g), ("PB", g)], W=psk(b))
            if s == 0:
                dve(lambda e, g=g, c0=c0: e.tensor_tensor_scan(out=CS[:, 0:c0], data0=ONESF[:, 0:c0], data1=PB[:, g, 16:16 + c0],
                                                               initial=0.0, op0=ALU.mult, op1=ALU.add),
                    R=[("PB", g), ("ONESF",)], W=[("CS",)])
                dve(lambda e, c0=c0: e.tensor_tensor(out=CS[:, 0:c0], in0=CS[:, 0:c0], in1=INVC[:, 0:c0], op=ALU.mult),
                    R=[("CS",), ("INVC",)], W=[("CS",)])
                dve(lambda e, g=g, c0=c0: e.tensor_tensor(out=DFB[:, 0:c0], in0=CS[:, 0:c0], in1=PB[:, g, 16:16 + c0], op=ALU.subtract),
                    R=[("CS",), ("PB", g)], W=[("DFB",)])
                mm(PS[b][:, 0:c0], WPC[:, g, :], DFB[:, 0:c0], True, True, R=[("WPC",), ("DFB",)], W=psk(b))
            c = 4 * L + l * 4 + g
            act(YP[:, g, :], PS[b][:, :], AF.Identity, R=psk(b) + [("PRM2T",)], W=[("YP", g)], scale=PRM2T[:, c:c + 1])
            pop_side(1)

        def phase_v(l, s):
            while reserved:
                pop_side(1)
            W, wk = acquire("v")
            et3 = ET[:, :].rearrange("p (h c) -> p h c", c=4)
            tl = gate_tails.get((l, s), [])
            if len(tl) == 4:
                tl.pop(0)()
            banks = []
            for c in range(4):
                tt = s * 4 + c
                b = nb()
                banks.append(b)
                for k in range(8):
                    mm(PS[b][:, :], XB[:, k, tt * 128:(tt + 1) * 128], W[:, k, :], k == 0, k == 7, R=[wk, ("XB", k, s)], W=psk(b))
                if c >= 1 and tl:
                    tl.pop(0)()
            while tl:
                tl.pop(0)()
            for c in range(4):
                b = banks[c]
                pv = PS[b][:, :].rearrange("p (h d) -> p h d", d=128)
                dve(lambda e, c=c, pv=pv: e.tensor_tensor(out=VP[:, c, :, :], in0=pv,
                                                          in1=et3[:, :, c].unsqueeze(2).to_broadcast([128, 4, 128]), op=ALU.mult),
                    R=psk(b) + [("ET",)], W=[("VP", c)])
            release()

        def phase_mlstm(l, s):
            while reserved:
                pop_side(1)
            pool_prep(l, s)
            scb3 = SCB[:, :].rearrange("p (h c) -> p h c", c=4)
            bO = [0, 1, 2, 3]
            bX = 4
            rot = [5, 6, 7]
            pX = PS[bX]
            kX = [("PS", bX)]
            def smv(i):
                return SM[:, i, :]

            def smc(i, c):
                return SM[:, i, :].rearrange("p (h c) -> p h c", c=4)[:, :, c]
            DNA, DN, RDN, SUM_, SSQ, MEAN_, EX2, VAR, R2, TT, SS, NBB = range(12)

            def stats(c):
                dve(lambda e: e.tensor_reduce(out=smc(SUM_, c), in_=pOs[c], axis=AX.X, op=ALU.add), R=psk(bO[c]), W=[("SM", SUM_)])
                act(SQs[c % 2][:, :, :], pOs[c], AF.Square, R=psk(bO[c]), W=[("SQ", c % 2)])
                dve(lambda e: e.tensor_reduce(out=smc(SSQ, c), in_=SQs[c % 2][:, :, :], axis=AX.X, op=ALU.add), R=[("SQ", c % 2)], W=[("SM", SSQ)])

            ri = 0
            pOs = []
            for c in range(4):
                first = (s == 0 and c == 0)
                tc_ = slice(c * 128, (c + 1) * 128)
                bS = rot[ri % 3]
                bDC = rot[(ri + 1) % 3]
                ri += 2
                pS = PS[bS][:, :].rearrange("p (h d) -> p h d", d=128)
                pDC = PS[bDC][:, :].rearrange("p (h d) -> p h d", d=128)
                pO = PS[bO[c]][:, :].rearrange("p (h d) -> p h d", d=128)
                pOs.append(pO)
                at = ATs[c % 2]
                ak = ("AT", c % 2)
                for h in range(4):
                    mm(pS[:, h, :], KT[:, h, tc_], QT[:, h, tc_], True, True, R=[("KT", h), ("QT", h)], W=psk(bS), inc=(h == 3))
                dve(lambda e, pS=pS, at=at: e.tensor_tensor(out=at[:, :, :], in0=pS, in1=MASK[:, :].unsqueeze(1).to_broadcast([128, 4, 128]),
                                                            op=ALU.mult), R=psk(bS) + [("MASK",)], W=[ak])
                for h in range(4):
                    col = h * 4 + c
                    mm(pDC[:, h, :], KTOK[:, c, h, :], VP[:, c, h, :], True, True, R=[("KTOK", h), ("VP", c)], W=psk(bDC), inc=False)
                    mm(pX[:, 16 + h:17 + h], KTOK[:, c, h, :], ETB[:, col:col + 1], True, True, R=[("KTOK", h), ("ETB",)], W=kX, inc=(h == 3))
                if not first:
                    dve(lambda e, c=c: e.tensor_tensor(out=C[:, :, :], in0=C[:, :, :],
                                                       in1=scb3[:, :, c].unsqueeze(2).to_broadcast([128, 4, 129]), op=ALU.mult),
                        R=[("C",), ("SCB",)], W=[("C",)])
                    act(CB[:, :, :], C[:, :, :], AF.Identity, R=[("C",)], W=[("CB",)])
                if c >= 1:
                    stats(c - 1)
                for h in range(4):
                    col = h * 4 + c
                    mm(pO[:, h, :], at[:, h, :], VP[:, c, h, :], True, first, R=[ak, ("VP", c)], W=psk(bO[c]), inc=False)
                    if not first:
                        mm(pO[:, h, :], QT[:, h, tc_], CB[:, h, 0:128], False, True, R=[("QT", h), ("CB",)], W=psk(bO[c]), inc=False)
                    mm(pX[:, col:col + 1], at[:, h, :], ETB[:, col:col + 1], True, first, R=[ak, ("ETB",)], W=kX, inc=(first and h == 3))
                    if not first:
                        mm(pX[:, col:col + 1], QT[:, h, tc_], CB[:, h, 128:129], False, True, R=[("QT", h), ("CB",)], W=kX, inc=(h == 3))
                if first:
                    dve(lambda e, pDC=pDC: e.tensor_copy(out=C[:, :, 0:128], in_=pDC), R=psk(bDC), W=[("C",)])
                    dve(lambda e: e.tensor_copy(out=C[:, :, 128:129], in_=pX[:, 16:20].unsqueeze(2)), R=kX, W=[("C",)])
                else:
                    dve(lambda e, pDC=pDC: e.tensor_tensor(out=C[:, :, 0:128], in0=C[:, :, 0:128], in1=pDC, op=ALU.add),
                        R=psk(bDC) + [("C",)], W=[("C",)])
                    dve(lambda e: e.tensor_tensor(out=C[:, :, 128:129], in0=C[:, :, 128:129], in1=pX[:, 16:20].unsqueeze(2), op=ALU.add),
                        R=kX + [("C",)], W=[("C",)])
            reserved.update((0, 1, 2, 3, 4))
            dve(lambda e: e.tensor_tensor(out=smv(DNA), in0=pX[:, 0:16], in1=THRT[:, 0:16], op=ALU.max), R=kX + [("THRT",)], W=[("SM", DNA)])
            dve(lambda e: e.scalar_tensor_tensor(out=smv(DN), in0=pX[:, 0:16], scalar=-1.0, in1=smv(DNA), op0=ALU.mult, op1=ALU.max),
                R=kX + [("SM", DNA)], W=[("SM", DN)])
            dve(lambda e: e.reciprocal(out=smv(RDN), in_=smv(DN)), R=[("SM", DN)], W=[("SM", RDN)])
            stats(3)
            pool_inproj(l, s, 0, bank=5)
            pool_inproj(l, s, 1, bank=7)
            dve(lambda e: e.tensor_scalar(out=smv(MEAN_), in0=smv(SUM_), scalar1=1.0 / 128, scalar2=None, op0=ALU.mult),
                R=[("SM", SUM_)], W=[("SM", MEAN_)])
            dve(lambda e: e.tensor_tensor(out=smv(EX2), in0=smv(MEAN_), in1=smv(MEAN_), op=ALU.mult), R=[("SM", MEAN_)], W=[("SM", EX2)])
            dve(lambda e: e.scalar_tensor_tensor(out=smv(VAR), in0=smv(SSQ), scalar=1.0 / 128, in1=smv(EX2), op0=ALU.mult, op1=ALU.subtract),
                R=[("SM", SSQ), ("SM", EX2)], W=[("SM", VAR)])
            dve(lambda e: e.tensor_tensor(out=smv(R2), in0=smv(RDN), in1=smv(RDN), op=ALU.mult), R=[("SM", RDN)], W=[("SM", R2)])
            dve(lambda e: e.tensor_tensor(out=smv(TT), in0=smv(R2), in1=smv(VAR), op=ALU.mult), R=[("SM", R2), ("SM", VAR)], W=[("SM", TT)])
            act(smv(TT), smv(TT), AF.Ln, R=[("SM", TT)], W=[("SM", TT)], bias=LN_EPS)
            act(smv(TT), smv(TT), AF.Exp, R=[("SM", TT)], W=[("SM", TT)], scale=-0.5)
            dve(lambda e: e.tensor_tensor(out=smv(SS), in0=smv(TT), in1=smv(RDN), op=ALU.mult), R=[("SM", TT), ("SM", RDN)], W=[("SM", SS)])
            dve(lambda e: e.scalar_tensor_tensor(out=smv(NBB), in0=smv(MEAN_), scalar=-1.0, in1=smv(SS), op0=ALU.mult, op1=ALU.mult),
                R=[("SM", MEAN_), ("SM", SS)], W=[("SM", NBB)])
            for c in (0, 1, 2, 3):
                for h in range(4):
                    col = h * 4 + c
                    if c < 1:
                        act(HN[:, c, h, :], pOs[c][:, h, :], AF.Identity, R=psk(bO[c]) + [("SM", SS), ("SM", NBB)], W=[("HN", c)],
                            scale=SM[:, SS, col:col + 1], bias=SM[:, NBB, col:col + 1])
                    else:
                        dve(lambda e, c=c, h=h, col=col: e.tensor_scalar(out=HN[:, c, h, :], in0=pOs[c][:, h, :], scalar1=SM[:, SS, col:col + 1],
                                                                         scalar2=SM[:, NBB, col:col + 1], op0=ALU.mult, op1=ALU.add),
                            R=psk(bO[c]) + [("SM", SS), ("SM", NBB)], W=[("HN", c)])
            pool_inproj(l, s, 2, bank=6)
            pool_inproj(l, s, 3, bank=5)
            for bb in (0, 1, 2, 3, 4):
                reserved.discard(bb)
            pool_group(l, s, 0)
            pool_group(l, s, 1)
            for pr in range(2):
                bT = nb()
                pHT = PS[bT][:, :].bitcast(BF16).rearrange("p (c h d) -> p c h d", h=4, d=128)
                for cc in range(2):
                    for h in range(4):
                        tp(pHT[:, cc, h, :], HN[:, 2 * pr + cc, h, :], IDENTB[:, :], R=[("HN", 2 * pr + cc), ("IDENTB",)], W=psk(bT),
                           inc=(cc == 1 and h == 3))
                for h in range(4):
                    cc_ = l * 4 + h
                    ymv = YM[:, h, pr * 256:(pr + 1) * 256].rearrange("p (c d) -> p c d", d=128)
                    dve(lambda e, h=h, cc_=cc_, ymv=ymv, pHT=pHT: e.scalar_tensor_tensor(out=ymv, in0=pHT[:, :, h, :], scalar=PRM2T[:, cc_:cc_ + 1],
                                                                                     in1=ymv, op0=ALU.mult, op1=ALU.mult),
                        R=psk(bT) + [("YM", h), ("PRM2T",)], W=[("YM", h)])

            pool_group(l, s, 2)
            pool_group(l, s, 3)

        def phase_outproj(l, s):
            cols = slice(s * SEG, (s + 1) * SEG)
            Ws = [acquire("wo0"), acquire("wo1")]
            for m in range(8):
                W, wk = Ws[m // 4]
                b = nb()
                for k in range(8):
                    rhs = YM[:, k, :] if k < 4 else YP[:, k - 4, :]
                    rk = ("YM", k) if k < 4 else ("YP", k - 4)
                    mm(PS[b][:, :], W[:, k, (m % 4) * 128:(m % 4 + 1) * 128], rhs, k == 0, k == 7, R=[wk, rk], W=psk(b))
                dve(lambda e, m=m, b=b: e.tensor_tensor(out=XF[:, m, cols], in0=XF[:, m, cols], in1=PS[b][:, :], op=ALU.add),
                    R=psk(b) + [("XF", m, s)], W=[("XF", m, s)])
                if m % 2 == 1:
                    pop_side(1)
                if m == 3:
                    release()
            release()

        ln_ctr = [0]

        def ln_a(l, which, tb, st, part):
            cols = slice(tb * SEG, (tb + 1) * SEG)
            if part == 0:
                st["j"] = ln_ctr[0] % 2
                ln_ctr[0] += 1
                st["b1"], st["b2"] = nb(), nb()
                reserved.update((st["b1"], st["b2"]))
            j, b1, b2 = st["j"], st["b1"], st["b2"]
            if part < 2:
                for d in range(part * 4, part * 4 + 4):
                    i = d % 3
                    act(RBt[i], XF[:, d, cols], AF.Identity, R=[("XF", d, tb)], W=[("RB", i)])
                    act(RSQt[i], XF[:, d, cols], AF.Square, R=[("XF", d, tb)], W=[("RSQ", i)])
                    mm(PS[b1][:, :], ONESM[:, :], RBt[i], d == 0, d == 7, R=[("ONESM",), ("RB", i)], W=psk(b1), inc=True)
                    mm(PS[b2][:, :], ONESM[:, :], RSQt[i], d == 0, d == 7, R=[("ONESM",), ("RSQ", i)], W=psk(b2), inc=True)
                return
            mean, rstd = MEANt[j], RSTDt[j]
            dve(lambda e: e.tensor_copy(out=mean, in_=PS[b1][:, :]), R=psk(b1), W=[("MEAN", j)])
            dve(lambda e: e.tensor_tensor(out=rstd, in0=mean, in1=mean, op=ALU.mult), R=[("MEAN", j)], W=[("RSTD", j)])
            dve(lambda e: e.tensor_tensor(out=rstd, in0=PS[b2][:, :], in1=rstd, op=ALU.subtract), R=psk(b2) + [("RSTD", j)], W=[("RSTD", j)])
            act(rstd, rstd, AF.Ln, R=[("RSTD", j)], W=[("RSTD", j)], bias=LN_EPS)
            act(rstd, rstd, AF.Exp, R=[("RSTD", j)], W=[("RSTD", j)], scale=-0.5)
            reserved.discard(b1)
            reserved.discard(b2)

        def ln_b(l, which, tb, j, d0, d1, scaled):
            ga, ba = (0, 1) if which == 1 else (2, 3)
            PA = PRMA if scaled else PRMT
            cols = slice(tb * SEG, (tb + 1) * SEG)
            mean, rstd = MEANt[j], RSTDt[j]
            for d in range(d0, d1):
                xf = XF[:, d, cols]
                dve(lambda e, xf=xf: e.tensor_tensor(out=xf, in0=xf, in1=mean, op=ALU.subtract), R=[("XF", d, tb), ("MEAN", j)], W=[("XF", d, tb)])
                dve(lambda e, xf=xf: e.tensor_tensor(out=xf, in0=xf, in1=rstd, op=ALU.mult), R=[("XF", d, tb), ("RSTD", j)], W=[("XF", d, tb)])
                cg, cb = lncol(ga, l, d), lncol(ba, l, d)
                act(XB[:, d, cols], xf, AF.Identity, R=[("XF", d, tb), ("PRMT",)], W=[("XB", d, tb)],
                    scale=PRMT[:, cg:cg + 1], bias=PRMT[:, cb:cb + 1])
                act(xf, xf, AF.Identity, R=[("XF", d, tb), ("PRMA",), ("PRMT",)], W=[("XF", d, tb)],
                    scale=PA[:, cg:cg + 1], bias=PA[:, cb:cb + 1])

        ffn_w = {}

        def ffn1(l, q, tb):
            if tb == 0:
                ffn_w["w1"] = [acquire("w10"), acquire("w11")]
            W1 = ffn_w["w1"]
            cols = slice(tb * SEG, (tb + 1) * SEG)
            for j in range(8):
                W, wk = W1[j // 4]
                b = nb()
                for k in range(8):
                    mm(PS[b][:, :], W[:, k, (j % 4) * 128:(j % 4 + 1) * 128], XB[:, k, cols], k == 0, k == 7,
                       R=[wk, ("XB", k, tb)], W=psk(b))
                hr = HR[j % 2]
                act(hr[:, :], PS[b][:, :], AF.Relu, R=psk(b), W=[("HR", j % 2)])
                dve(lambda e, hr=hr, j=j: e.tensor_tensor(out=H[:, j, cols], in0=hr[:, :], in1=hr[:, :], op=ALU.mult),
                    R=[("HR", j % 2)], W=[("H", j, tb)])
                if j % 2 == 1:
                    pop_side(1)
            if tb == 3:
                release()
                release()

        def ffn2(l, q, tb):
            if tb == 0:
                ffn_w["w2"] = [acquire("w20"), acquire("w21")]
            W2 = ffn_w["w2"]
            cols = slice(tb * SEG, (tb + 1) * SEG)
            for grp in ((0, 1, 2), (3, 4, 5), (6, 7)):
                banks = [nb() for _ in grp]
                for j in range(8):
                    W, wk = W2[j // 4]
                    for mi, m in enumerate(grp):
                        mm(PS[banks[mi]][:, :], W[:, j % 4, m * 128:(m + 1) * 128], H[:, j, cols], j == 0, j == 7,
                           R=[wk, ("H", j, tb)], W=psk(banks[mi]))
                for mi, m in enumerate(grp):
                    b = banks[mi]
                    dve(lambda e, m=m, b=b: e.tensor_tensor(out=XF[:, m, cols], in0=XF[:, m, cols], in1=PS[b][:, :], op=ALU.add),
                        R=psk(b) + [("XF", m, tb)], W=[("XF", m, tb)])
                pop_side(3 if q == 3 else 1)
            if tb == 3:
                release()
                release()

        from collections import deque
        side = deque()
        NOSIDE = bool(os.environ.get("NOSIDE"))

        def enqueue_ln(l, which, tb, scaled=True):
            tag = (l, which, tb)
            st = {}
            for part in range(3):
                side.append((tag, lambda part=part: ln_a(l, which, tb, st, part)))
            for d0 in range(0, 8, 2):
                side.append((tag, lambda d0=d0: ln_b(l, which, tb, st["j"], d0, d0 + 2, scaled)))
            if NOSIDE:
                drain(tag)

        def drain(tag=None):
            if tag is not None and not any(t == tag for t, _ in side):
                return
            while side:
                t, fn = side.popleft()
                fn()
                if tag is not None and not any(tt == tag for tt, _ in side):
                    break

        def pop_side(n=1):
            for _ in range(n):
                if side:
                    side.popleft()[1]()

        steps = []
        for l in range(L):
            for s in range(NSEG):
                need = (l - 1, 2, s) if l > 0 else None
                steps.append((need, lambda l=l, s=s: phase_gates(l, s)))
                steps.append((None, lambda l=l, s=s: phase_qk(l, s, "q")))
                steps.append((None, lambda l=l, s=s: phase_qk(l, s, "k")))
                steps.append((None, lambda l=l, s=s: phase_o(l, s)))
                steps.append((None, lambda l=l, s=s: phase_v(l, s)))
                steps.append((None, lambda l=l, s=s: phase_mlstm(l, s)))
                steps.append((None, lambda l=l, s=s: (phase_outproj(l, s), enqueue_ln(l, 1, s))))
            last_scaled = not (l == L - 1 and last_unscaled)
            for q in range(4):
                for tb in range(4):
                    steps.append(((l, 1, tb), lambda l=l, q=q, tb=tb: ffn1(l, q, tb)))
                for tb in range(4):
                    if q < 3:
                        steps.append((None, lambda l=l, q=q, tb=tb: ffn2(l, q, tb)))
                    else:
                        steps.append((None, lambda l=l, q=q, tb=tb, sc=last_scaled: (ffn2(l, q, tb), enqueue_ln(l, 2, tb, sc))))
        for i, (need, st) in enumerate(steps):
            if dbg is not None and i >= dbg:
                break
            if need is not None:
                drain(need)
            st()
        if dbg is not None:
            drain(None)

        for tt in range(int(os.environ.get('NOUT', 16))):
            xs = XS[tt % 4]
            tb = tt // 4
            if dbg is None and tt % 4 == 0:
                drain((L - 1, 2, tb))
            for dg in range(2):
                b = nb()
                for di in range(4):
                    d = dg * 4 + di
                    tp(PS[b][:, di * 128:(di + 1) * 128], XF[:, d, tt * 128:(tt + 1) * 128], IDENTF[:, :],
                       R=[("XF", d, tb), ("IDENTF",)], W=psk(b, di * 128, di * 128 + 128), inc=(di == 3))
                if dg == 0:
                    act(xs[:, 0:512], PS[b][:, :], AF.Identity, R=psk(b), W=[("XS", tt % 4)])
                else:
                    dve(lambda e, xs=xs, b=b: e.tensor_copy(out=xs[:, 512:1024], in_=PS[b][:, :]), R=psk(b), W=[("XS", tt % 4)])
            T.dma("sp", d_y[tt % 4], y_d[tt * 128:(tt + 1) * 128, :], xs, R=[("XS", tt % 4)], W=[("Y", tt)])
        drain(None)
        for dd in d_y:
            nc.sync.wait_ge(dd["sem"], dd["cnt"])
        build.stats = dict(n_wait=T.n_wait, cnt={k: v["cnt"] for k, v in T.E.items()})
    return nc


_NAMES = ["x", "w_in", "b_gate", "w_conv", "hn_g", "w_pool", "pool_scale", "w_out",
          "ln1_g", "ln1_b", "w_ff1", "w_ff2", "ln2_g", "ln2_b"]


def kernel(**inputs):
    arrs = {k: np.ascontiguousarray(np.asarray(inputs[k], dtype=np.float32)) for k in _NAMES}
    B = arrs["x"].shape[0]
    L = arrs["w_in"].shape[0]
    nc = build(depth=L)
    in_maps = []
    for b in range(B):
        m = {k: arrs[k] for k in _NAMES if k != "x"}
        m["x"] = np.ascontiguousarray(arrs["x"][b])
        in_maps.append(m)
    res = run_bass_kernel_spmd(nc, in_maps, core_ids=list(range(B)))
    return np.stack([res.results[b]["y"] for b in range(B)], axis=0).astype(np.float32)
```

```python
import math, os
from contextlib import ExitStack
import numpy as np
import concourse.bass as bass
import concourse.mybir as mybir
from concourse.bass_utils import run_bass_kernel_spmd

F32 = mybir.dt.float32
BF16 = mybir.dt.bfloat16
AF = mybir.ActivationFunctionType
ALU = mybir.AluOpType
AX = mybir.AxisListType

S = 2048
D = 1024
DIN = 2568
DFF = 4096
NSEG = 4
SEG = 512
ALPHA_FULL = (2.0 * 4) ** 0.25
LN_EPS = 1e-5
LNK = math.log(128.0 ** -0.5)
RING = 4


class Trk:
    def __init__(self, nc, es):
        self.nc, self.es = nc, es
        self.E = {}
        self.lw = {}
        self.rd = {}
        self.reg_owner = {}
        self.reg_ev = {}
        self.region_of = lambda k: None
        self.n_wait = 0
        self.snap = {}

    def add_eng(self, name, eng, own=True):
        sem = self.es.enter_context(self.nc.semaphore("s_" + name)) if own else None
        self.E[name] = dict(eng=eng, sem=sem, cnt=0, seen={}, id=name)

    def dsem(self, name):
        return dict(sem=self.es.enter_context(self.nc.semaphore("d_" + name)), cnt=0, id="d_" + name)

    def _deps(self, R, W):
        deps = {}

        def add(ev):
            if ev is None:
                return
            sid, sh, v = ev
            if sid not in deps or deps[sid][1] < v:
                deps[sid] = (sh, v)

        for k in R:
            add(self.lw.get(k))
        for k in W:
            add(self.lw.get(k))
            for sid, (sh, v) in self.rd.get(k, {}).items():
                add((sid, sh, v))
        for k in list(R) + list(W):
            rg = self.region_of(k)
            if rg is not None:
                reg, tag = rg
                if self.reg_owner.get(reg) != tag:
                    for sid, (sh, v) in self.reg_ev.get(reg, {}).items():
                        add((sid, sh, v))
        return deps

    def _wait(self, e, deps, en):
        for sid, (sh, v) in deps.items():
            if sid == en:
                if en == "pe" or v > e["cnt"] or os.environ.get("NO_OWN_WAIT"):
                    continue
            if e["seen"].get(sid, 0) >= v:
                continue
            e["eng"].wait_ge(sh, v)
            e["seen"][sid] = v
            self.n_wait += 1
            for k2, v2 in self.snap.get((sid, v), {}).items():
                if e["seen"].get(k2, 0) < v2:
                    e["seen"][k2] = v2

    def _record(self, ev, R, W, seen=None):
        sid, sh, v = ev
        if seen is not None:
            d0 = self.snap.setdefault((sid, v), {})
            for k2, v2 in seen.items():
                if d0.get(k2, 0) < v2:
                    d0[k2] = v2
        for k in W:
            self.lw[k] = ev
            self.rd[k] = {}
        for k in R:
            d = self.rd.setdefault(k, {})
            if sid not in d or d[sid][1] < v:
                d[sid] = (sh, v)
        for k in list(R) + list(W):
            rg = self.region_of(k)
            if rg is not None:
                reg, tag = rg
                if self.reg_owner.get(reg) != tag:
                    self.reg_owner[reg] = tag
                    self.reg_ev[reg] = {}
                d = self.reg_ev[reg]
                if sid not in d or d[sid][1] < v:
                    d[sid] = (sh, v)

    def op(self, en, fn, R=(), W=(), inc=True):
        e = self.E[en]
        W = list(W) + [k for k in R if k[0] == "PS" and k not in W]
        R = [k for k in R if k[0] != "PS"]
        self._wait(e, self._deps(R, W), en)
        ins = fn(e["eng"])
        if inc:
            ins.then_inc(e["sem"], 1)
            e["cnt"] += 1
            ev = (en, e["sem"], e["cnt"])
        else:
            ev = (en, e["sem"], e["cnt"] + 1)
        self._record(ev, R, W, seen=e["seen"])

    def dma(self, qn, ds, out, in_, R=(), W=(), **kw):
        e = self.E[qn]
        self._wait(e, self._deps(R, W), qn + "_q")
        e["eng"].dma_start(out=out, in_=in_, **kw).then_inc(ds["sem"], 16)
        ds["cnt"] += 16
        self._record((ds["id"], ds["sem"], ds["cnt"]), R, W, seen=e["seen"])

    def wait_all(self, qn, keys):
        e = self.E[qn]
        self._wait(e, self._deps(keys, ()), qn + "_q")


def build(depth=4, last_unscaled=True, dbg=None):
    L = depth
    ALPHA = ALPHA_FULL
    nc = bass.Bass("TRN2", target_bir_lowering=False)
    x_d = nc.dram_tensor("x", [S, D], F32, kind="ExternalInput").ap()
    w_in_d = nc.dram_tensor("w_in", [L, D, DIN], F32, kind="ExternalInput").ap()
    b_gate_d = nc.dram_tensor("b_gate", [L, 8], F32, kind="ExternalInput").ap()
    w_conv_d = nc.dram_tensor("w_conv", [L, 4, D], F32, kind="ExternalInput").ap()
    hn_g_d = nc.dram_tensor("hn_g", [L, 512], F32, kind="ExternalInput").ap()
    w_pool_d = nc.dram_tensor("w_pool", [L, 4, 128, 128], F32, kind="ExternalInput").ap()
    pool_scale_d = nc.dram_tensor("pool_scale", [L, 512], F32, kind="ExternalInput").ap()
    w_out_d = nc.dram_tensor("w_out", [L, D, D], F32, kind="ExternalInput").ap()
    ln_d = [nc.dram_tensor(n, [L, D], F32, kind="ExternalInput").ap() for n in ("ln1_g", "ln1_b", "ln2_g", "ln2_b")]
    w_ff1_d = nc.dram_tensor("w_ff1", [L, D, DFF], F32, kind="ExternalInput").ap()
    w_ff2_d = nc.dram_tensor("w_ff2", [L, DFF, D], F32, kind="ExternalInput").ap()
    y_d = nc.dram_tensor("y", [S, D], F32, kind="ExternalOutput").ap()
    gscr_d = nc.dram_tensor("gscr", [L, NSEG, 8, SEG], F32, kind="Internal").ap()

    es = ExitStack()
    with es:
        def sb(name, shape, dt):
            return es.enter_context(nc.sbuf_tensor(name, shape, dt))

        T = Trk(nc, es)
        T.add_eng("pe", nc.tensor)
        T.add_eng("act", nc.scalar)
        T.add_eng("dve", nc.vector)
        T.add_eng("pool", nc.gpsimd)
        T.add_eng("sp", nc.sync, own=False)

        XF = sb("XF", [128, 8, S], F32)
        XB = sb("XB", [128, 8, S], BF16)
        RG = [sb(f"RG{i}", [128, 4096], BF16) for i in range(RING)]
        ARENA = sb("ARENA", [128, 16384], BF16)
        PS = [es.enter_context(nc.psum_tensor(f"PS{i}", [128, 512], F32)) for i in range(8)]

        def av(c0, n, b):
            return ARENA[:, c0:c0 + n].rearrange("p (a b) -> p a b", b=b)
        QT = av(0, 2048, 512)
        KT = av(2048, 2048, 512)
        KTOK = ARENA[:, 4096:6144].rearrange("p (c h d) -> p c h d", h=4, d=128)
        VP = ARENA[:, 6144:8192].rearrange("p (c h d) -> p c h d", h=4, d=128)
        YM = av(8192, 2048, 512)
        YP = av(10240, 2048, 512)
        PB = av(12288, 2112, 528)
        UQ = [ARENA[:, 14400:14916], ARENA[:, 14916:15432]]
        H = ARENA[:, :].rearrange("p (j t) -> p j t", t=S)
        XS = [ARENA[:, i * 2048:(i + 1) * 2048].bitcast(F32) for i in range(4)]

        AB_NAMES = {"QT", "KT", "KTOK", "VP", "YM", "YP", "PB", "UQ"}
        def region_of(k):
            n = k[0]
            if n in AB_NAMES:
                return ("AR", "AB")
            if n == "H":
                return ("AR", "FFN")
            if n == "XS":
                return ("AR", "IO")
            return None
        T.region_of = region_of

        RBt = [sb(f"RB{i}", [128, 512], BF16)[:, :] for i in range(3)]
        RSQt = [sb(f"RSQ{i}", [128, 512], BF16)[:, :] for i in range(3)]
        MEANt = [sb(f"MEAN{i}", [128, 512], F32)[:, :] for i in range(2)]
        RSTDt = [sb(f"RSTD{i}", [128, 512], F32)[:, :] for i in range(2)]
        DG = [sb(f"DG{i}", [128, 4, 128], BF16) for i in range(2)]
        G8 = sb("G8", [8, SEG], F32)
        GI = sb("GI", [16, 128], F32)
        GF = sb("GF", [16, 128], F32)
        NA = sb("NA", [16, 128], F32)
        GM = sb("GM", [16, 4], F32)
        ROW = sb("ROW", [1, 32], F32)
        MFULL = sb("MFULL", [1, 4, 5], F32)
        MC = sb("MC", [1, 16], F32)
        TR = sb("TR", [1, 16], F32)
        SCR = sb("SCR", [1, 16], F32)
        NMC = sb("NMC", [16, 2], F32)
        SCB = sb("SCB", [128, 16], F32)
        ET = sb("ET", [128, 16], F32)
        ETB = sb("ETB", [128, 16], BF16)
        THRT = sb("THRT", [128, 16], F32)
        C = sb("C", [128, 4, 129], F32)
        CB = sb("CB", [128, 4, 129], BF16)
        ATs = [sb(f"AT{i}", [128, 4, 128], BF16) for i in range(2)]
        HN = sb("HN", [128, 4, 4, 128], BF16)
        SQs = [sb(f"SQ{i}", [128, 4, 128], F32) for i in range(2)]
        SM = sb("SM", [128, 12, 16], F32)
        HALO = sb("HALO", [128, 8, 3], BF16)
        PHALO = sb("PHALO", [128, 4, 16], BF16)
        CS = sb("CS", [128, 16], F32)
        DFB = sb("DFB", [128, 16], BF16)
        HR = [sb(f"HR{i}", [128, 512], BF16) for i in range(2)]
        IDENTB = sb("IDENTB", [128, 128], BF16)
        IDENTF = sb("IDENTF", [128, 128], F32)
        MASK = sb("MASK", [128, 128], BF16)
        ONESM = sb("ONESM", [128, 128], BF16)
        ONESF = sb("ONESF", [128, 128], F32)
        INVC = sb("INVC", [128, 16], F32)
        PRMS = sb("PRMS", [128, 128], F32)
        PRMT = sb("PRMT", [128, 128], F32)
        PRMA = sb("PRMA", [128, 128], F32)
        PRM2T = sb("PRM2T", [128, 32], F32)
        WCVT = sb("WCVT", [128, 128], F32)
        BG = sb("BG", [8, 4], F32)
        WPF = sb("WPF", [128, 4, 128], F32)
        WPA = sb("WPA", [128, 4, 128], BF16)
        WPBt = sb("WPBt", [128, 4, 128], BF16)
        WPC = sb("WPC", [128, 4, 128], BF16)
        GW = [sb(f"GW{i}", [128, 8, 8], BF16) for i in range(L)]

        d_x = [T.dsem(f"x{i}") for i in range(4)]
        d_y = [T.dsem(f"y{i}") for i in range(4)]
        d_prm = T.dsem("prm")
        d_bg = T.dsem("bg")
        d_g = [T.dsem("g0"), T.dsem("g1"), T.dsem("g2")]
        d_gw = T.dsem("gw")
        d_wp = T.dsem("wp")
        d_w = [T.dsem(f"w{i}") for i in range(RING)]

        bank_ctr = [0]
        reserved = set()

        def nb():
            while True:
                b = bank_ctr[0] % 8
                bank_ctr[0] += 1
                if b not in reserved:
                    return b

        def psk(b, c0=0, c1=512):
            return [("PS", b)]

        def psbf(b):
            return PS[b][:, 0:256].bitcast(BF16)

        def mm(out, lhsT, rhs, start, stop, R, W, inc=None):
            if inc is None:
                inc = stop
            T.op("pe", lambda e: e.matmul(out, lhsT=lhsT, rhs=rhs, start=start, stop=stop), R=R, W=W, inc=inc)

        def tp(out, in_, ident, R, W, inc=True):
            T.op("pe", lambda e: e.transpose(out=out, in_=in_, identity=ident), R=R, W=W, inc=inc)

        def act(out, in_, func, R, W, bias=None, scale=None):
            kw = {}
            if bias is not None:
                kw["bias"] = bias
            if scale is not None:
                kw["scale"] = scale
            T.op("act", lambda e: e.activation(out=out, in_=in_, func=func, **kw), R=R, W=W)

        def dve(fn, R, W):
            T.op("dve", fn, R=R, W=W)

        plan = []
        for l in range(L):
            for s in range(NSEG):
                for kind, c0 in (("q", 0), ("k", 512), ("o", 1544), ("v", 1024), ("p", 2056)):
                    plan.append((kind, w_in_d[l][:, c0:c0 + 512].rearrange("(k p) n -> p k n", p=128), "kn"))
                for i in range(2):
                    plan.append((f"wo{i}", w_out_d[l][:, i * 512:(i + 1) * 512].rearrange("(k p) n -> p k n", p=128), "kn"))
            for q in range(4):
                for i in range(2):
                    c0 = q * 1024 + i * 512
                    plan.append((f"w1{i}", w_ff1_d[l][:, c0:c0 + 512].rearrange("(k p) n -> p k n", p=128), "kn"))
                for i in range(2):
                    r0 = q * 1024 + i * 512
                    plan.append((f"w2{i}", w_ff2_d[l][r0:r0 + 512, :].rearrange("(j p) n -> p j n", p=128), "jn"))
        ring = dict(acq=0, rel=0)

        def slot_view(slot, lay):
            if lay == "kn":
                return RG[slot][:, :].rearrange("p (k n) -> p k n", n=512)
            return RG[slot][:, :].rearrange("p (j n) -> p j n", n=1024)

        def issue_fill(i):
            kind, src, lay = plan[i]
            slot = i % RING
            T.dma("pool", d_w[slot], slot_view(slot, lay), src, R=(), W=[("W", slot)])

        def acquire(kind):
            i = ring["acq"]
            assert plan[i][0] == kind, (plan[i][0], kind)
            ring["acq"] += 1
            slot = i % RING
            return slot_view(slot, plan[i][2]), ("W", slot)

        def release():
            i = ring["rel"]
            ring["rel"] += 1
            if i + RING < len(plan):
                issue_fill(i + RING)

        T.op("pool", lambda e: e.memset(ONESF[:], 1.0), W=[("ONESF",)])
        T.op("pool", lambda e: e.memset(ONESM[:], 1.0 / 1024.0), W=[("ONESM",)])
        T.op("pool", lambda e: e.affine_select(out=IDENTF[:], in_=ONESF[:], pattern=[[1, 128]], compare_op=ALU.is_equal,
                                               fill=0.0, base=0, channel_multiplier=-1), R=[("ONESF",)], W=[("IDENTF",)])
        T.op("pool", lambda e: e.affine_select(out=IDENTB[:], in_=ONESF[:], pattern=[[1, 128]], compare_op=ALU.is_equal,
                                               fill=0.0, base=0, channel_multiplier=-1), R=[("ONESF",)], W=[("IDENTB",)])
        T.op("pool", lambda e: e.affine_select(out=MASK[:], in_=ONESF[:], pattern=[[1, 128]], compare_op=ALU.is_ge,
                                               fill=0.0, base=0, channel_multiplier=-1), R=[("ONESF",)], W=[("MASK",)])
        for t in range(16):
            T.op("pool", lambda e, t=t: e.memset(INVC[:, t:t + 1], 1.0 / (t + 1)), W=[("INVC",)])

        d_gws = [T.dsem(f"gw{i}") for i in range(L)]
        def _gw(l_):
            with nc.allow_non_contiguous_dma(reason="gate weights, 32B rows"):
                T.dma("pool", d_gws[l_], GW[l_][:, :, :], w_in_d[l_][:, 1536:1544].rearrange("(k p) n -> p k n", p=128),
                      R=(), W=[("GW", l_)])
        _gw(0)
        for i in range(min(RING, len(plan))):
            if not os.environ.get("SKIP_PREFETCH"):
                issue_fill(i)
        for l_ in range(1, L):
            _gw(l_)

        def load_T(rows_list, dst, ncols, key):
            r = 0
            for src in rows_list:
                n = src.shape[0]
                T.dma("sp", d_prm, PRMS[r:r + n, :], src, R=(), W=[("PRMS",)])
                r += n
            b = nb()
            tp(PS[b][:, 0:r], PRMS[0:r, :], IDENTF[0:r, 0:r], R=[("PRMS",), ("IDENTF",)], W=psk(b, 0, r))
            dve(lambda e: e.tensor_copy(out=dst[:, 0:r], in_=PS[b][:, 0:r]), R=psk(b, 0, r), W=[key])

        if os.environ.get("SKIP_PARAMS"):
            load_T = lambda *a, **k: None
        load_T([a.rearrange("l (k c) -> (l k) c", c=128) for a in ln_d], PRMT, 128, ("PRMT",))
        dve(lambda e: e.tensor_scalar(out=PRMA[:, 0:32 * L], in0=PRMT[:, 0:32 * L], scalar1=ALPHA, scalar2=None, op0=ALU.mult),
            R=[("PRMT",)], W=[("PRMA",)])
        load_T([hn_g_d.rearrange("l (h c) -> (l h) c", c=128), pool_scale_d.rearrange("l (h c) -> (l h) c", c=128)],
               PRM2T, 32, ("PRM2T",))
        load_T([w_conv_d.rearrange("l j (k c) -> (l j k) c", c=128)], WCVT, 128, ("WCVT",))
        with nc.allow_non_contiguous_dma(reason="tiny bias"):
          if not os.environ.get("SKIP_BG"):
            T.dma("sp", d_bg, BG[0:8, 0:L], b_gate_d.rearrange("l g -> g l"), R=(), W=[("BG",)])

        def lncol(arr, l, k):
            return arr * 8 * L + l * 8 + k

        XM = int(os.environ.get('XSMOD', 4))
        for tt in range(int(os.environ.get('NX', 16))):
            xs = XS[tt % XM]
            T.dma("sp", d_x[tt % XM], xs, x_d[tt * 128:(tt + 1) * 128, :], R=(), W=[("XS", tt % XM)])
            for dg in range(2):
                b = nb()
                for di in range(4):
                    d = dg * 4 + di
                    tp(PS[b][:, di * 128:(di + 1) * 128], xs[:, d * 128:(d + 1) * 128], IDENTF[:],
                       R=[("XS", tt % XM), ("IDENTF",)], W=psk(b, di * 128, di * 128 + 128), inc=(di == 3))
                pv = PS[b][:, :].rearrange("p (a b) -> p a b", b=128)
                tb = tt // 4
                T.op("act", lambda e, pv=pv, dg=dg, tt=tt: e.mul(out=XF[:, dg * 4:dg * 4 + 4, tt * 128:(tt + 1) * 128], in_=pv, mul=ALPHA),
                     R=psk(b), W=[("XF", d, tb) for d in range(dg * 4, dg * 4 + 4)])
                dve(lambda e, pv=pv, dg=dg, tt=tt: e.tensor_copy(out=XB[:, dg * 4:dg * 4 + 4, tt * 128:(tt + 1) * 128], in_=pv),
                    R=psk(b), W=[("XB", d, tb) for d in range(dg * 4, dg * 4 + 4)])

        gate_tails = {}

        def phase_gates(l, s):
            cols = slice(s * SEG, (s + 1) * SEG)
            gw = GW[l]
            if s == 0:
                dve(lambda e: e.memset(MFULL[0:1, :, :], 0.0), R=(), W=[("MFULL",)])
            b = nb()
            for k in range(8):
                mm(PS[b][0:8, 0:512], gw[:, k, :], XB[:, k, cols], k == 0, k == 7,
                   R=[("GW", l), ("XB", k, s)], W=psk(b))
            act(G8[0:8, :], PS[b][0:8, 0:512], AF.Identity, R=psk(b) + [("BG",)], W=[("G8",)], bias=BG[0:8, l:l + 1])
            T.dma("sp", d_g[0], gscr_d[l, s], G8[0:8, :], R=[("G8",)], W=[("gscr",)])
            T.dma("sp", d_g[1], GI[0:16, :], gscr_d[l, s, 0:4, :].rearrange("g (c t) -> (g c) t", t=128), R=[("gscr",)], W=[("GI",)])
            T.dma("sp", d_g[2], GF[0:16, :], gscr_d[l, s, 4:8, :].rearrange("g (c t) -> (g c) t", t=128), R=[("gscr",)], W=[("GF",)])
            def t0():
                act(GF[:, :], GF[:, :], AF.Exp, R=[("GF",)], W=[("GF",)], scale=-1.0)
                act(GF[:, :], GF[:, :], AF.Ln, R=[("GF",)], W=[("GF",)], bias=1.0)
                dve(lambda e: e.tensor_tensor_scan(out=NA[:, :], data0=ONESF[0:16, :], data1=GF[:, :], initial=0.0,
                                                   op0=ALU.mult, op1=ALU.add), R=[("GF",), ("ONESF",)], W=[("NA",)])
                dve(lambda e: e.tensor_tensor(out=GI[:, :], in0=GI[:, :], in1=NA[:, :], op=ALU.add), R=[("GI",), ("NA",)], W=[("GI",)])
                dve(lambda e: e.tensor_reduce(out=GM[:, 2:3], in_=GI[:, :], axis=AX.X, op=ALU.max), R=[("GI",)], W=[("GM", 2)])
                dve(lambda e: e.tensor_scalar(out=GM[:, 0:1], in0=NA[:, 127:128], scalar1=-1.0, scalar2=None, op0=ALU.mult),
                    R=[("NA",)], W=[("GM", 0)])
                dve(lambda e: e.tensor_tensor(out=GM[:, 1:2], in0=GM[:, 2:3], in1=GM[:, 0:1], op=ALU.add),
                    R=[("GM", 2), ("GM", 0)], W=[("GM", 1)])

            def t1():
                b2 = nb()
                tp(PS[b2][0:1, 0:16], GM[0:16, 0:1], IDENTF[0:16, 0:16], R=[("GM", 0), ("IDENTF",)], W=psk(b2, 0, 16), inc=False)
                tp(PS[b2][0:1, 16:32], GM[0:16, 1:2], IDENTF[0:16, 0:16], R=[("GM", 1), ("IDENTF",)], W=psk(b2, 16, 32))
                dve(lambda e: e.tensor_copy(out=ROW[0:1, 0:32], in_=PS[b2][0:1, 0:32]), R=psk(b2, 0, 32), W=[("ROW",)])
                for h in range(4):
                    dve(lambda e, h=h: e.tensor_tensor_scan(out=MFULL[0:1, h, 1:5], data0=ROW[0:1, h * 4:(h + 1) * 4],
                                                            data1=ROW[0:1, 16 + h * 4:16 + (h + 1) * 4], initial=MFULL[0:1, h, 0:1],
                                                            op0=ALU.add, op1=ALU.max), R=[("ROW",), ("MFULL",)], W=[("MFULL",)])
                mc3 = MC[0:1, :].rearrange("p (h c) -> p h c", c=4)
                tr3 = TR[0:1, :].rearrange("p (h c) -> p h c", c=4)
                row3 = ROW[0:1, 0:16].rearrange("p (h c) -> p h c", c=4)
                dve(lambda e: e.tensor_tensor(out=mc3, in0=MFULL[0:1, :, 1:5], in1=row3, op=ALU.subtract), R=[("MFULL",), ("ROW",)], W=[("MC",)])
                dve(lambda e: e.tensor_tensor(out=tr3, in0=MFULL[0:1, :, 0:4], in1=mc3, op=ALU.subtract), R=[("MFULL",), ("MC",)], W=[("TR",)])
                act(SCR[0:1, :], TR[0:1, :], AF.Exp, R=[("TR",)], W=[("SCR",)])
                dve(lambda e: e.tensor_copy(out=MFULL[0:1, :, 0:1], in_=MFULL[0:1, :, 4:5]), R=[("MFULL",)], W=[("MFULL",)])

            def t2():
                b3 = nb()
                mm(PS[b3][:, 0:16], ONESF[0:1, 0:128], SCR[0:1, 0:16], True, True, R=[("ONESF",), ("SCR",)], W=psk(b3, 0, 16))
                dve(lambda e: e.tensor_copy(out=SCB[:, :], in_=PS[b3][:, 0:16]), R=psk(b3, 0, 16), W=[("SCB",)])
                b4 = nb()
                tp(PS[b4][0:16, 0:1], MC[0:1, 0:16], IDENTF[0:1, 0:1], R=[("MC",), ("IDENTF",)], W=psk(b4, 0, 1))
                dve(lambda e: e.tensor_scalar(out=NMC[:, 0:1], in0=PS[b4][0:16, 0:1], scalar1=-1.0, scalar2=None, op0=ALU.mult),
                    R=psk(b4, 0, 1), W=[("NMC", 0)])
                dve(lambda e: e.tensor_scalar(out=NMC[:, 1:2], in0=PS[b4][0:16, 0:1], scalar1=-1.0, scalar2=LNK, op0=ALU.mult, op1=ALU.add),
                    R=psk(b4, 0, 1), W=[("NMC", 1)])
                act(GI[:, :], GI[:, :], AF.Exp, R=[("GI",), ("NMC", 1)], W=[("GI",)], bias=NMC[:, 1:2])
                act(NA[:, :], NA[:, :], AF.Exp, R=[("NA",), ("NMC", 0)], W=[("NA",)], bias=NMC[:, 0:1])

            def t3():
                b5 = nb()
                tp(PS[b5][:, 0:16], GI[0:16, :], IDENTF[0:16, 0:16], R=[("GI",), ("IDENTF",)], W=psk(b5, 0, 16), inc=False)
                tp(PS[b5][:, 16:32], NA[0:16, :], IDENTF[0:16, 0:16], R=[("NA",), ("IDENTF",)], W=psk(b5, 16, 32))
                dve(lambda e: e.tensor_copy(out=ET[:, :], in_=PS[b5][:, 0:16]), R=psk(b5, 0, 16), W=[("ET",)])
                dve(lambda e: e.tensor_copy(out=ETB[:, :], in_=PS[b5][:, 0:16]), R=psk(b5, 0, 16), W=[("ETB",)])
                dve(lambda e: e.tensor_copy(out=THRT[:, :], in_=PS[b5][:, 16:32]), R=psk(b5, 16, 32), W=[("THRT",)])


            gate_tails[(l, s)] = [t0, t1, t2, t3]

        def phase_qk(l, s, which):
            cols = slice(s * SEG, (s + 1) * SEG)
            W, wk = acquire(which)
            DST, dname = (QT, "QT") if which == "q" else (KT, "KT")
            inject = gate_tails.get((l, s), [])
            sched = {}

            def inj(hm):
                for _ in range(sched.get(hm, 0)):
                    if inject:
                        inject.pop(0)()

            def main(h):
                tile = (0 if which == "q" else 4) + h
                uq = UQ[tile % 2]
                uk = ("UQ", tile % 2)
                b = nb()
                for k in range(8):
                    mm(PS[b][:, :], W[:, k, h * 128:(h + 1) * 128], XB[:, k, cols], k == 0, k == 7, R=[wk, ("XB", k, s)], W=psk(b))
                if s == 0:
                    dve(lambda e, uq=uq: e.memset(uq[:, 0:3], 0.0), R=(), W=[uk])
                else:
                    dve(lambda e, uq=uq, tile=tile: e.tensor_copy(out=uq[:, 0:3], in_=HALO[:, tile, :]), R=[("HALO", tile)], W=[uk])
                act(uq[:, 3:515], PS[b][:, :], AF.Identity, R=psk(b), W=[uk])
                if s < NSEG - 1:
                    dve(lambda e, uq=uq, tile=tile: e.tensor_copy(out=HALO[:, tile, :], in_=uq[:, 512:515]), R=[uk], W=[("HALO", tile)])
                dg = DG[tile % 2]
                for j in range(4):
                    c = l * 32 + j * 8 + tile
                    dve(lambda e, dg=dg, j=j, c=c: e.tensor_scalar(out=dg[:, j, :], in0=IDENTB[:, :], scalar1=WCVT[:, c:c + 1],
                                                                   scalar2=None, op0=ALU.mult),
                        R=[("IDENTB",), ("WCVT",)], W=[("DG", tile % 2, j)])

            def conv(h):
                tile = (0 if which == "q" else 4) + h
                uq = UQ[tile % 2]
                uk = ("UQ", tile % 2)
                dg = DG[tile % 2]
                b2 = nb()
                for j in range(4):
                    mm(PS[b2][:, :], dg[:, j, :], uq[:, j:j + 512], j == 0, j == 3, R=[("DG", tile % 2, j), uk], W=psk(b2))
                act(DST[:, h, :], PS[b2][:, :], AF.Silu, R=psk(b2), W=[(dname, h)])

            main(0)
            inj(0)
            for h in range(4):
                if h + 1 < 4:
                    main(h + 1)
                    inj(h + 1)
                conv(h)
            release()

        def phase_ktok(l, s):
            for h in range(4):
                b3 = nb()
                pb = psbf(b3)
                for c in range(4):
                    tp(pb[:, c * 128:(c + 1) * 128], KT[:, h, c * 128:(c + 1) * 128], IDENTB[:, :],
                       R=[("KT", h), ("IDENTB",)], W=psk(b3, 0, 256), inc=(c == 3))
                dve(lambda e, h=h, pb=pb: e.tensor_copy(out=KTOK[:, :, h, :], in_=pb[:, :].rearrange("p (c d) -> p c d", d=128)),
                    R=psk(b3, 0, 256), W=[("KTOK", h)])

        def phase_o(l, s):
            cols = slice(s * SEG, (s + 1) * SEG)
            W, wk = acquire("o")
            for h in range(4):
                b = nb()
                for k in range(8):
                    mm(PS[b][:, :], W[:, k, h * 128:(h + 1) * 128], XB[:, k, cols], k == 0, k == 7, R=[wk, ("XB", k, s)], W=psk(b))
                act(YM[:, h, :], PS[b][:, :], AF.Sigmoid, R=psk(b), W=[("YM", h)])
                if h == 3:
                    tl = gate_tails.get((l, s), [])
                    if tl:
                        tl.pop(0)()

            release()
            phase_ktok(l, s)

        pool_w = {}

        def pool_prep(l, s):
            if s == 0:
                T.dma("sp", d_wp, WPF[:, :, :], w_pool_d[l].rearrange("g c d -> c g d"), R=(), W=[("WPF",)])
                for g in range(4):
                    win = 2 ** (g + 1)
                    dve(lambda e, g=g, win=win: e.tensor_scalar(out=WPA[:, g, :], in0=WPF[:, g, :], scalar1=(1.0 / win - 1.0),
                                                                 scalar2=None, op0=ALU.mult), R=[("WPF",)], W=[("WPA", g)])
                    dve(lambda e, g=g, win=win: e.tensor_scalar(out=WPBt[:, g, :], in0=WPF[:, g, :], scalar1=1.0 / win,
                                                                 scalar2=None, op0=ALU.mult), R=[("WPF",)], W=[("WPB", g)])
                dve(lambda e: e.tensor_copy(out=WPC[:, :, :], in_=WPF[:, :, :]), R=[("WPF",)], W=[("WPC",)])

        def pool_inproj(l, s, g, bank=None):
            cols = slice(s * SEG, (s + 1) * SEG)
            if g == 0:
                pool_w["w"] = acquire("p")
            W, wk = pool_w["w"]
            b = nb() if bank is None else bank
            for k in range(8):
                mm(PS[b][:, :], W[:, k, g * 128:(g + 1) * 128], XB[:, k, cols], k == 0, k == 7, R=[wk, ("XB", k, s)], W=psk(b))
            if s > 0:
                act(PB[:, g, 0:16], PHALO[:, g, :], AF.Identity, R=[("PHALO", g)], W=[("PB", g)])
            act(PB[:, g, 16:528], PS[b][:, :], AF.Identity, R=psk(b), W=[("PB", g)])
            if s < NSEG - 1:
                act(PHALO[:, g, :], PB[:, g, 512:528], AF.Identity, R=[("PB", g)], W=[("PHALO", g)])
            if g == 3:
                release()

        def pool_group(l, s, g):
            win = 2 ** (g + 1)
            c0 = win - 1 if s == 0 else 0
            b = nb()
            mm(PS[b][:, c0:512], WPA[:, g, :], PB[:, g, 16 + c0:528], True, False, R=[("WPA", g), ("PB", g)], W=psk(b), inc=False)
            for j in range(1, win):
                mm(PS[b][:, c0:512], WPBt[:, g, :], PB[:, g, 16 + c0 - j:528 - j], False, j == win - 1,
                   R=[("WPB", g), ("PB", g)], W=psk(b))
            if s == 0:
                dve(lambda e, g=g, c0=c0: e.tensor_tensor_scan(out=CS[:, 0:c0], data0=ONESF[:, 0:c0], data1=PB[:, g, 16:16 + c0],
                                                               initial=0.0, op0=ALU.mult, op1=ALU.add),
                    R=[("PB", g), ("ONESF",)], W=[("CS",)])
                dve(lambda e, c0=c0: e.tensor_tensor(out=CS[:, 0:c0], in0=CS[:, 0:c0], in1=INVC[:, 0:c0], op=ALU.mult),
                    R=[("CS",), ("INVC",)], W=[("CS",)])
                dve(lambda e, g=g, c0=c0: e.tensor_tensor(out=DFB[:, 0:c0], in0=CS[:, 0:c0], in1=PB[:, g, 16:16 + c0], op=ALU.subtract),
                    R=[("CS",), ("PB", g)], W=[("DFB",)])
                mm(PS[b][:, 0:c0], WPC[:, g, :], DFB[:, 0:c0], True, True, R=[("WPC",), ("DFB",)], W=psk(b))
            c = 4 * L + l * 4 + g
            act(YP[:, g, :], PS[b][:, :], AF.Identity, R=psk(b) + [("PRM2T",)], W=[("YP", g)], scale=PRM2T[:, c:c + 1])
            pop_side(1)

        def phase_v(l, s):
            while reserved:
                pop_side(1)
            W, wk = acquire("v")
            et3 = ET[:, :].rearrange("p (h c) -> p h c", c=4)
            tl = gate_tails.get((l, s), [])
            if len(tl) == 4:
                tl.pop(0)()
            banks = []
            for c in range(4):
                tt = s * 4 + c
                b = nb()
                banks.append(b)
                for k in range(8):
                    mm(PS[b][:, :], XB[:, k, tt * 128:(tt + 1) * 128], W[:, k, :], k == 0, k == 7, R=[wk, ("XB", k, s)], W=psk(b))
                if c >= 1 and tl:
                    tl.pop(0)()
            while tl:
                tl.pop(0)()
            for c in range(4):
                b = banks[c]
                pv = PS[b][:, :].rearrange("p (h d) -> p h d", d=128)
                dve(lambda e, c=c, pv=pv: e.tensor_tensor(out=VP[:, c, :, :], in0=pv,
                                                          in1=et3[:, :, c].unsqueeze(2).to_broadcast([128, 4, 128]), op=ALU.mult),
                    R=psk(b) + [("ET",)], W=[("VP", c)])
            release()

        def phase_mlstm(l, s):
            while reserved:
                pop_side(1)
            pool_prep(l, s)
            scb3 = SCB[:, :].rearrange("p (h c) -> p h c", c=4)
            bO = [0, 1, 2, 3]
            bX = 4
            rot = [5, 6, 7]
            pX = PS[bX]
            kX = [("PS", bX)]
            def smv(i):
                return SM[:, i, :]

            def smc(i, c):
                return SM[:, i, :].rearrange("p (h c) -> p h c", c=4)[:, :, c]
            DNA, DN, RDN, SUM_, SSQ, MEAN_, EX2, VAR, R2, TT, SS, NBB = range(12)

            def stats(c):
                dve(lambda e: e.tensor_reduce(out=smc(SUM_, c), in_=pOs[c], axis=AX.X, op=ALU.add), R=psk(bO[c]), W=[("SM", SUM_)])
                act(SQs[c % 2][:, :, :], pOs[c], AF.Square, R=psk(bO[c]), W=[("SQ", c % 2)])
                dve(lambda e: e.tensor_reduce(out=smc(SSQ, c), in_=SQs[c % 2][:, :, :], axis=AX.X, op=ALU.add), R=[("SQ", c % 2)], W=[("SM", SSQ)])

            ri = 0
            pOs = []
            for c in range(4):
                first = (s == 0 and c == 0)
                tc_ = slice(c * 128, (c + 1) * 128)
                bS = rot[ri % 3]
                bDC = rot[(ri + 1) % 3]
                ri += 2
                pS = PS[bS][:, :].rearrange("p (h d) -> p h d", d=128)
                pDC = PS[bDC][:, :].rearrange("p (h d) -> p h d", d=128)
                pO = PS[bO[c]][:, :].rearrange("p (h d) -> p h d", d=128)
                pOs.append(pO)
                at = ATs[c % 2]
                ak = ("AT", c % 2)
                for h in range(4):
                    mm(pS[:, h, :], KT[:, h, tc_], QT[:, h, tc_], True, True, R=[("KT", h), ("QT", h)], W=psk(bS), inc=(h == 3))
                dve(lambda e, pS=pS, at=at: e.tensor_tensor(out=at[:, :, :], in0=pS, in1=MASK[:, :].unsqueeze(1).to_broadcast([128, 4, 128]),
                                                            op=ALU.mult), R=psk(bS) + [("MASK",)], W=[ak])
                for h in range(4):
                    col = h * 4 + c
                    mm(pDC[:, h, :], KTOK[:, c, h, :], VP[:, c, h, :], True, True, R=[("KTOK", h), ("VP", c)], W=psk(bDC), inc=False)
                    mm(pX[:, 16 + h:17 + h], KTOK[:, c, h, :], ETB[:, col:col + 1], True, True, R=[("KTOK", h), ("ETB",)], W=kX, inc=(h == 3))
                if not first:
                    dve(lambda e, c=c: e.tensor_tensor(out=C[:, :, :], in0=C[:, :, :],
                                                       in1=scb3[:, :, c].unsqueeze(2).to_broadcast([128, 4, 129]), op=ALU.mult),
                        R=[("C",), ("SCB",)], W=[("C",)])
                    act(CB[:, :, :], C[:, :, :], AF.Identity, R=[("C",)], W=[("CB",)])
                if c >= 1:
                    stats(c - 1)
                for h in range(4):
                    col = h * 4 + c
                    mm(pO[:, h, :], at[:, h, :], VP[:, c, h, :], True, first, R=[ak, ("VP", c)], W=psk(bO[c]), inc=False)
                    if not first:
                        mm(pO[:, h, :], QT[:, h, tc_], CB[:, h, 0:128], False, True, R=[("QT", h), ("CB",)], W=psk(bO[c]), inc=False)
                    mm(pX[:, col:col + 1], at[:, h, :], ETB[:, col:col + 1], True, first, R=[ak, ("ETB",)], W=kX, inc=(first and h == 3))
                    if not first:
                        mm(pX[:, col:col + 1], QT[:, h, tc_], CB[:, h, 128:129], False, True, R=[("QT", h), ("CB",)], W=kX, inc=(h == 3))
                if first:
                    dve(lambda e, pDC=pDC: e.tensor_copy(out=C[:, :, 0:128], in_=pDC), R=psk(bDC), W=[("C",)])
                    dve(lambda e: e.tensor_copy(out=C[:, :, 128:129], in_=pX[:, 16:20].unsqueeze(2)), R=kX, W=[("C",)])
                else:
                    dve(lambda e, pDC=pDC: e.tensor_tensor(out=C[:, :, 0:128], in0=C[:, :, 0:128], in1=pDC, op=ALU.add),
                        R=psk(bDC) + [("C",)], W=[("C",)])
                    dve(lambda e: e.tensor_tensor(out=C[:, :, 128:129], in0=C[:, :, 128:129], in1=pX[:, 16:20].unsqueeze(2), op=ALU.add),
                        R=kX + [("C",)], W=[("C",)])
            reserved.update((0, 1, 2, 3, 4))
            dve(lambda e: e.tensor_tensor(out=smv(DNA), in0=pX[:, 0:16], in1=THRT[:, 0:16], op=ALU.max), R=kX + [("THRT",)], W=[("SM", DNA)])
            dve(lambda e: e.scalar_tensor_tensor(out=smv(DN), in0=pX[:, 0:16], scalar=-1.0, in1=smv(DNA), op0=ALU.mult, op1=ALU.max),
                R=kX + [("SM", DNA)], W=[("SM", DN)])
            dve(lambda e: e.reciprocal(out=smv(RDN), in_=smv(DN)), R=[("SM", DN)], W=[("SM", RDN)])
            stats(3)
            pool_inproj(l, s, 0, bank=5)
            pool_inproj(l, s, 1, bank=7)
            dve(lambda e: e.tensor_scalar(out=smv(MEAN_), in0=smv(SUM_), scalar1=1.0 / 128, scalar2=None, op0=ALU.mult),
                R=[("SM", SUM_)], W=[("SM", MEAN_)])
            dve(lambda e: e.tensor_tensor(out=smv(EX2), in0=smv(MEAN_), in1=smv(MEAN_), op=ALU.mult), R=[("SM", MEAN_)], W=[("SM", EX2)])
            dve(lambda e: e.scalar_tensor_tensor(out=smv(VAR), in0=smv(SSQ), scalar=1.0 / 128, in1=smv(EX2), op0=ALU.mult, op1=ALU.subtract),
                R=[("SM", SSQ), ("SM", EX2)], W=[("SM", VAR)])
            dve(lambda e: e.tensor_tensor(out=smv(R2), in0=smv(RDN), in1=smv(RDN), op=ALU.mult), R=[("SM", RDN)], W=[("SM", R2)])
            dve(lambda e: e.tensor_tensor(out=smv(TT), in0=smv(R2), in1=smv(VAR), op=ALU.mult), R=[("SM", R2), ("SM", VAR)], W=[("SM", TT)])
            act(smv(TT), smv(TT), AF.Ln, R=[("SM", TT)], W=[("SM", TT)], bias=LN_EPS)
            act(smv(TT), smv(TT), AF.Exp, R=[("SM", TT)], W=[("SM", TT)], scale=-0.5)
            dve(lambda e: e.tensor_tensor(out=smv(SS), in0=smv(TT), in1=smv(RDN), op=ALU.mult), R=[("SM", TT), ("SM", RDN)], W=[("SM", SS)])
            dve(lambda e: e.scalar_tensor_tensor(out=smv(NBB), in0=smv(MEAN_), scalar=-1.0, in1=smv(SS), op0=ALU.mult, op1=ALU.mult),
                R=[("SM", MEAN_), ("SM", SS)], W=[("SM", NBB)])
            for c in (0, 1, 2, 3):
                for h in range(4):
                    col = h * 4 + c
                    if c < 1:
                        act(HN[:, c, h, :], pOs[c][:, h, :], AF.Identity, R=psk(bO[c]) + [("SM", SS), ("SM", NBB)], W=[("HN", c)],
                            scale=SM[:, SS, col:col + 1], bias=SM[:, NBB, col:col + 1])
                    else:
                        dve(lambda e, c=c, h=h, col=col: e.tensor_scalar(out=HN[:, c, h, :], in0=pOs[c][:, h, :], scalar1=SM[:, SS, col:col + 1],
                                                                         scalar2=SM[:, NBB, col:col + 1], op0=ALU.mult, op1=ALU.add),
                            R=psk(bO[c]) + [("SM", SS), ("SM", NBB)], W=[("HN", c)])
            pool_inproj(l, s, 2, bank=6)
            pool_inproj(l, s, 3, bank=5)
            for bb in (0, 1, 2, 3, 4):
                reserved.discard(bb)
            pool_group(l, s, 0)
            pool_group(l, s, 1)
            for pr in range(2):
                bT = nb()
                pHT = PS[bT][:, :].bitcast(BF16).rearrange("p (c h d) -> p c h d", h=4, d=128)
                for cc in range(2):
                    for h in range(4):
                        tp(pHT[:, cc, h, :], HN[:, 2 * pr + cc, h, :], IDENTB[:, :], R=[("HN", 2 * pr + cc), ("IDENTB",)], W=psk(bT),
                           inc=(cc == 1 and h == 3))
                for h in range(4):
                    cc_ = l * 4 + h
                    ymv = YM[:, h, pr * 256:(pr + 1) * 256].rearrange("p (c d) -> p c d", d=128)
                    dve(lambda e, h=h, cc_=cc_, ymv=ymv, pHT=pHT: e.scalar_tensor_tensor(out=ymv, in0=pHT[:, :, h, :], scalar=PRM2T[:, cc_:cc_ + 1],
                                                                                     in1=ymv, op0=ALU.mult, op1=ALU.mult),
                        R=psk(bT) + [("YM", h), ("PRM2T",)], W=[("YM", h)])

            pool_group(l, s, 2)
            pool_group(l, s, 3)

        def phase_outproj(l, s):
            cols = slice(s * SEG, (s + 1) * SEG)
            Ws = [acquire("wo0"), acquire("wo1")]
            for m in range(8):
                W, wk = Ws[m // 4]
                b = nb()
                for k in range(8):
                    rhs = YM[:, k, :] if k < 4 else YP[:, k - 4, :]
                    rk = ("YM", k) if k < 4 else ("YP", k - 4)
                    mm(PS[b][:, :], W[:, k, (m % 4) * 128:(m % 4 + 1) * 128], rhs, k == 0, k == 7, R=[wk, rk], W=psk(b))
                dve(lambda e, m=m, b=b: e.tensor_tensor(out=XF[:, m, cols], in0=XF[:, m, cols], in1=PS[b][:, :], op=ALU.add),
                    R=psk(b) + [("XF", m, s)], W=[("XF", m, s)])
                if m % 2 == 1:
                    pop_side(1)
                if m == 3:
                    release()
            release()

        ln_ctr = [0]

        def ln_a(l, which, tb, st, part):
            cols = slice(tb * SEG, (tb + 1) * SEG)
            if part == 0:
                st["j"] = ln_ctr[0] % 2
                ln_ctr[0] += 1
                st["b1"], st["b2"] = nb(), nb()
                reserved.update((st["b1"], st["b2"]))
            j, b1, b2 = st["j"], st["b1"], st["b2"]
            if part < 2:
                for d in range(part * 4, part * 4 + 4):
                    i = d % 3
                    act(RBt[i], XF[:, d, cols], AF.Identity, R=[("XF", d, tb)], W=[("RB", i)])
                    act(RSQt[i], XF[:, d, cols], AF.Square, R=[("XF", d, tb)], W=[("RSQ", i)])
                    mm(PS[b1][:, :], ONESM[:, :], RBt[i], d == 0, d == 7, R=[("ONESM",), ("RB", i)], W=psk(b1), inc=True)
                    mm(PS[b2][:, :], ONESM[:, :], RSQt[i], d == 0, d == 7, R=[("ONESM",), ("RSQ", i)], W=psk(b2), inc=True)
                return
            mean, rstd = MEANt[j], RSTDt[j]
            dve(lambda e: e.tensor_copy(out=mean, in_=PS[b1][:, :]), R=psk(b1), W=[("MEAN", j)])
            dve(lambda e: e.tensor_tensor(out=rstd, in0=mean, in1=mean, op=ALU.mult), R=[("MEAN", j)], W=[("RSTD", j)])
            dve(lambda e: e.tensor_tensor(out=rstd, in0=PS[b2][:, :], in1=rstd, op=ALU.subtract), R=psk(b2) + [("RSTD", j)], W=[("RSTD", j)])
            act(rstd, rstd, AF.Ln, R=[("RSTD", j)], W=[("RSTD", j)], bias=LN_EPS)
            act(rstd, rstd, AF.Exp, R=[("RSTD", j)], W=[("RSTD", j)], scale=-0.5)
            reserved.discard(b1)
            reserved.discard(b2)

        def ln_b(l, which, tb, j, d0, d1, scaled):
            ga, ba = (0, 1) if which == 1 else (2, 3)
            PA = PRMA if scaled else PRMT
            cols = slice(tb * SEG, (tb + 1) * SEG)
            mean, rstd = MEANt[j], RSTDt[j]
            for d in range(d0, d1):
                xf = XF[:, d, cols]
                dve(lambda e, xf=xf: e.tensor_tensor(out=xf, in0=xf, in1=mean, op=ALU.subtract), R=[("XF", d, tb), ("MEAN", j)], W=[("XF", d, tb)])
                dve(lambda e, xf=xf: e.tensor_tensor(out=xf, in0=xf, in1=rstd, op=ALU.mult), R=[("XF", d, tb), ("RSTD", j)], W=[("XF", d, tb)])
                cg, cb = lncol(ga, l, d), lncol(ba, l, d)
                act(XB[:, d, cols], xf, AF.Identity, R=[("XF", d, tb), ("PRMT",)], W=[("XB", d, tb)],
                    scale=PRMT[:, cg:cg + 1], bias=PRMT[:, cb:cb + 1])
                act(xf, xf, AF.Identity, R=[("XF", d, tb), ("PRMA",), ("PRMT",)], W=[("XF", d, tb)],
                    scale=PA[:, cg:cg + 1], bias=PA[:, cb:cb + 1])

        ffn_w = {}

        def ffn1(l, q, tb):
            if tb == 0:
                ffn_w["w1"] = [acquire("w10"), acquire("w11")]
            W1 = ffn_w["w1"]
            cols = slice(tb * SEG, (tb + 1) * SEG)
            for j in range(8):
                W, wk = W1[j // 4]
                b = nb()
                for k in range(8):
                    mm(PS[b][:, :], W[:, k, (j % 4) * 128:(j % 4 + 1) * 128], XB[:, k, cols], k == 0, k == 7,
                       R=[wk, ("XB", k, tb)], W=psk(b))
                hr = HR[j % 2]
                act(hr[:, :], PS[b][:, :], AF.Relu, R=psk(b), W=[("HR", j % 2)])
                dve(lambda e, hr=hr, j=j: e.tensor_tensor(out=H[:, j, cols], in0=hr[:, :], in1=hr[:, :], op=ALU.mult),
                    R=[("HR", j % 2)], W=[("H", j, tb)])
                if j % 2 == 1:
                    pop_side(1)
            if tb == 3:
                release()
                release()

        def ffn2(l, q, tb):
            if tb == 0:
                ffn_w["w2"] = [acquire("w20"), acquire("w21")]
            W2 = ffn_w["w2"]
            cols = slice(tb * SEG, (tb + 1) * SEG)
            for grp in ((0, 1, 2), (3, 4, 5), (6, 7)):
                banks = [nb() for _ in grp]
                for j in range(8):
                    W, wk = W2[j // 4]
                    for mi, m in enumerate(grp):
                        mm(PS[banks[mi]][:, :], W[:, j % 4, m * 128:(m + 1) * 128], H[:, j, cols], j == 0, j == 7,
                           R=[wk, ("H", j, tb)], W=psk(banks[mi]))
                for mi, m in enumerate(grp):
                    b = banks[mi]
                    dve(lambda e, m=m, b=b: e.tensor_tensor(out=XF[:, m, cols], in0=XF[:, m, cols], in1=PS[b][:, :], op=ALU.add),
                        R=psk(b) + [("XF", m, tb)], W=[("XF", m, tb)])
                pop_side(5 if q == 3 else 1)
            if tb == 3:
                release()
                release()

        from collections import deque
        side = deque()
        NOSIDE = bool(os.environ.get("NOSIDE"))

        def enqueue_ln(l, which, tb, scaled=True):
            tag = (l, which, tb)
            st = {}
            for part in range(3):
                side.append((tag, lambda part=part: ln_a(l, which, tb, st, part)))
            for d0 in range(0, 8, 2):
                side.append((tag, lambda d0=d0: ln_b(l, which, tb, st["j"], d0, d0 + 2, scaled)))
            if NOSIDE:
                drain(tag)

        def drain(tag=None):
            if tag is not None and not any(t == tag for t, _ in side):
                return
            while side:
                t, fn = side.popleft()
                fn()
                if tag is not None and not any(tt == tag for tt, _ in side):
                    break

        def pop_side(n=1):
            for _ in range(n):
                if side:
                    side.popleft()[1]()

        steps = []
        for l in range(L):
            for s in range(NSEG):
                need = (l - 1, 2, s) if l > 0 else None
                steps.append((need, lambda l=l, s=s: phase_gates(l, s)))
                steps.append((None, lambda l=l, s=s: phase_qk(l, s, "q")))
                steps.append((None, lambda l=l, s=s: phase_qk(l, s, "k")))
                steps.append((None, lambda l=l, s=s: phase_o(l, s)))
                steps.append((None, lambda l=l, s=s: phase_v(l, s)))
                steps.append((None, lambda l=l, s=s: phase_mlstm(l, s)))
                steps.append((None, lambda l=l, s=s: (phase_outproj(l, s), enqueue_ln(l, 1, s))))
            last_scaled = not (l == L - 1 and last_unscaled)
            for q in range(4):
                for tb in range(4):
                    steps.append(((l, 1, tb), lambda l=l, q=q, tb=tb: ffn1(l, q, tb)))
                for tb in range(4):
                    if q < 3:
                        steps.append((None, lambda l=l, q=q, tb=tb: ffn2(l, q, tb)))
                    else:
                        steps.append((None, lambda l=l, q=q, tb=tb, sc=last_scaled: (ffn2(l, q, tb), enqueue_ln(l, 2, tb, sc))))
        for i, (need, st) in enumerate(steps):
            if dbg is not None and i >= dbg:
                break
            if need is not None:
                drain(need)
            st()
        if dbg is not None:
            drain(None)

        for tt in range(int(os.environ.get('NOUT', 16))):
            xs = XS[tt % 4]
            tb = tt // 4
            if dbg is None and tt % 4 == 0:
                drain((L - 1, 2, tb))
            for dg in range(2):
                b = nb()
                for di in range(4):
                    d = dg * 4 + di
                    tp(PS[b][:, di * 128:(di + 1) * 128], XF[:, d, tt * 128:(tt + 1) * 128], IDENTF[:, :],
                       R=[("XF", d, tb), ("IDENTF",)], W=psk(b, di * 128, di * 128 + 128), inc=(di == 3))
                if dg == 0:
                    act(xs[:, 0:512], PS[b][:, :], AF.Identity, R=psk(b), W=[("XS", tt % 4)])
                else:
                    dve(lambda e, xs=xs, b=b: e.tensor_copy(out=xs[:, 512:1024], in_=PS[b][:, :]), R=psk(b), W=[("XS", tt % 4)])
            T.dma("sp", d_y[tt % 4], y_d[tt * 128:(tt + 1) * 128, :], xs, R=[("XS", tt % 4)], W=[("Y", tt)])
        drain(None)
        for dd in d_y:
            nc.sync.wait_ge(dd["sem"], dd["cnt"])
        build.stats = dict(n_wait=T.n_wait, cnt={k: v["cnt"] for k, v in T.E.items()})
    return nc


_NAMES = ["x", "w_in", "b_gate", "w_conv", "hn_g", "w_pool", "pool_scale", "w_out",
          "ln1_g", "ln1_b", "w_ff1", "w_ff2", "ln2_g", "ln2_b"]


def kernel(**inputs):
    arrs = {k: np.ascontiguousarray(np.asarray(inputs[k], dtype=np.float32)) for k in _NAMES}
    B = arrs["x"].shape[0]
    L = arrs["w_in"].shape[0]
    nc = build(depth=L)
    in_maps = []
    for b in range(B):
        m = {k: arrs[k] for k in _NAMES if k != "x"}
        m["x"] = np.ascontiguousarray(arrs["x"][b])
        in_maps.append(m)
    res = run_bass_kernel_spmd(nc, in_maps, core_ids=list(range(B)))
    return np.stack([res.results[b]["y"] for b in range(B)], axis=0).astype(np.float32)
```

```python
import math, os
from contextlib import ExitStack
import numpy as np
import concourse.bass as bass
import concourse.mybir as mybir
from concourse.bass_utils import run_bass_kernel_spmd

F32 = mybir.dt.float32
BF16 = mybir.dt.bfloat16
AF = mybir.ActivationFunctionType
ALU = mybir.AluOpType
AX = mybir.AxisListType

S = 2048
D = 1024
DIN = 2568
DFF = 4096
NSEG = 4
SEG = 512
ALPHA_FULL = (2.0 * 4) ** 0.25
LN_EPS = 1e-5
LNK = math.log(128.0 ** -0.5)
RING = 4


class Trk:
    def __init__(self, nc, es):
        self.nc, self.es = nc, es
        self.E = {}
        self.lw = {}
        self.rd = {}
        self.reg_owner = {}
        self.reg_ev = {}
        self.region_of = lambda k: None
        self.n_wait = 0
        self.snap = {}

    def add_eng(self, name, eng, own=True):
        sem = self.es.enter_context(self.nc.semaphore("s_" + name)) if own else None
        self.E[name] = dict(eng=eng, sem=sem, cnt=0, seen={}, id=name)

    def dsem(self, name):
        return dict(sem=self.es.enter_context(self.nc.semaphore("d_" + name)), cnt=0, id="d_" + name)

    def _deps(self, R, W):
        deps = {}

        def add(ev):
            if ev is None:
                return
            sid, sh, v = ev
            if sid not in deps or deps[sid][1] < v:
                deps[sid] = (sh, v)

        for k in R:
            add(self.lw.get(k))
        for k in W:
            add(self.lw.get(k))
            for sid, (sh, v) in self.rd.get(k, {}).items():
                add((sid, sh, v))
        for k in list(R) + list(W):
            rg = self.region_of(k)
            if rg is not None:
                reg, tag = rg
                if self.reg_owner.get(reg) != tag:
                    for sid, (sh, v) in self.reg_ev.get(reg, {}).items():
                        add((sid, sh, v))
        return deps

    def _wait(self, e, deps, en):
        for sid, (sh, v) in deps.items():
            if sid == en:
                if en == "pe" or v > e["cnt"] or os.environ.get("NO_OWN_WAIT"):
                    continue
            if e["seen"].get(sid, 0) >= v:
                continue
            e["eng"].wait_ge(sh, v)
            e["seen"][sid] = v
            self.n_wait += 1
            for k2, v2 in self.snap.get((sid, v), {}).items():
                if e["seen"].get(k2, 0) < v2:
                    e["seen"][k2] = v2

    def _record(self, ev, R, W, seen=None):
        sid, sh, v = ev
        if seen is not None:
            d0 = self.snap.setdefault((sid, v), {})
            for k2, v2 in seen.items():
                if d0.get(k2, 0) < v2:
                    d0[k2] = v2
        for k in W:
            self.lw[k] = ev
            self.rd[k] = {}
        for k in R:
            d = self.rd.setdefault(k, {})
            if sid not in d or d[sid][1] < v:
                d[sid] = (sh, v)
        for k in list(R) + list(W):
            rg = self.region_of(k)
            if rg is not None:
                reg, tag = rg
                if self.reg_owner.get(reg) != tag:
                    self.reg_owner[reg] = tag
                    self.reg_ev[reg] = {}
                d = self.reg_ev[reg]
                if sid not in d or d[sid][1] < v:
                    d[sid] = (sh, v)

    def op(self, en, fn, R=(), W=(), inc=True):
        e = self.E[en]
        W = list(W) + [k for k in R if k[0] == "PS" and k not in W]
        R = [k for k in R if k[0] != "PS"]
        self._wait(e, self._deps(R, W), en)
        ins = fn(e["eng"])
        if inc:
            ins.then_inc(e["sem"], 1)
            e["cnt"] += 1
            ev = (en, e["sem"], e["cnt"])
        else:
            ev = (en, e["sem"], e["cnt"] + 1)
        self._record(ev, R, W, seen=e["seen"])

    def dma(self, qn, ds, out, in_, R=(), W=(), **kw):
        e = self.E[qn]
        self._wait(e, self._deps(R, W), qn + "_q")
        e["eng"].dma_start(out=out, in_=in_, **kw).then_inc(ds["sem"], 16)
        ds["cnt"] += 16
        self._record((ds["id"], ds["sem"], ds["cnt"]), R, W, seen=e["seen"])

    def wait_all(self, qn, keys):
        e = self.E[qn]
        self._wait(e, self._deps(keys, ()), qn + "_q")


def build(depth=4, last_unscaled=True, dbg=None):
    L = depth
    ALPHA = ALPHA_FULL
    nc = bass.Bass("TRN2", target_bir_lowering=False)
    x_d = nc.dram_tensor("x", [S, D], F32, kind="ExternalInput").ap()
    w_in_d = nc.dram_tensor("w_in", [L, D, DIN], F32, kind="ExternalInput").ap()
    b_gate_d = nc.dram_tensor("b_gate", [L, 8], F32, kind="ExternalInput").ap()
    w_conv_d = nc.dram_tensor("w_conv", [L, 4, D], F32, kind="ExternalInput").ap()
    hn_g_d = nc.dram_tensor("hn_g", [L, 512], F32, kind="ExternalInput").ap()
    w_pool_d = nc.dram_tensor("w_pool", [L, 4, 128, 128], F32, kind="ExternalInput").ap()
    pool_scale_d = nc.dram_tensor("pool_scale", [L, 512], F32, kind="ExternalInput").ap()
    w_out_d = nc.dram_tensor("w_out", [L, D, D], F32, kind="ExternalInput").ap()
    ln_d = [nc.dram_tensor(n, [L, D], F32, kind="ExternalInput").ap() for n in ("ln1_g", "ln1_b", "ln2_g", "ln2_b")]
    w_ff1_d = nc.dram_tensor("w_ff1", [L, D, DFF], F32, kind="ExternalInput").ap()
    w_ff2_d = nc.dram_tensor("w_ff2", [L, DFF, D], F32, kind="ExternalInput").ap()
    y_d = nc.dram_tensor("y", [S, D], F32, kind="ExternalOutput").ap()
    gscr_d = nc.dram_tensor("gscr", [L, NSEG, 8, SEG], F32, kind="Internal").ap()

    es = ExitStack()
    with es:
        def sb(name, shape, dt):
            return es.enter_context(nc.sbuf_tensor(name, shape, dt))

        T = Trk(nc, es)
        T.add_eng("pe", nc.tensor)
        T.add_eng("act", nc.scalar)
        T.add_eng("dve", nc.vector)
        T.add_eng("pool", nc.gpsimd)
        T.add_eng("sp", nc.sync, own=False)

        XF = sb("XF", [128, 8, S], F32)
        XB = sb("XB", [128, 8, S], BF16)
        RG = [sb(f"RG{i}", [128, 4096], BF16) for i in range(RING)]
        ARENA = sb("ARENA", [128, 16384], BF16)
        PS = [es.enter_context(nc.psum_tensor(f"PS{i}", [128, 512], F32)) for i in range(8)]

        def av(c0, n, b):
            return ARENA[:, c0:c0 + n].rearrange("p (a b) -> p a b", b=b)
        QT = av(0, 2048, 512)
        KT = av(2048, 2048, 512)
        KTOK = ARENA[:, 4096:6144].rearrange("p (c h d) -> p c h d", h=4, d=128)
        VP = ARENA[:, 6144:8192].rearrange("p (c h d) -> p c h d", h=4, d=128)
        YM = av(8192, 2048, 512)
        YP = av(10240, 2048, 512)
        PB = av(12288, 2112, 528)
        UQ = [ARENA[:, 14400:14916], ARENA[:, 14916:15432]]
        H = ARENA[:, :].rearrange("p (j t) -> p j t", t=S)
        XS = [ARENA[:, i * 2048:(i + 1) * 2048].bitcast(F32) for i in range(4)]

        AB_NAMES = {"QT", "KT", "KTOK", "VP", "YM", "YP", "PB", "UQ"}
        def region_of(k):
            n = k[0]
            if n in AB_NAMES:
                return ("AR", "AB")
            if n == "H":
                return ("AR", "FFN")
            if n == "XS":
                return ("AR", "IO")
            return None
        T.region_of = region_of

        RBt = [sb(f"RB{i}", [128, 512], BF16)[:, :] for i in range(3)]
        RSQt = [sb(f"RSQ{i}", [128, 512], BF16)[:, :] for i in range(3)]
        MEANt = [sb(f"MEAN{i}", [128, 512], F32)[:, :] for i in range(2)]
        RSTDt = [sb(f"RSTD{i}", [128, 512], F32)[:, :] for i in range(2)]
        DG = [sb(f"DG{i}", [128, 4, 128], BF16) for i in range(2)]
        G8 = sb("G8", [8, SEG], F32)
        GI = sb("GI", [16, 128], F32)
        GF = sb("GF", [16, 128], F32)
        NA = sb("NA", [16, 128], F32)
        GM = sb("GM", [16, 4], F32)
        ROW = sb("ROW", [1, 32], F32)
        MFULL = sb("MFULL", [1, 4, 5], F32)
        MC = sb("MC", [1, 16], F32)
        TR = sb("TR", [1, 16], F32)
        SCR = sb("SCR", [1, 16], F32)
        NMC = sb("NMC", [16, 2], F32)
        SCB = sb("SCB", [128, 16], F32)
        ET = sb("ET", [128, 16], F32)
        ETB = sb("ETB", [128, 16], BF16)
        THRT = sb("THRT", [128, 16], F32)
        C = sb("C", [128, 4, 129], F32)
        CB = sb("CB", [128, 4, 129], BF16)
        ATs = [sb(f"AT{i}", [128, 4, 128], BF16) for i in range(2)]
        HN = sb("HN", [128, 4, 4, 128], BF16)
        SQs = [sb(f"SQ{i}", [128, 4, 128], F32) for i in range(2)]
        SM = sb("SM", [128, 12, 16], F32)
        HALO = sb("HALO", [128, 8, 3], BF16)
        PHALO = sb("PHALO", [128, 4, 16], BF16)
        CS = sb("CS", [128, 16], F32)
        DFB = sb("DFB", [128, 16], BF16)
        HR = [sb(f"HR{i}", [128, 512], BF16) for i in range(2)]
        IDENTB = sb("IDENTB", [128, 128], BF16)
        IDENTF = sb("IDENTF", [128, 128], F32)
        MASK = sb("MASK", [128, 128], BF16)
        ONESM = sb("ONESM", [128, 128], BF16)
        ONESF = sb("ONESF", [128, 128], F32)
        INVC = sb("INVC", [128, 16], F32)
        PRMS = sb("PRMS", [128, 128], F32)
        PRMT = sb("PRMT", [128, 128], F32)
        PRMA = sb("PRMA", [128, 128], F32)
        PRM2T = sb("PRM2T", [128, 32], F32)
        WCVT = sb("WCVT", [128, 128], F32)
        BG = sb("BG", [8, 4], F32)
        WPF = sb("WPF", [128, 4, 128], F32)
        WPA = sb("WPA", [128, 4, 128], BF16)
        WPBt = sb("WPBt", [128, 4, 128], BF16)
        WPC = sb("WPC", [128, 4, 128], BF16)
        GW = [sb(f"GW{i}", [128, 8, 8], BF16) for i in range(L)]

        d_x = [T.dsem(f"x{i}") for i in range(4)]
        d_y = [T.dsem(f"y{i}") for i in range(4)]
        d_prm = T.dsem("prm")
        d_bg = T.dsem("bg")
        d_g = [T.dsem("g0"), T.dsem("g1"), T.dsem("g2")]
        d_gw = T.dsem("gw")
        d_wp = T.dsem("wp")
        d_w = [T.dsem(f"w{i}") for i in range(RING)]

        bank_ctr = [0]
        reserved = set()

        def nb():
            while True:
                b = bank_ctr[0] % 8
                bank_ctr[0] += 1
                if b not in reserved:
                    return b

        def psk(b, c0=0, c1=512):
            return [("PS", b)]

        def psbf(b):
            return PS[b][:, 0:256].bitcast(BF16)

        def mm(out, lhsT, rhs, start, stop, R, W, inc=None):
            if inc is None:
                inc = stop
            T.op("pe", lambda e: e.matmul(out, lhsT=lhsT, rhs=rhs, start=start, stop=stop), R=R, W=W, inc=inc)

        def tp(out, in_, ident, R, W, inc=True):
            T.op("pe", lambda e: e.transpose(out=out, in_=in_, identity=ident), R=R, W=W, inc=inc)

        def act(out, in_, func, R, W, bias=None, scale=None):
            kw = {}
            if bias is not None:
                kw["bias"] = bias
            if scale is not None:
                kw["scale"] = scale
            T.op("act", lambda e: e.activation(out=out, in_=in_, func=func, **kw), R=R, W=W)

        def dve(fn, R, W):
            T.op("dve", fn, R=R, W=W)

        plan = []
        for l in range(L):
            for s in range(NSEG):
                for kind, c0 in (("q", 0), ("k", 512), ("o", 1544), ("v", 1024), ("p", 2056)):
                    plan.append((kind, w_in_d[l][:, c0:c0 + 512].rearrange("(k p) n -> p k n", p=128), "kn"))
                for i in range(2):
                    plan.append((f"wo{i}", w_out_d[l][:, i * 512:(i + 1) * 512].rearrange("(k p) n -> p k n", p=128), "kn"))
            for q in range(4):
                for i in range(2):
                    c0 = q * 1024 + i * 512
                    plan.append((f"w1{i}", w_ff1_d[l][:, c0:c0 + 512].rearrange("(k p) n -> p k n", p=128), "kn"))
                for i in range(2):
                    r0 = q * 1024 + i * 512
                    plan.append((f"w2{i}", w_ff2_d[l][r0:r0 + 512, :].rearrange("(j p) n -> p j n", p=128), "jn"))
        ring = dict(acq=0, rel=0)

        def slot_view(slot, lay):
            if lay == "kn":
                return RG[slot][:, :].rearrange("p (k n) -> p k n", n=512)
            return RG[slot][:, :].rearrange("p (j n) -> p j n", n=1024)

        def issue_fill(i):
            kind, src, lay = plan[i]
            slot = i % RING
            T.dma("pool", d_w[slot], slot_view(slot, lay), src, R=(), W=[("W", slot)])

        def acquire(kind):
            i = ring["acq"]
            assert plan[i][0] == kind, (plan[i][0], kind)
            ring["acq"] += 1
            slot = i % RING
            return slot_view(slot, plan[i][2]), ("W", slot)

        def release():
            i = ring["rel"]
            ring["rel"] += 1
            if i + RING < len(plan):
                issue_fill(i + RING)

        T.op("pool", lambda e: e.memset(ONESF[:], 1.0), W=[("ONESF",)])
        T.op("pool", lambda e: e.memset(ONESM[:], 1.0 / 1024.0), W=[("ONESM",)])
        T.op("pool", lambda e: e.affine_select(out=IDENTF[:], in_=ONESF[:], pattern=[[1, 128]], compare_op=ALU.is_equal,
                                               fill=0.0, base=0, channel_multiplier=-1), R=[("ONESF",)], W=[("IDENTF",)])
        T.op("pool", lambda e: e.affine_select(out=IDENTB[:], in_=ONESF[:], pattern=[[1, 128]], compare_op=ALU.is_equal,
                                               fill=0.0, base=0, channel_multiplier=-1), R=[("ONESF",)], W=[("IDENTB",)])
        T.op("pool", lambda e: e.affine_select(out=MASK[:], in_=ONESF[:], pattern=[[1, 128]], compare_op=ALU.is_ge,
                                               fill=0.0, base=0, channel_multiplier=-1), R=[("ONESF",)], W=[("MASK",)])
        for t in range(16):
            T.op("pool", lambda e, t=t: e.memset(INVC[:, t:t + 1], 1.0 / (t + 1)), W=[("INVC",)])

        d_gws = [T.dsem(f"gw{i}") for i in range(L)]
        def _gw(l_):
            with nc.allow_non_contiguous_dma(reason="gate weights, 32B rows"):
                T.dma("pool", d_gws[l_], GW[l_][:, :, :], w_in_d[l_][:, 1536:1544].rearrange("(k p) n -> p k n", p=128),
                      R=(), W=[("GW", l_)])
        _gw(0)
        for i in range(min(RING, len(plan))):
            if not os.environ.get("SKIP_PREFETCH"):
                issue_fill(i)
        for l_ in range(1, L):
            _gw(l_)

        def load_T(rows_list, dst, ncols, key):
            r = 0
            for src in rows_list:
                n = src.shape[0]
                T.dma("sp", d_prm, PRMS[r:r + n, :], src, R=(), W=[("PRMS",)])
                r += n
            b = nb()
            tp(PS[b][:, 0:r], PRMS[0:r, :], IDENTF[0:r, 0:r], R=[("PRMS",), ("IDENTF",)], W=psk(b, 0, r))
            dve(lambda e: e.tensor_copy(out=dst[:, 0:r], in_=PS[b][:, 0:r]), R=psk(b, 0, r), W=[key])

        if os.environ.get("SKIP_PARAMS"):
            load_T = lambda *a, **k: None
        load_T([a.rearrange("l (k c) -> (l k) c", c=128) for a in ln_d], PRMT, 128, ("PRMT",))
        dve(lambda e: e.tensor_scalar(out=PRMA[:, 0:32 * L], in0=PRMT[:, 0:32 * L], scalar1=ALPHA, scalar2=None, op0=ALU.mult),
            R=[("PRMT",)], W=[("PRMA",)])
        load_T([hn_g_d.rearrange("l (h c) -> (l h) c", c=128), pool_scale_d.rearrange("l (h c) -> (l h) c", c=128)],
               PRM2T, 32, ("PRM2T",))
        load_T([w_conv_d.rearrange("l j (k c) -> (l j k) c", c=128)], WCVT, 128, ("WCVT",))
        with nc.allow_non_contiguous_dma(reason="tiny bias"):
          if not os.environ.get("SKIP_BG"):
            T.dma("sp", d_bg, BG[0:8, 0:L], b_gate_d.rearrange("l g -> g l"), R=(), W=[("BG",)])

        def lncol(arr, l, k):
            return arr * 8 * L + l * 8 + k

        XM = int(os.environ.get('XSMOD', 4))
        for tt in range(int(os.environ.get('NX', 16))):
            xs = XS[tt % XM]
            T.dma("sp", d_x[tt % XM], xs, x_d[tt * 128:(tt + 1) * 128, :], R=(), W=[("XS", tt % XM)])
            for dg in range(2):
                b = nb()
                for di in range(4):
                    d = dg * 4 + di
                    tp(PS[b][:, di * 128:(di + 1) * 128], xs[:, d * 128:(d + 1) * 128], IDENTF[:],
                       R=[("XS", tt % XM), ("IDENTF",)], W=psk(b, di * 128, di * 128 + 128), inc=(di == 3))
                pv = PS[b][:, :].rearrange("p (a b) -> p a b", b=128)
                tb = tt // 4
                T.op("act", lambda e, pv=pv, dg=dg, tt=tt: e.mul(out=XF[:, dg * 4:dg * 4 + 4, tt * 128:(tt + 1) * 128], in_=pv, mul=ALPHA),
                     R=psk(b), W=[("XF", d, tb) for d in range(dg * 4, dg * 4 + 4)])
                dve(lambda e, pv=pv, dg=dg, tt=tt: e.tensor_copy(out=XB[:, dg * 4:dg * 4 + 4, tt * 128:(tt + 1) * 128], in_=pv),
                    R=psk(b), W=[("XB", d, tb) for d in range(dg * 4, dg * 4 + 4)])

        gate_tails = {}

        def phase_gates(l, s):
            cols = slice(s * SEG, (s + 1) * SEG)
            gw = GW[l]
            if s == 0:
                dve(lambda e: e.memset(MFULL[0:1, :, :], 0.0), R=(), W=[("MFULL",)])
            b = nb()
            for k in range(8):
                mm(PS[b][0:8, 0:512], gw[:, k, :], XB[:, k, cols], k == 0, k == 7,
                   R=[("GW", l), ("XB", k, s)], W=psk(b))
            act(G8[0:8, :], PS[b][0:8, 0:512], AF.Identity, R=psk(b) + [("BG",)], W=[("G8",)], bias=BG[0:8, l:l + 1])
            T.dma("sp", d_g[0], gscr_d[l, s], G8[0:8, :], R=[("G8",)], W=[("gscr",)])
            T.dma("sp", d_g[1], GI[0:16, :], gscr_d[l, s, 0:4, :].rearrange("g (c t) -> (g c) t", t=128), R=[("gscr",)], W=[("GI",)])
            T.dma("sp", d_g[2], GF[0:16, :], gscr_d[l, s, 4:8, :].rearrange("g (c t) -> (g c) t", t=128), R=[("gscr",)], W=[("GF",)])
            def t0():
                act(GF[:, :], GF[:, :], AF.Exp, R=[("GF",)], W=[("GF",)], scale=-1.0)
                act(GF[:, :], GF[:, :], AF.Ln, R=[("GF",)], W=[("GF",)], bias=1.0)
                dve(lambda e: e.tensor_tensor_scan(out=NA[:, :], data0=ONESF[0:16, :], data1=GF[:, :], initial=0.0,
                                                   op0=ALU.mult, op1=ALU.add), R=[("GF",), ("ONESF",)], W=[("NA",)])
                dve(lambda e: e.tensor_tensor(out=GI[:, :], in0=GI[:, :], in1=NA[:, :], op=ALU.add), R=[("GI",), ("NA",)], W=[("GI",)])
                dve(lambda e: e.tensor_reduce(out=GM[:, 2:3], in_=GI[:, :], axis=AX.X, op=ALU.max), R=[("GI",)], W=[("GM", 2)])
                dve(lambda e: e.tensor_scalar(out=GM[:, 0:1], in0=NA[:, 127:128], scalar1=-1.0, scalar2=None, op0=ALU.mult),
                    R=[("NA",)], W=[("GM", 0)])
                dve(lambda e: e.tensor_tensor(out=GM[:, 1:2], in0=GM[:, 2:3], in1=GM[:, 0:1], op=ALU.add),
                    R=[("GM", 2), ("GM", 0)], W=[("GM", 1)])

            def t1():
                b2 = nb()
                tp(PS[b2][0:1, 0:16], GM[0:16, 0:1], IDENTF[0:16, 0:16], R=[("GM", 0), ("IDENTF",)], W=psk(b2, 0, 16), inc=False)
                tp(PS[b2][0:1, 16:32], GM[0:16, 1:2], IDENTF[0:16, 0:16], R=[("GM", 1), ("IDENTF",)], W=psk(b2, 16, 32))
                dve(lambda e: e.tensor_copy(out=ROW[0:1, 0:32], in_=PS[b2][0:1, 0:32]), R=psk(b2, 0, 32), W=[("ROW",)])
                for h in range(4):
                    dve(lambda e, h=h: e.tensor_tensor_scan(out=MFULL[0:1, h, 1:5], data0=ROW[0:1, h * 4:(h + 1) * 4],
                                                            data1=ROW[0:1, 16 + h * 4:16 + (h + 1) * 4], initial=MFULL[0:1, h, 0:1],
                                                            op0=ALU.add, op1=ALU.max), R=[("ROW",), ("MFULL",)], W=[("MFULL",)])
                mc3 = MC[0:1, :].rearrange("p (h c) -> p h c", c=4)
                tr3 = TR[0:1, :].rearrange("p (h c) -> p h c", c=4)
                row3 = ROW[0:1, 0:16].rearrange("p (h c) -> p h c", c=4)
                dve(lambda e: e.tensor_tensor(out=mc3, in0=MFULL[0:1, :, 1:5], in1=row3, op=ALU.subtract), R=[("MFULL",), ("ROW",)], W=[("MC",)])
                dve(lambda e: e.tensor_tensor(out=tr3, in0=MFULL[0:1, :, 0:4], in1=mc3, op=ALU.subtract), R=[("MFULL",), ("MC",)], W=[("TR",)])
                act(SCR[0:1, :], TR[0:1, :], AF.Exp, R=[("TR",)], W=[("SCR",)])
                dve(lambda e: e.tensor_copy(out=MFULL[0:1, :, 0:1], in_=MFULL[0:1, :, 4:5]), R=[("MFULL",)], W=[("MFULL",)])

            def t2():
                b3 = nb()
                mm(PS[b3][:, 0:16], ONESF[0:1, 0:128], SCR[0:1, 0:16], True, True, R=[("ONESF",), ("SCR",)], W=psk(b3, 0, 16))
                dve(lambda e: e.tensor_copy(out=SCB[:, :], in_=PS[b3][:, 0:16]), R=psk(b3, 0, 16), W=[("SCB",)])
                b4 = nb()
                tp(PS[b4][0:16, 0:1], MC[0:1, 0:16], IDENTF[0:1, 0:1], R=[("MC",), ("IDENTF",)], W=psk(b4, 0, 1))
                dve(lambda e: e.tensor_scalar(out=NMC[:, 0:1], in0=PS[b4][0:16, 0:1], scalar1=-1.0, scalar2=None, op0=ALU.mult),
                    R=psk(b4, 0, 1), W=[("NMC", 0)])
                dve(lambda e: e.tensor_scalar(out=NMC[:, 1:2], in0=PS[b4][0:16, 0:1], scalar1=-1.0, scalar2=LNK, op0=ALU.mult, op1=ALU.add),
                    R=psk(b4, 0, 1), W=[("NMC", 1)])
                act(GI[:, :], GI[:, :], AF.Exp, R=[("GI",), ("NMC", 1)], W=[("GI",)], bias=NMC[:, 1:2])
                act(NA[:, :], NA[:, :], AF.Exp, R=[("NA",), ("NMC", 0)], W=[("NA",)], bias=NMC[:, 0:1])

            def t3():
                b5 = nb()
                tp(PS[b5][:, 0:16], GI[0:16, :], IDENTF[0:16, 0:16], R=[("GI",), ("IDENTF",)], W=psk(b5, 0, 16), inc=False)
                tp(PS[b5][:, 16:32], NA[0:16, :], IDENTF[0:16, 0:16], R=[("NA",), ("IDENTF",)], W=psk(b5, 16, 32))
                dve(lambda e: e.tensor_copy(out=ET[:, :], in_=PS[b5][:, 0:16]), R=psk(b5, 0, 16), W=[("ET",)])
                dve(lambda e: e.tensor_copy(out=ETB[:, :], in_=PS[b5][:, 0:16]), R=psk(b5, 0, 16), W=[("ETB",)])
                dve(lambda e: e.tensor_copy(out=THRT[:, :], in_=PS[b5][:, 16:32]), R=psk(b5, 16, 32), W=[("THRT",)])


            gate_tails[(l, s)] = [t0, t1, t2, t3]

        def phase_qk(l, s, which):
            cols = slice(s * SEG, (s + 1) * SEG)
            W, wk = acquire(which)
            DST, dname = (QT, "QT") if which == "q" else (KT, "KT")
            inject = gate_tails.get((l, s), [])
            sched = {}

            def inj(hm):
                for _ in range(sched.get(hm, 0)):
                    if inject:
                        inject.pop(0)()

            def main(h):
                tile = (0 if which == "q" else 4) + h
                uq = UQ[tile % 2]
                uk = ("UQ", tile % 2)
                b = nb()
                for k in range(8):
                    mm(PS[b][:, :], W[:, k, h * 128:(h + 1) * 128], XB[:, k, cols], k == 0, k == 7, R=[wk, ("XB", k, s)], W=psk(b))
                if s == 0:
                    dve(lambda e, uq=uq: e.memset(uq[:, 0:3], 0.0), R=(), W=[uk])
                else:
                    dve(lambda e, uq=uq, tile=tile: e.tensor_copy(out=uq[:, 0:3], in_=HALO[:, tile, :]), R=[("HALO", tile)], W=[uk])
                act(uq[:, 3:515], PS[b][:, :], AF.Identity, R=psk(b), W=[uk])
                if s < NSEG - 1:
                    dve(lambda e, uq=uq, tile=tile: e.tensor_copy(out=HALO[:, tile, :], in_=uq[:, 512:515]), R=[uk], W=[("HALO", tile)])
                dg = DG[tile % 2]
                for j in range(4):
                    c = l * 32 + j * 8 + tile
                    dve(lambda e, dg=dg, j=j, c=c: e.tensor_scalar(out=dg[:, j, :], in0=IDENTB[:, :], scalar1=WCVT[:, c:c + 1],
                                                                   scalar2=None, op0=ALU.mult),
                        R=[("IDENTB",), ("WCVT",)], W=[("DG", tile % 2, j)])

            def conv(h):
                tile = (0 if which == "q" else 4) + h
                uq = UQ[tile % 2]
                uk = ("UQ", tile % 2)
                dg = DG[tile % 2]
                b2 = nb()
                for j in range(4):
                    mm(PS[b2][:, :], dg[:, j, :], uq[:, j:j + 512], j == 0, j == 3, R=[("DG", tile % 2, j), uk], W=psk(b2))
                act(DST[:, h, :], PS[b2][:, :], AF.Silu, R=psk(b2), W=[(dname, h)])

            main(0)
            inj(0)
            for h in range(4):
                if h + 1 < 4:
                    main(h + 1)
                    inj(h + 1)
                conv(h)
            release()

        def phase_ktok(l, s):
            for h in range(4):
                b3 = nb()
                pb = psbf(b3)
                for c in range(4):
                    tp(pb[:, c * 128:(c + 1) * 128], KT[:, h, c * 128:(c + 1) * 128], IDENTB[:, :],
                       R=[("KT", h), ("IDENTB",)], W=psk(b3, 0, 256), inc=(c == 3))
                dve(lambda e, h=h, pb=pb: e.tensor_copy(out=KTOK[:, :, h, :], in_=pb[:, :].rearrange("p (c d) -> p c d", d=128)),
                    R=psk(b3, 0, 256), W=[("KTOK", h)])

        def phase_o(l, s):
            cols = slice(s * SEG, (s + 1) * SEG)
            W, wk = acquire("o")
            for h in range(4):
                b = nb()
                for k in range(8):
                    mm(PS[b][:, :], W[:, k, h * 128:(h + 1) * 128], XB[:, k, cols], k == 0, k == 7, R=[wk, ("XB", k, s)], W=psk(b))
                act(YM[:, h, :], PS[b][:, :], AF.Sigmoid, R=psk(b), W=[("YM", h)])
                if h == 3:
                    tl = gate_tails.get((l, s), [])
                    if tl:
                        tl.pop(0)()

            release()
            phase_ktok(l, s)

        pool_w = {}

        def pool_prep(l, s):
            if s == 0:
                T.dma("sp", d_wp, WPF[:, :, :], w_pool_d[l].rearrange("g c d -> c g d"), R=(), W=[("WPF",)])
                for g in range(4):
                    win = 2 ** (g + 1)
                    dve(lambda e, g=g, win=win: e.tensor_scalar(out=WPA[:, g, :], in0=WPF[:, g, :], scalar1=(1.0 / win - 1.0),
                                                                 scalar2=None, op0=ALU.mult), R=[("WPF",)], W=[("WPA", g)])
                    dve(lambda e, g=g, win=win: e.tensor_scalar(out=WPBt[:, g, :], in0=WPF[:, g, :], scalar1=1.0 / win,
                                                                 scalar2=None, op0=ALU.mult), R=[("WPF",)], W=[("WPB", g)])
                dve(lambda e: e.tensor_copy(out=WPC[:, :, :], in_=WPF[:, :, :]), R=[("WPF",)], W=[("WPC",)])

        def pool_inproj(l, s, g, bank=None):
            cols = slice(s * SEG, (s + 1) * SEG)
            if g == 0:
                pool_w["w"] = acquire("p")
            W, wk = pool_w["w"]
            b = nb() if bank is None else bank
            for k in range(8):
                mm(PS[b][:, :], W[:, k, g * 128:(g + 1) * 128], XB[:, k, cols], k == 0, k == 7, R=[wk, ("XB", k, s)], W=psk(b))
            if s > 0:
                act(PB[:, g, 0:16], PHALO[:, g, :], AF.Identity, R=[("PHALO", g)], W=[("PB", g)])
            act(PB[:, g, 16:528], PS[b][:, :], AF.Identity, R=psk(b), W=[("PB", g)])
            if s < NSEG - 1:
                act(PHALO[:, g, :], PB[:, g, 512:528], AF.Identity, R=[("PB", g)], W=[("PHALO", g)])
            if g == 3:
                release()

        def pool_group(l, s, g):
            win = 2 ** (g + 1)
            c0 = win - 1 if s == 0 else 0
            b = nb()
            mm(PS[b][:, c0:512], WPA[:, g, :], PB[:, g, 16 + c0:528], True, False, R=[("WPA", g), ("PB", g)], W=psk(b), inc=False)
            for j in range(1, win):
                mm(PS[b][:, c0:512], WPBt[:, g, :], PB[:, g, 16 + c0 - j:528 - j], False, j == win - 1,
                   R=[("WPB", g), ("PB", g)], W=psk(b))
            if s == 0:
                dve(lambda e, g=g, c0=c0: e.tensor_tensor_scan(out=CS[:, 0:c0], data0=ONESF[:, 0:c0], data1=PB[:, g, 16:16 + c0],
                                                               initial=0.0, op0=ALU.mult, op1=ALU.add),
                    R=[("PB", g), ("ONESF",)], W=[("CS",)])
                dve(lambda e, c0=c0: e.tensor_tensor(out=CS[:, 0:c0], in0=CS[:, 0:c0], in1=INVC[:, 0:c0], op=ALU.mult),
                    R=[("CS",), ("INVC",)], W=[("CS",)])
                dve(lambda e, g=g, c0=c0: e.tensor_tensor(out=DFB[:, 0:c0], in0=CS[:, 0:c0], in1=PB[:, g, 16:16 + c0], op=ALU.subtract),
                    R=[("CS",), ("PB", g)], W=[("DFB",)])
                mm(PS[b][:, 0:c0], WPC[:, g, :], DFB[:, 0:c0], True, True, R=[("WPC",), ("DFB",)], W=psk(b))
            c = 4 * L + l * 4 + g
            act(YP[:, g, :], PS[b][:, :], AF.Identity, R=psk(b) + [("PRM2T",)], W=[("YP", g)], scale=PRM2T[:, c:c + 1])
            pop_side(1)

        def phase_v(l, s):
            while reserved:
                pop_side(1)
            W, wk = acquire("v")
            et3 = ET[:, :].rearrange("p (h c) -> p h c", c=4)
            tl = gate_tails.get((l, s), [])
            if len(tl) == 4:
                tl.pop(0)()
            banks = []
            for c in range(4):
                tt = s * 4 + c
                b = nb()
                banks.append(b)
                for k in range(8):
                    mm(PS[b][:, :], XB[:, k, tt * 128:(tt + 1) * 128], W[:, k, :], k == 0, k == 7, R=[wk, ("XB", k, s)], W=psk(b))
                if c >= 1 and tl:
                    tl.pop(0)()
            while tl:
                tl.pop(0)()
            for c in range(4):
                b = banks[c]
                pv = PS[b][:, :].rearrange("p (h d) -> p h d", d=128)
                dve(lambda e, c=c, pv=pv: e.tensor_tensor(out=VP[:, c, :, :], in0=pv,
                                                          in1=et3[:, :, c].unsqueeze(2).to_broadcast([128, 4, 128]), op=ALU.mult),
                    R=psk(b) + [("ET",)], W=[("VP", c)])
            release()

        def phase_mlstm(l, s):
            while reserved:
                pop_side(1)
            pool_prep(l, s)
            scb3 = SCB[:, :].rearrange("p (h c) -> p h c", c=4)
            bO = [0, 1, 2, 3]
            bX = 4
            rot = [5, 6, 7]
            pX = PS[bX]
            kX = [("PS", bX)]
            def smv(i):
                return SM[:, i, :]

            def smc(i, c):
                return SM[:, i, :].rearrange("p (h c) -> p h c", c=4)[:, :, c]
            DNA, DN, RDN, SUM_, SSQ, MEAN_, EX2, VAR, R2, TT, SS, NBB = range(12)

            def stats(c):
                dve(lambda e: e.tensor_reduce(out=smc(SUM_, c), in_=pOs[c], axis=AX.X, op=ALU.add), R=psk(bO[c]), W=[("SM", SUM_)])
                act(SQs[c % 2][:, :, :], pOs[c], AF.Square, R=psk(bO[c]), W=[("SQ", c % 2)])
                dve(lambda e: e.tensor_reduce(out=smc(SSQ, c), in_=SQs[c % 2][:, :, :], axis=AX.X, op=ALU.add), R=[("SQ", c % 2)], W=[("SM", SSQ)])

            ri = 0
            pOs = []
            for c in range(4):
                first = (s == 0 and c == 0)
                tc_ = slice(c * 128, (c + 1) * 128)
                bS = rot[ri % 3]
                bDC = rot[(ri + 1) % 3]
                ri += 2
                pS = PS[bS][:, :].rearrange("p (h d) -> p h d", d=128)
                pDC = PS[bDC][:, :].rearrange("p (h d) -> p h d", d=128)
                pO = PS[bO[c]][:, :].rearrange("p (h d) -> p h d", d=128)
                pOs.append(pO)
                at = ATs[c % 2]
                ak = ("AT", c % 2)
                for h in range(4):
                    mm(pS[:, h, :], KT[:, h, tc_], QT[:, h, tc_], True, True, R=[("KT", h), ("QT", h)], W=psk(bS), inc=(h == 3))
                dve(lambda e, pS=pS, at=at: e.tensor_tensor(out=at[:, :, :], in0=pS, in1=MASK[:, :].unsqueeze(1).to_broadcast([128, 4, 128]),
                                                            op=ALU.mult), R=psk(bS) + [("MASK",)], W=[ak])
                for h in range(4):
                    col = h * 4 + c
                    mm(pDC[:, h, :], KTOK[:, c, h, :], VP[:, c, h, :], True, True, R=[("KTOK", h), ("VP", c)], W=psk(bDC), inc=False)
                    mm(pX[:, 16 + h:17 + h], KTOK[:, c, h, :], ETB[:, col:col + 1], True, True, R=[("KTOK", h), ("ETB",)], W=kX, inc=(h == 3))
                if not first:
                    dve(lambda e, c=c: e.tensor_tensor(out=C[:, :, :], in0=C[:, :, :],
                                                       in1=scb3[:, :, c].unsqueeze(2).to_broadcast([128, 4, 129]), op=ALU.mult),
                        R=[("C",), ("SCB",)], W=[("C",)])
                    act(CB[:, :, :], C[:, :, :], AF.Identity, R=[("C",)], W=[("CB",)])
                if c >= 1:
                    stats(c - 1)
                for h in range(4):
                    col = h * 4 + c
                    mm(pO[:, h, :], at[:, h, :], VP[:, c, h, :], True, first, R=[ak, ("VP", c)], W=psk(bO[c]), inc=False)
                    if not first:
                        mm(pO[:, h, :], QT[:, h, tc_], CB[:, h, 0:128], False, True, R=[("QT", h), ("CB",)], W=psk(bO[c]), inc=False)
                    mm(pX[:, col:col + 1], at[:, h, :], ETB[:, col:col + 1], True, first, R=[ak, ("ETB",)], W=kX, inc=(first and h == 3))
                    if not first:
                        mm(pX[:, col:col + 1], QT[:, h, tc_], CB[:, h, 128:129], False, True, R=[("QT", h), ("CB",)], W=kX, inc=(h == 3))
                if first:
                    dve(lambda e, pDC=pDC: e.tensor_copy(out=C[:, :, 0:128], in_=pDC), R=psk(bDC), W=[("C",)])
                    dve(lambda e: e.tensor_copy(out=C[:, :, 128:129], in_=pX[:, 16:20].unsqueeze(2)), R=kX, W=[("C",)])
                else:
                    dve(lambda e, pDC=pDC: e.tensor_tensor(out=C[:, :, 0:128], in0=C[:, :, 0:128], in1=pDC, op=ALU.add),
                        R=psk(bDC) + [("C",)], W=[("C",)])
                    dve(lambda e: e.tensor_tensor(out=C[:, :, 128:129], in0=C[:, :, 128:129], in1=pX[:, 16:20].unsqueeze(2), op=ALU.add),
                        R=kX + [("C",)], W=[("C",)])
            reserved.update((0, 1, 2, 3, 4))
            dve(lambda e: e.tensor_tensor(out=smv(DNA), in0=pX[:, 0:16], in1=THRT[:, 0:16], op=ALU.max), R=kX + [("THRT",)], W=[("SM", DNA)])
            dve(lambda e: e.scalar_tensor_tensor(out=smv(DN), in0=pX[:, 0:16], scalar=-1.0, in1=smv(DNA), op0=ALU.mult, op1=ALU.max),
                R=kX + [("SM", DNA)], W=[("SM", DN)])
            dve(lambda e: e.reciprocal(out=smv(RDN), in_=smv(DN)), R=[("SM", DN)], W=[("SM", RDN)])
            stats(3)
            pool_inproj(l, s, 0, bank=5)
            pool_inproj(l, s, 1, bank=7)
            dve(lambda e: e.tensor_scalar(out=smv(MEAN_), in0=smv(SUM_), scalar1=1.0 / 128, scalar2=None, op0=ALU.mult),
                R=[("SM", SUM_)], W=[("SM", MEAN_)])
            dve(lambda e: e.tensor_tensor(out=smv(EX2), in0=smv(MEAN_), in1=smv(MEAN_), op=ALU.mult), R=[("SM", MEAN_)], W=[("SM", EX2)])
            dve(lambda e: e.scalar_tensor_tensor(out=smv(VAR), in0=smv(SSQ), scalar=1.0 / 128, in1=smv(EX2), op0=ALU.mult, op1=ALU.subtract),
                R=[("SM", SSQ), ("SM", EX2)], W=[("SM", VAR)])
            dve(lambda e: e.tensor_tensor(out=smv(R2), in0=smv(RDN), in1=smv(RDN), op=ALU.mult), R=[("SM", RDN)], W=[("SM", R2)])
            dve(lambda e: e.tensor_tensor(out=smv(TT), in0=smv(R2), in1=smv(VAR), op=ALU.mult), R=[("SM", R2), ("SM", VAR)], W=[("SM", TT)])
            act(smv(TT), smv(TT), AF.Ln, R=[("SM", TT)], W=[("SM", TT)], bias=LN_EPS)
            act(smv(TT), smv(TT), AF.Exp, R=[("SM", TT)], W=[("SM", TT)], scale=-0.5)
            dve(lambda e: e.tensor_tensor(out=smv(SS), in0=smv(TT), in1=smv(RDN), op=ALU.mult), R=[("SM", TT), ("SM", RDN)], W=[("SM", SS)])
            dve(lambda e: e.scalar_tensor_tensor(out=smv(NBB), in0=smv(MEAN_), scalar=-1.0, in1=smv(SS), op0=ALU.mult, op1=ALU.mult),
                R=[("SM", MEAN_), ("SM", SS)], W=[("SM", NBB)])
            for c in (0, 1, 2, 3):
                for h in range(4):
                    col = h * 4 + c
                    if c < 1:
                        act(HN[:, c, h, :], pOs[c][:, h, :], AF.Identity, R=psk(bO[c]) + [("SM", SS), ("SM", NBB)], W=[("HN", c)],
                            scale=SM[:, SS, col:col + 1], bias=SM[:, NBB, col:col + 1])
                    else:
                        dve(lambda e, c=c, h=h, col=col: e.tensor_scalar(out=HN[:, c, h, :], in0=pOs[c][:, h, :], scalar1=SM[:, SS, col:col + 1],
                                                                         scalar2=SM[:, NBB, col:col + 1], op0=ALU.mult, op1=ALU.add),
                            R=psk(bO[c]) + [("SM", SS), ("SM", NBB)], W=[("HN", c)])
            pool_inproj(l, s, 2, bank=6)
            pool_inproj(l, s, 3, bank=5)
            for bb in (0, 1, 2, 3, 4):
                reserved.discard(bb)
            pool_group(l, s, 0)
            pool_group(l, s, 1)
            for pr in range(2):
                bT = nb()
                pHT = PS[bT][:, :].bitcast(BF16).rearrange("p (c h d) -> p c h d", h=4, d=128)
                for cc in range(2):
                    for h in range(4):
                        tp(pHT[:, cc, h, :], HN[:, 2 * pr + cc, h, :], IDENTB[:, :], R=[("HN", 2 * pr + cc), ("IDENTB",)], W=psk(bT),
                           inc=(cc == 1 and h == 3))
                for h in range(4):
                    cc_ = l * 4 + h
                    ymv = YM[:, h, pr * 256:(pr + 1) * 256].rearrange("p (c d) -> p c d", d=128)
                    dve(lambda e, h=h, cc_=cc_, ymv=ymv, pHT=pHT: e.scalar_tensor_tensor(out=ymv, in0=pHT[:, :, h, :], scalar=PRM2T[:, cc_:cc_ + 1],
                                                                                     in1=ymv, op0=ALU.mult, op1=ALU.mult),
                        R=psk(bT) + [("YM", h), ("PRM2T",)], W=[("YM", h)])

            pool_group(l, s, 2)
            pool_group(l, s, 3)

        def phase_outproj(l, s):
            cols = slice(s * SEG, (s + 1) * SEG)
            Ws = [acquire("wo0"), acquire("wo1")]
            for m in range(8):
                W, wk = Ws[m // 4]
                b = nb()
                for k in range(8):
                    rhs = YM[:, k, :] if k < 4 else YP[:, k - 4, :]
                    rk = ("YM", k) if k < 4 else ("YP", k - 4)
                    mm(PS[b][:, :], W[:, k, (m % 4) * 128:(m % 4 + 1) * 128], rhs, k == 0, k == 7, R=[wk, rk], W=psk(b))
                dve(lambda e, m=m, b=b: e.tensor_tensor(out=XF[:, m, cols], in0=XF[:, m, cols], in1=PS[b][:, :], op=ALU.add),
                    R=psk(b) + [("XF", m, s)], W=[("XF", m, s)])
                if m % 2 == 1:
                    pop_side(1)
                if m == 3:
                    release()
            release()

        ln_ctr = [0]

        def ln_a(l, which, tb, st, part):
            cols = slice(tb * SEG, (tb + 1) * SEG)
            if part == 0:
                st["j"] = ln_ctr[0] % 2
                ln_ctr[0] += 1
                st["b1"], st["b2"] = nb(), nb()
                reserved.update((st["b1"], st["b2"]))
            j, b1, b2 = st["j"], st["b1"], st["b2"]
            if part < 2:
                for d in range(part * 4, part * 4 + 4):
                    i = d % 3
                    act(RBt[i], XF[:, d, cols], AF.Identity, R=[("XF", d, tb)], W=[("RB", i)])
                    act(RSQt[i], XF[:, d, cols], AF.Square, R=[("XF", d, tb)], W=[("RSQ", i)])
                    mm(PS[b1][:, :], ONESM[:, :], RBt[i], d == 0, d == 7, R=[("ONESM",), ("RB", i)], W=psk(b1), inc=True)
                    mm(PS[b2][:, :], ONESM[:, :], RSQt[i], d == 0, d == 7, R=[("ONESM",), ("RSQ", i)], W=psk(b2), inc=True)
                return
            mean, rstd = MEANt[j], RSTDt[j]
            dve(lambda e: e.tensor_copy(out=mean, in_=PS[b1][:, :]), R=psk(b1), W=[("MEAN", j)])
            dve(lambda e: e.tensor_tensor(out=rstd, in0=mean, in1=mean, op=ALU.mult), R=[("MEAN", j)], W=[("RSTD", j)])
            dve(lambda e: e.tensor_tensor(out=rstd, in0=PS[b2][:, :], in1=rstd, op=ALU.subtract), R=psk(b2) + [("RSTD", j)], W=[("RSTD", j)])
            act(rstd, rstd, AF.Ln, R=[("RSTD", j)], W=[("RSTD", j)], bias=LN_EPS)
            act(rstd, rstd, AF.Exp, R=[("RSTD", j)], W=[("RSTD", j)], scale=-0.5)
            reserved.discard(b1)
            reserved.discard(b2)

        def ln_b(l, which, tb, j, d0, d1, scaled):
            ga, ba = (0, 1) if which == 1 else (2, 3)
            PA = PRMA if scaled else PRMT
            cols = slice(tb * SEG, (tb + 1) * SEG)
            mean, rstd = MEANt[j], RSTDt[j]
            for d in range(d0, d1):
                xf = XF[:, d, cols]
                dve(lambda e, xf=xf: e.tensor_tensor(out=xf, in0=xf, in1=mean, op=ALU.subtract), R=[("XF", d, tb), ("MEAN", j)], W=[("XF", d, tb)])
                dve(lambda e, xf=xf: e.tensor_tensor(out=xf, in0=xf, in1=rstd, op=ALU.mult), R=[("XF", d, tb), ("RSTD", j)], W=[("XF", d, tb)])
                cg, cb = lncol(ga, l, d), lncol(ba, l, d)
                act(XB[:, d, cols], xf, AF.Identity, R=[("XF", d, tb), ("PRMT",)], W=[("XB", d, tb)],
                    scale=PRMT[:, cg:cg + 1], bias=PRMT[:, cb:cb + 1])
                act(xf, xf, AF.Identity, R=[("XF", d, tb), ("PRMA",), ("PRMT",)], W=[("XF", d, tb)],
                    scale=PA[:, cg:cg + 1], bias=PA[:, cb:cb + 1])

        ffn_w = {}

        def ffn1(l, q, tb):
            if tb == 0:
                ffn_w["w1"] = [acquire("w10"), acquire("w11")]
            W1 = ffn_w["w1"]
            cols = slice(tb * SEG, (tb + 1) * SEG)
            for j in range(8):
                W, wk = W1[j // 4]
                b = nb()
                for k in range(8):
                    mm(PS[b][:, :], W[:, k, (j % 4) * 128:(j % 4 + 1) * 128], XB[:, k, cols], k == 0, k == 7,
                       R=[wk, ("XB", k, tb)], W=psk(b))
                hr = HR[j % 2]
                act(hr[:, :], PS[b][:, :], AF.Relu, R=psk(b), W=[("HR", j % 2)])
                dve(lambda e, hr=hr, j=j: e.tensor_tensor(out=H[:, j, cols], in0=hr[:, :], in1=hr[:, :], op=ALU.mult),
                    R=[("HR", j % 2)], W=[("H", j, tb)])
                if j % 2 == 1:
                    pop_side(2 if q == 0 else 1)
            if tb == 3:
                release()
                release()

        def ffn2(l, q, tb):
            if tb == 0:
                ffn_w["w2"] = [acquire("w20"), acquire("w21")]
            W2 = ffn_w["w2"]
            cols = slice(tb * SEG, (tb + 1) * SEG)
            for grp in ((0, 1, 2), (3, 4, 5), (6, 7)):
                banks = [nb() for _ in grp]
                for j in range(8):
                    W, wk = W2[j // 4]
                    for mi, m in enumerate(grp):
                        mm(PS[banks[mi]][:, :], W[:, j % 4, m * 128:(m + 1) * 128], H[:, j, cols], j == 0, j == 7,
                           R=[wk, ("H", j, tb)], W=psk(banks[mi]))
                for mi, m in enumerate(grp):
                    b = banks[mi]
                    dve(lambda e, m=m, b=b: e.tensor_tensor(out=XF[:, m, cols], in0=XF[:, m, cols], in1=PS[b][:, :], op=ALU.add),
                        R=psk(b) + [("XF", m, tb)], W=[("XF", m, tb)])
                pop_side(5 if q == 3 else 1)
            if tb == 3:
                release()
                release()

        from collections import deque
        side = deque()
        NOSIDE = bool(os.environ.get("NOSIDE"))

        def enqueue_ln(l, which, tb, scaled=True):
            tag = (l, which, tb)
            st = {}
            for part in range(3):
                side.append((tag, lambda part=part: ln_a(l, which, tb, st, part)))
            for d0 in range(0, 8, 2):
                side.append((tag, lambda d0=d0: ln_b(l, which, tb, st["j"], d0, d0 + 2, scaled)))
            if NOSIDE:
                drain(tag)

        def drain(tag=None):
            if tag is not None and not any(t == tag for t, _ in side):
                return
            while side:
                t, fn = side.popleft()
                fn()
                if tag is not None and not any(tt == tag for tt, _ in side):
                    break

        def pop_side(n=1):
            for _ in range(n):
                if side:
                    side.popleft()[1]()

        steps = []
        for l in range(L):
            for s in range(NSEG):
                need = (l - 1, 2, s) if l > 0 else None
                steps.append((need, lambda l=l, s=s: phase_gates(l, s)))
                steps.append((None, lambda l=l, s=s: phase_qk(l, s, "q")))
                steps.append((None, lambda l=l, s=s: phase_qk(l, s, "k")))
                steps.append((None, lambda l=l, s=s: phase_o(l, s)))
                steps.append((None, lambda l=l, s=s: phase_v(l, s)))
                steps.append((None, lambda l=l, s=s: phase_mlstm(l, s)))
                steps.append((None, lambda l=l, s=s: (phase_outproj(l, s), enqueue_ln(l, 1, s))))
            last_scaled = not (l == L - 1 and last_unscaled)
            for q in range(4):
                for tb in range(4):
                    steps.append(((l, 1, tb), lambda l=l, q=q, tb=tb: ffn1(l, q, tb)))
                for tb in range(4):
                    if q < 3:
                        steps.append((None, lambda l=l, q=q, tb=tb: ffn2(l, q, tb)))
                    else:
                        steps.append((None, lambda l=l, q=q, tb=tb, sc=last_scaled: (ffn2(l, q, tb), enqueue_ln(l, 2, tb, sc))))
        for i, (need, st) in enumerate(steps):
            if dbg is not None and i >= dbg:
                break
            if need is not None:
                drain(need)
            st()
        if dbg is not None:
            drain(None)

        for tt in range(int(os.environ.get('NOUT', 16))):
            xs = XS[tt % 4]
            tb = tt // 4
            if dbg is None and tt % 4 == 0:
                drain((L - 1, 2, tb))
            for dg in range(2):
                b = nb()
                for di in range(4):
                    d = dg * 4 + di
                    tp(PS[b][:, di * 128:(di + 1) * 128], XF[:, d, tt * 128:(tt + 1) * 128], IDENTF[:, :],
                       R=[("XF", d, tb), ("IDENTF",)], W=psk(b, di * 128, di * 128 + 128), inc=(di == 3))
                if dg == 0:
                    act(xs[:, 0:512], PS[b][:, :], AF.Identity, R=psk(b), W=[("XS", tt % 4)])
                else:
                    dve(lambda e, xs=xs, b=b: e.tensor_copy(out=xs[:, 512:1024], in_=PS[b][:, :]), R=psk(b), W=[("XS", tt % 4)])
            T.dma("sp", d_y[tt % 4], y_d[tt * 128:(tt + 1) * 128, :], xs, R=[("XS", tt % 4)], W=[("Y", tt)])
        drain(None)
        for dd in d_y:
            nc.sync.wait_ge(dd["sem"], dd["cnt"])
        build.stats = dict(n_wait=T.n_wait, cnt={k: v["cnt"] for k, v in T.E.items()})
    return nc


_NAMES = ["x", "w_in", "b_gate", "w_conv", "hn_g", "w_pool", "pool_scale", "w_out",
          "ln1_g", "ln1_b", "w_ff1", "w_ff2", "ln2_g", "ln2_b"]


def kernel(**inputs):
    arrs = {k: np.ascontiguousarray(np.asarray(inputs[k], dtype=np.float32)) for k in _NAMES}
    B = arrs["x"].shape[0]
    L = arrs["w_in"].shape[0]
    nc = build(depth=L)
    in_maps = []
    for b in range(B):
        m = {k: arrs[k] for k in _NAMES if k != "x"}
        m["x"] = np.ascontiguousarray(arrs["x"][b])
        in_maps.append(m)
    res = run_bass_kernel_spmd(nc, in_maps, core_ids=list(range(B)))
    return np.stack([res.results[b]["y"] for b in range(B)], axis=0).astype(np.float32)
```
